# Optimizing a Trainium2 kernel written in Bass

```python
import math
import jax, jax.numpy as jnp
from jax import lax
import numpy as np

D_MODEL = 2048
BATCH = 2
SEQ = 4096
DEPTH = 4

N_EVEN = (DEPTH + 1) // 2
N_ODD = DEPTH // 2

S5_WIDTH = D_MODEL // 2
RWKV_WIDTH = D_MODEL - S5_WIDTH
S5_GROUP = 16
S5_GROUPS = S5_WIDTH // S5_GROUP
S5_STATE = 64
S5_DT_MIN = 0.001
S5_DT_MAX = 0.1
RWKV_HEAD = 64
RWKV_HEADS = RWKV_WIDTH // RWKV_HEAD
RWKV_W_RANK = 64
RWKV_A_RANK = 64
RWKV_G_RANK = 160
EVEN_IN = S5_WIDTH + 3 * RWKV_WIDTH + RWKV_W_RANK + RWKV_A_RANK + RWKV_G_RANK
SHIFT_COLS = EVEN_IN - S5_WIDTH
GN_EPS = 64e-5

LRU_WIDTH = D_MODEL
LRU_BLOCKS = 8
LRU_BLOCK = LRU_WIDTH // LRU_BLOCKS
CONV_WIDTH = 4
LRU_C = 8.0

D_FF = -(-8 * D_MODEL // (3 * 256)) * 256
NORM_EPS = 1e-6

kernel_name = "hybrid_s5_rwkv7_rglru_trunk"


def rms_norm(x, g):
    xf = x.astype(jnp.float32)
    y = xf * lax.rsqrt(jnp.mean(xf * xf, axis=-1, keepdims=True) + NORM_EPS)
    return (y * g.astype(jnp.float32)).astype(x.dtype)


def token_shift(z):
    return jnp.pad(z, ((0, 0), (1, 0), (0, 0)))[:, :-1]


def complex_linear_combine(e1, e2):
    a1r, a1i, b1r, b1i = e1
    a2r, a2i, b2r, b2i = e2
    ar = a1r * a2r - a1i * a2i
    ai = a1r * a2i + a1i * a2r
    br = a2r * b1r - a2i * b1i + b2r
    bi = a2r * b1i + a2i * b1r + b2i
    return ar, ai, br, bi


def real_linear_combine(e1, e2):
    a1, b1 = e1
    a2, b2 = e2
    return a1 * a2, a2 * b1 + b2


def s5_mixer(u, lam_re, lam_im, log_dt, b_re, b_im, c_re, c_im, d_skip, w_glu):
    f32 = jnp.float32
    bsz, t_len, _ = u.shape
    uf = u.astype(f32).reshape(bsz, t_len, S5_GROUPS, S5_GROUP)
    dt = jnp.exp(log_dt.astype(f32))[:, None]
    lr = lam_re.astype(f32)
    li = lam_im.astype(f32)
    mag = jnp.exp(lr * dt)
    abar_re = mag * jnp.cos(li * dt)
    abar_im = mag * jnp.sin(li * dt)
    den = lr * lr + li * li
    nr = abar_re - 1.0
    ni = abar_im
    gam_re = (nr * lr + ni * li) / den
    gam_im = (ni * lr - nr * li) / den
    br_ = b_re.astype(f32)
    bi_ = b_im.astype(f32)
    bb_re = gam_re[..., None] * br_ - gam_im[..., None] * bi_
    bb_im = gam_re[..., None] * bi_ + gam_im[..., None] * br_
    bu_re = jnp.einsum('btgc,gpc->tbgp', uf, bb_re)
    bu_im = jnp.einsum('btgc,gpc->tbgp', uf, bb_im)
    a_re = jnp.broadcast_to(abar_re, (t_len, 1, S5_GROUPS, S5_STATE))
    a_im = jnp.broadcast_to(abar_im, (t_len, 1, S5_GROUPS, S5_STATE))
    _, _, s_re, s_im = lax.associative_scan(
        complex_linear_combine, (a_re, a_im, bu_re, bu_im), axis=0)
    y = (jnp.einsum('tbgp,gcp->btgc', s_re, c_re.astype(f32))
         - jnp.einsum('tbgp,gcp->btgc', s_im, c_im.astype(f32)))
    y = y + d_skip.astype(f32).reshape(S5_GROUPS, S5_GROUP) * uf
    y = jax.nn.gelu(y.reshape(bsz, t_len, S5_WIDTH)).astype(u.dtype)
    return y * jax.nn.sigmoid(y @ w_glu)


def rwkv7_mixer(r, k, v, w_lr, a_lr, g_lr, w0, w2, a0, a2, g2, k_k, k_a, r_k,
                lnx_w, lnx_b):
    f32 = jnp.float32
    bsz, t_len, _ = r.shape
    r, k, v = r.astype(f32), k.astype(f32), v.astype(f32)
    w = -jax.nn.softplus(-(w0.astype(f32) + jnp.tanh(w_lr.astype(f32)) @ w2.astype(f32))) - 0.5
    decay = jnp.exp(-jnp.exp(w))
    a = jax.nn.sigmoid(a0.astype(f32) + a_lr.astype(f32) @ a2.astype(f32))
    g = jax.nn.sigmoid(g_lr.astype(f32)) @ g2.astype(f32)
    hs = (bsz, t_len, RWKV_HEADS, RWKV_HEAD)
    kk = (k * k_k.astype(f32)).reshape(hs)
    kk = kk * lax.rsqrt(jnp.maximum(jnp.sum(kk * kk, -1, keepdims=True), 1e-24))
    k = k * (1.0 + (a - 1.0) * k_a.astype(f32))
    rh, kh, vh = r.reshape(hs), k.reshape(hs), v.reshape(hs)
    ah = a.reshape(hs)
    wh = decay.reshape(hs)
    vec_a = -kk
    vec_b = kk * ah
    tf = lambda z: jnp.swapaxes(z, 0, 1)

    def step(state, inp):
        r_t, w_t, k_t, v_t, a_t, b_t = inp
        sa = jnp.einsum('bhvk,bhk->bhv', state, a_t)
        state = (state * w_t[:, :, None, :]
                 + sa[..., None] * b_t[:, :, None, :]
                 + v_t[..., None] * k_t[:, :, None, :])
        return state, jnp.einsum('bhvk,bhk->bhv', state, r_t)

    s0 = jnp.zeros((bsz, RWKV_HEADS, RWKV_HEAD, RWKV_HEAD), f32)
    _, y = lax.scan(step, s0, (tf(rh), tf(wh), tf(kh), tf(vh), tf(vec_a), tf(vec_b)))
    y = tf(y)
    mu = jnp.mean(y, -1, keepdims=True)
    var = jnp.mean(jnp.square(y - mu), -1, keepdims=True)
    y = (y - mu) * lax.rsqrt(var + GN_EPS)
    y = y * lnx_w.astype(f32).reshape(RWKV_HEADS, RWKV_HEAD) + lnx_b.astype(f32).reshape(RWKV_HEADS, RWKV_HEAD)
    bonus = jnp.sum(rh * kh * r_k.astype(f32), -1, keepdims=True) * vh
    y = (y + bonus).reshape(bsz, t_len, RWKV_WIDTH)
    return y * g


def even_mixer(h, w_in, shift_mu, s5_lam_re, s5_lam_im, s5_log_dt, s5_b_re, s5_b_im,
               s5_c_re, s5_c_im, s5_d, s5_w_glu, rw_w0, rw_w2, rw_a0, rw_a2, rw_g2,
               rw_k_k, rw_k_a, rw_r_k, rw_lnx_w, rw_lnx_b, w_out):
    p = h @ w_in
    u = p[..., :S5_WIDTH]
    z = p[..., S5_WIDTH:]
    z = z + (token_shift(z) - z) * shift_mu
    o = 0
    r = z[..., o:o + RWKV_WIDTH]; o += RWKV_WIDTH
    k = z[..., o:o + RWKV_WIDTH]; o += RWKV_WIDTH
    v = z[..., o:o + RWKV_WIDTH]; o += RWKV_WIDTH
    w_lr = z[..., o:o + RWKV_W_RANK]; o += RWKV_W_RANK
    a_lr = z[..., o:o + RWKV_A_RANK]; o += RWKV_A_RANK
    g_lr = z[..., o:o + RWKV_G_RANK]
    y_s5 = s5_mixer(u, s5_lam_re, s5_lam_im, s5_log_dt, s5_b_re, s5_b_im,
                    s5_c_re, s5_c_im, s5_d, s5_w_glu)
    y_rw = rwkv7_mixer(r, k, v, w_lr, a_lr, g_lr, rw_w0, rw_w2, rw_a0, rw_a2, rw_g2,
                       rw_k_k, rw_k_a, rw_r_k, rw_lnx_w, rw_lnx_b)
    y = jnp.concatenate([y_s5, y_rw.astype(h.dtype)], axis=-1)
    return y @ w_out


def odd_mixer(h, w_in, conv_w, conv_b, w_r, b_r, w_i, b_i, lam, w_out):
    f32 = jnp.float32
    bsz, t_len, _ = h.shape
    p = h @ w_in
    gate = jax.nn.gelu(p[..., :LRU_WIDTH])
    xb = p[..., LRU_WIDTH:]
    xc = lax.conv_general_dilated(
        xb, conv_w.astype(xb.dtype)[:, None, :], window_strides=(1,),
        padding=[(CONV_WIDTH - 1, 0)], dimension_numbers=('NWC', 'WIO', 'NWC'),
        feature_group_count=LRU_WIDTH) + conv_b
    xf = xc.astype(f32)
    xblk = xf.reshape(bsz, t_len, LRU_BLOCKS, LRU_BLOCK)
    gr = (jnp.einsum('btnc,ncd->btnd', xblk, w_r.astype(f32)).reshape(bsz, t_len, LRU_WIDTH)
          + b_r.astype(f32))
    gi = (jnp.einsum('btnc,ncd->btnd', xblk, w_i.astype(f32)).reshape(bsz, t_len, LRU_WIDTH)
          + b_i.astype(f32))
    log_a = -LRU_C * jax.nn.sigmoid(gr) * jax.nn.softplus(-lam.astype(f32))
    a = jnp.exp(log_a)
    mult = jnp.sqrt(-jnp.expm1(2.0 * log_a))
    bx = mult * jax.nn.sigmoid(gi) * xf
    _, hseq = lax.associative_scan(real_linear_combine, (a, bx), axis=1)
    return (hseq.astype(h.dtype) * gate) @ w_out


def swiglu(h, w_gate, w_up, w_down):
    return (jax.nn.silu(h @ w_gate) * (h @ w_up)) @ w_down


def setup_inputs(seed: int = 0) -> dict:
    key = jax.random.key(seed)
    ks = iter(jax.random.split(key, 48))
    nrm = lambda shape, s: jax.random.normal(next(ks), shape, jnp.float32) * s
    uni = lambda shape, lo, hi: jax.random.uniform(next(ks), shape, jnp.float32, lo, hi)
    E, O = N_EVEN, N_ODD
    G, P, C = S5_GROUPS, S5_STATE, S5_GROUP
    a8 = uni((O, LRU_WIDTH), 0.9, 0.999)
    a_base = a8 ** (1.0 / LRU_C)
    return {
        "x": nrm((BATCH, SEQ, D_MODEL), 1.0),
        "ev_w_in": nrm((E, D_MODEL, EVEN_IN), D_MODEL ** -0.5),
        "ev_shift_mu": uni((E, SHIFT_COLS), 0.0, 1.0),
        "s5_lam_re": -0.5 + nrm((E, G, P), 0.01),
        "s5_lam_im": math.pi * jnp.arange(P, dtype=jnp.float32) + nrm((E, G, P), 0.01),
        "s5_log_dt": uni((E, G), math.log(S5_DT_MIN), math.log(S5_DT_MAX)),
        "s5_b_re": nrm((E, G, P, C), (2.0 * C) ** -0.5),
        "s5_b_im": nrm((E, G, P, C), (2.0 * C) ** -0.5),
        "s5_c_re": nrm((E, G, C, P), (2.0 * P) ** -0.5),
        "s5_c_im": nrm((E, G, C, P), (2.0 * P) ** -0.5),
        "s5_d": nrm((E, S5_WIDTH), 1.0),
        "s5_w_glu": nrm((E, S5_WIDTH, S5_WIDTH), S5_WIDTH ** -0.5),
        "rw_w0": uni((E, RWKV_WIDTH), -5.0, 1.0),
        "rw_w2": nrm((E, RWKV_W_RANK, RWKV_WIDTH), 0.1),
        "rw_a0": nrm((E, RWKV_WIDTH), 0.1),
        "rw_a2": nrm((E, RWKV_A_RANK, RWKV_WIDTH), 0.5 * RWKV_A_RANK ** -0.5),
        "rw_g2": nrm((E, RWKV_G_RANK, RWKV_WIDTH), RWKV_G_RANK ** -0.5),
        "rw_k_k": 0.85 + nrm((E, RWKV_WIDTH), 0.02),
        "rw_k_a": 1.0 + nrm((E, RWKV_WIDTH), 0.02),
        "rw_r_k": nrm((E, RWKV_HEADS, RWKV_HEAD), 0.1),
        "rw_lnx_w": 1.0 + nrm((E, RWKV_WIDTH), 0.02),
        "rw_lnx_b": nrm((E, RWKV_WIDTH), 0.01),
        "ev_w_out": nrm((E, D_MODEL, D_MODEL), D_MODEL ** -0.5),
        "od_w_in": nrm((O, D_MODEL, 2 * LRU_WIDTH), D_MODEL ** -0.5),
        "od_conv_w": nrm((O, CONV_WIDTH, LRU_WIDTH), CONV_WIDTH ** -0.5),
        "od_conv_b": nrm((O, LRU_WIDTH), 0.01),
        "lru_w_r": nrm((O, LRU_BLOCKS, LRU_BLOCK, LRU_BLOCK), LRU_BLOCK ** -0.5),
        "lru_b_r": nrm((O, LRU_WIDTH), 0.01),
        "lru_w_i": nrm((O, LRU_BLOCKS, LRU_BLOCK, LRU_BLOCK), LRU_BLOCK ** -0.5),
        "lru_b_i": nrm((O, LRU_WIDTH), 0.01),
        "lru_lam": jnp.log(a_base) - jnp.log1p(-a_base),
        "od_w_out": nrm((O, LRU_WIDTH, D_MODEL), LRU_WIDTH ** -0.5),
        "ffn_w_gate": nrm((DEPTH, D_MODEL, D_FF), D_MODEL ** -0.5),
        "ffn_w_up": nrm((DEPTH, D_MODEL, D_FF), D_MODEL ** -0.5),
        "ffn_w_down": nrm((DEPTH, D_FF, D_MODEL), D_FF ** -0.5),
        "norm_mix_pre": 1.0 + nrm((DEPTH, D_MODEL), 0.02),
        "norm_mix_post": 1.0 + nrm((DEPTH, D_MODEL), 0.02),
        "norm_ffn_pre": 1.0 + nrm((DEPTH, D_MODEL), 0.02),
        "norm_ffn_post": 1.0 + nrm((DEPTH, D_MODEL), 0.02),
    }


def reference(x, ev_w_in, ev_shift_mu, s5_lam_re, s5_lam_im, s5_log_dt, s5_b_re, s5_b_im,
              s5_c_re, s5_c_im, s5_d, s5_w_glu, rw_w0, rw_w2, rw_a0, rw_a2, rw_g2,
              rw_k_k, rw_k_a, rw_r_k, rw_lnx_w, rw_lnx_b, ev_w_out,
              od_w_in, od_conv_w, od_conv_b, lru_w_r, lru_b_r, lru_w_i, lru_b_i, lru_lam,
              od_w_out, ffn_w_gate, ffn_w_up, ffn_w_down,
              norm_mix_pre, norm_mix_post, norm_ffn_pre, norm_ffn_post):
    for layer in range(DEPTH):
        i = layer // 2
        h = rms_norm(x, norm_mix_pre[layer])
        if layer % 2 == 0:
            y = even_mixer(h, ev_w_in[i], ev_shift_mu[i], s5_lam_re[i], s5_lam_im[i],
                           s5_log_dt[i], s5_b_re[i], s5_b_im[i], s5_c_re[i], s5_c_im[i],
                           s5_d[i], s5_w_glu[i], rw_w0[i], rw_w2[i], rw_a0[i], rw_a2[i],
                           rw_g2[i], rw_k_k[i], rw_k_a[i], rw_r_k[i], rw_lnx_w[i],
                           rw_lnx_b[i], ev_w_out[i])
        else:
            y = odd_mixer(h, od_w_in[i], od_conv_w[i], od_conv_b[i], lru_w_r[i], lru_b_r[i],
                          lru_w_i[i], lru_b_i[i], lru_lam[i], od_w_out[i])
        x = x + rms_norm(y.astype(x.dtype), norm_mix_post[layer])
        h = rms_norm(x, norm_ffn_pre[layer])
        y = swiglu(h, ffn_w_gate[layer], ffn_w_up[layer], ffn_w_down[layer])
        x = x + rms_norm(y, norm_ffn_post[layer])
    return x
```

```python
import contextlib
import numpy as np
import concourse.bass as bass
import concourse.mybir as mybir
from concourse.bass_utils import run_bass_kernel_spmd

F32 = mybir.dt.float32
BF16 = mybir.dt.bfloat16
AF = mybir.ActivationFunctionType
ALU = mybir.AluOpType
AX = mybir.AxisListType

NCORES = 8
D = 2048
B = 2
T = 4096
NT = 1024
DFF = 5632
EVEN_IN = 4384
NORM_EPS = 1e-6

ENGS = ("pe", "act", "dve", "pool", "sp")


class Res:
    __slots__ = ("lw", "rd")

    def __init__(self):
        self.lw = None
        self.rd = []


class Chan:
    __slots__ = ("sem", "cnt")

    def __init__(self, sem):
        self.sem = sem
        self.cnt = 0


class _Rec:
    def __init__(self):
        self.call = None

    def __getattr__(self, name):
        def f(*a, **k):
            assert self.call is None, "op closure must emit exactly one instruction"
            self.call = (name, a, k)
            return self
        return f


class Prog:
    def __init__(self, nc):
        self.nc = nc
        self.stack = contextlib.ExitStack()
        self.q = {e: [] for e in ENGS}
        self.esem = {e: self.stack.enter_context(nc.semaphore("es_" + e)) for e in ENGS}
        self.ecnt = {e: 0 for e in ENGS}
        self.seen = {e: {} for e in ENGS}
        self.out_events = []
        self._n = 0

    def name(self, base):
        self._n += 1
        return "%s_%d" % (base, self._n)

    def sbuf(self, shape, dt, name="sb"):
        return self.stack.enter_context(self.nc.sbuf_tensor(self.name(name), list(shape), dt))

    def psum(self, shape, dt=F32, name="ps"):
        return self.stack.enter_context(self.nc.psum_tensor(self.name(name), list(shape), dt))

    def chan(self, name="ch"):
        return Chan(self.stack.enter_context(self.nc.semaphore(self.name(name))))

    def _deps(self, eng, reads, writes):
        evs = []
        for r in reads:
            if r.lw is not None:
                evs.append(r.lw)
        for w in writes:
            if w.lw is not None:
                evs.append(w.lw)
            for ev in w.rd:
                if ev[2] != eng:
                    evs.append(ev)
        waits = {}
        seen = self.seen[eng]
        for (sem, val, _e) in evs:
            k = id(sem)
            if seen.get(k, 0) >= val:
                continue
            if k not in waits or waits[k][1] < val:
                waits[k] = (sem, val)
        for k, (sem, val) in waits.items():
            seen[k] = val
        return list(waits.values())

    def _commit(self, ev, reads, writes):
        for w in writes:
            w.lw = ev
            w.rd = []
        for r in reads:
            r.rd.append(ev)

    def op(self, eng, fn, reads=(), writes=()):
        waits = self._deps(eng, reads, writes)
        self.ecnt[eng] += 1
        ev = (self.esem[eng], self.ecnt[eng], eng)
        rec = _Rec()
        fn(rec)
        self.q[eng].append((waits, rec.call, (self.esem[eng], 1)))
        self._commit(ev, reads, writes)
        return ev

    def dma(self, queue, chan, out, in_, reads=(), writes=(), is_output=False, **kw):
        waits = self._deps(queue, reads, writes)
        chan.cnt += 1
        ev = (chan.sem, 16 * chan.cnt, "dma")
        self.q[queue].append((waits, ("dma_start", (), dict(out=out, in_=in_, **kw)), (chan.sem, 16)))
        self._commit(ev, reads, writes)
        if is_output:
            self.out_events.append(ev)
        return ev

    def finish(self):
        fin = {}
        for (sem, val, _e) in self.out_events:
            k = id(sem)
            if k not in fin or fin[k][1] < val:
                fin[k] = (sem, val)
        nc = self.nc
        q = self.q
        fin_waits = list(fin.values())

        def replay(engobj, items, extra_waits=()):
            for (waits, fn, inc) in items:
                for (sem, val) in waits:
                    engobj.wait_ge(sem, val)
                name, a, k = fn
                ins = getattr(engobj, name)(*a, **k)
                if inc is not None:
                    ins.then_inc(inc[0], inc[1])
            for (sem, val) in extra_waits:
                engobj.wait_ge(sem, val)

        with nc.Block() as block:
            @block.sync
            def _(e):
                replay(e, q["sp"], fin_waits)

            @block.tensor
            def _(e):
                replay(e, q["pe"])

            @block.scalar
            def _(e):
                replay(e, q["act"])

            @block.vector
            def _(e):
                replay(e, q["dve"])

            @block.gpsimd
            def _(e):
                replay(e, q["pool"])
        self.stack.close()


class Dense:
    def __init__(self, p, nt=NT):
        self.p = p
        nc = p.nc
        self.nt = nt
        self.ntb = nt // 512
        self.ones = p.sbuf([128, 128], BF16, "ones")
        self.ones_r = Res()
        p.op("pool", lambda e: e.memset(self.ones[:], 1.0), writes=[self.ones_r])
        self.banks = [p.psum([128, 512], F32, "bank") for _ in range(8)]
        self.bank_r = [Res() for _ in range(8)]
        self._bk = 0
        self.wslots = [p.sbuf([128, 16, 256], BF16, "wslot") for _ in range(3)]
        self.wslot_r = [Res() for _ in range(3)]
        self.wchan = [p.chan("wch") for _ in range(3)]
        self._ws = 0
        self.ostg = [p.sbuf([128, 512], F32, "ostg") for _ in range(4)]
        self.ostg_r = [Res() for _ in range(4)]
        self.ochan = [p.chan("och") for _ in range(4)]
        self._os = 0

    def bank(self):
        i = self._bk
        self._bk = (self._bk + 1) % 8
        return self.banks[i], self.bank_r[i]

    def wslot(self):
        i = self._ws
        self._ws = (self._ws + 1) % 3
        return self.wslots[i], self.wslot_r[i], self.wchan[i]

    def ostage(self):
        i = self._os
        self._os = (self._os + 1) % 4
        return self.ostg[i], self.ostg_r[i], self.ochan[i]


def load_fm(p, chan, dram_ap, nchunks, nt, dt=F32, name="act", queue="sp"):
    t = p.sbuf([128, nchunks, nt], dt, name)
    rs = [Res() for _ in range(nchunks)]
    src = dram_ap.rearrange("(c q) n -> q c n", q=128)
    step = max(1, 4)
    for c0 in range(0, nchunks, step):
        c1 = min(nchunks, c0 + step)
        p.dma(queue, chan, t[:, c0:c1, :], src[:, c0:c1, :], writes=rs[c0:c1])
    return t, rs


def load_vec(p, chan, dram_ap, nchunks, name="vec"):
    t = p.sbuf([128, nchunks], F32, name)
    r = Res()
    src = dram_ap.rearrange("(c q) -> q c", q=128)
    p.dma("sp", chan, t[:], src, writes=[r], allow_slow_non_contiguous=True)
    return t, r


def rmsnorm_fm(p, dn, x_sb, x_r, nchunks, g_sb, g_r, out_dt=BF16, name="h", inplace=False):
    nt = dn.nt
    dmodel = nchunks * 128
    if inplace:
        h, h_r = x_sb, x_r
    else:
        h = p.sbuf([128, nchunks, nt], out_dt, name)
        h_r = [Res() for _ in range(nchunks)]
    sq = [p.sbuf([128, nt], BF16, "sq") for _ in range(2)]
    sq_r = [Res(), Res()]
    rstd = p.sbuf([128, nt], F32, "rstd")
    rstd_r = [Res() for _ in range(dn.ntb)]
    banks = [dn.bank() for _ in range(dn.ntb)]
    for c in range(nchunks):
        s, sr = sq[c % 2], sq_r[c % 2]
        p.op("act", lambda e, s=s, c=c: e.activation(out=s[:], in_=x_sb[:, c, :], func=AF.Square),
             reads=[x_r[c]], writes=[sr])
        for tb in range(dn.ntb):
            bk, bkr = banks[tb]
            p.op("pe", lambda e, bk=bk, s=s, tb=tb, c=c: e.matmul(
                bk[:], dn.ones[:], s[:, tb * 512:(tb + 1) * 512],
                start=(c == 0), stop=(c == nchunks - 1)),
                reads=[sr, dn.ones_r], writes=[bkr])
    for tb in range(dn.ntb):
        bk, bkr = banks[tb]
        sl = slice(tb * 512, (tb + 1) * 512)
        p.op("dve", lambda e, bk=bk, sl=sl: e.tensor_scalar(
            rstd[:, sl], bk[:], 1.0 / dmodel, NORM_EPS, ALU.mult, ALU.add),
            reads=[bkr], writes=[rstd_r[tb]])
        p.op("act", lambda e, sl=sl: e.activation(out=rstd[:, sl], in_=rstd[:, sl], func=AF.Sqrt),
             reads=[rstd_r[tb]], writes=[rstd_r[tb]])
        p.op("dve", lambda e, sl=sl: e.reciprocal(out=rstd[:, sl], in_=rstd[:, sl]),
             reads=[rstd_r[tb]], writes=[rstd_r[tb]])
    for c in range(nchunks):
        for tb in range(dn.ntb):
            sl = slice(tb * 512, (tb + 1) * 512)
            p.op("dve", lambda e, c=c, sl=sl: e.scalar_tensor_tensor(
                out=h[:, c, sl], in0=x_sb[:, c, sl], scalar=g_sb[:, c:c + 1], in1=rstd[:, sl],
                op0=ALU.mult, op1=ALU.mult),
                reads=[x_r[c], g_r, rstd_r[tb]], writes=[h_r[c]])
    return h, h_r


def linear_fm(p, dn, h, h_r, kchunks, w_dram, fdim, epilogue, w2_dram=None):
    nt = dn.nt
    FB = 256
    kstep = 16
    for f0 in range(0, fdim, FB):
        fb = min(FB, fdim - f0)
        slots = []
        for wd in ([w_dram] if w2_dram is None else [w_dram, w2_dram]):
            parts = []
            for k0 in range(0, kchunks, kstep):
                k1 = min(kchunks, k0 + kstep)
                ws, wr, wc = dn.wslot()
                src = wd[k0 * 128:k1 * 128, f0:f0 + fb].rearrange("(c q) f -> q c f", q=128)
                p.dma("pool", wc, ws[:, 0:k1 - k0, 0:fb], src, writes=[wr])
                parts.append((ws, wr, k0, k1))
            slots.append(parts)
        for fs in range(0, fb, 128):
            fsz = min(128, fb - fs)
            for tb in range(dn.ntb):
                outs = []
                for parts in slots:
                    bk, bkr = dn.bank()
                    nk = kchunks
                    for (ws, wr, k0, k1) in parts:
                        for k in range(k0, k1):
                            p.op("pe", lambda e, bk=bk, ws=ws, k=k, k0=k0, fs=fs, fsz=fsz, tb=tb: e.matmul(
                                bk[0:fsz, :], ws[:, k - k0, fs:fs + fsz], h[:, k, tb * 512:(tb + 1) * 512],
                                start=(k == 0), stop=(k == nk - 1)),
                                reads=[wr, h_r[k]], writes=[bkr])
                    outs += [bk, bkr]
                epilogue(f0 + fs, fsz, tb, *outs)


def build_k1(fdim, swiglu=False, nt=NT):
    nc = bass.Bass("TRN2", target_bir_lowering=False)
    xT = nc.dram_tensor("xT", [D, nt], F32, kind="ExternalInput").ap()
    g = nc.dram_tensor("g", [D], F32, kind="ExternalInput").ap()
    w = nc.dram_tensor("w", [D, fdim], F32, kind="ExternalInput").ap()
    w2 = nc.dram_tensor("w2", [D, fdim], F32, kind="ExternalInput").ap() if swiglu else None
    odt = BF16 if swiglu else F32
    oT = nc.dram_tensor("oT", [fdim, nt], odt, kind="ExternalOutput").ap()
    p = Prog(nc)
    dn = Dense(p, nt)
    ch_in = p.chan("chin")
    x_sb, x_r = load_fm(p, ch_in, xT, D // 128, nt, name="x")
    g_sb, g_r = load_vec(p, p.chan("chg"), g, D // 128)
    h, h_r = rmsnorm_fm(p, dn, x_sb, x_r, D // 128, g_sb, g_r)
    ostg_bf = [p.sbuf([128, 512], BF16, "ostgb") for _ in range(4)]
    sil = [p.sbuf([128, 512], F32, "sil") for _ in range(2)]
    sil_r = [Res(), Res()]
    cnt = [0]

    def epi_plain(f0, fsz, tb, bk, bkr):
        st, sr, sc = dn.ostage()
        eng = "act" if cnt[0] % 2 == 0 else "dve"
        cnt[0] += 1
        if eng == "act":
            p.op("act", lambda e: e.copy(out=st[0:fsz, :], in_=bk[0:fsz, :]), reads=[bkr], writes=[sr])
        else:
            p.op("dve", lambda e: e.tensor_copy(out=st[0:fsz, :], in_=bk[0:fsz, :]), reads=[bkr], writes=[sr])
        p.dma("sp", sc, oT[f0:f0 + fsz, tb * 512:(tb + 1) * 512], st[0:fsz, :], reads=[sr], is_output=True)

    def epi_swiglu(f0, fsz, tb, bg, bgr, bu, bur):
        i = dn._os
        st, sr, sc = dn.ostage()
        stb = ostg_bf[i]
        s, s_r = sil[cnt[0] % 2], sil_r[cnt[0] % 2]
        cnt[0] += 1
        p.op("act", lambda e: e.activation(out=s[0:fsz, :], in_=bg[0:fsz, :], func=AF.Silu),
             reads=[bgr], writes=[s_r])
        p.op("dve", lambda e: e.tensor_tensor(out=stb[0:fsz, :], in0=s[0:fsz, :], in1=bu[0:fsz, :], op=ALU.mult),
             reads=[s_r, bur], writes=[sr])
        p.dma("sp", sc, oT[f0:f0 + fsz, tb * 512:(tb + 1) * 512], stb[0:fsz, :], reads=[sr], is_output=True)

    linear_fm(p, dn, h, h_r, D // 128, w, fdim, epi_swiglu if swiglu else epi_plain, w2_dram=w2)
    p.finish()
    return nc


def build_k2(kdim, in_bf16=False, glu=False, nt=NT):
    nc = bass.Bass("TRN2", target_bir_lowering=False)
    in_dt = BF16 if in_bf16 else F32
    inT = nc.dram_tensor("inT", [kdim, nt], in_dt, kind="ExternalInput").ap()
    xT = nc.dram_tensor("xT", [D, nt], F32, kind="ExternalInput").ap()
    g = nc.dram_tensor("g", [D], F32, kind="ExternalInput").ap()
    w = nc.dram_tensor("w", [kdim, D], F32, kind="ExternalInput").ap()
    wg = nc.dram_tensor("wglu", [1024, 1024], F32, kind="ExternalInput").ap() if glu else None
    oT = nc.dram_tensor("oT", [D, nt], F32, kind="ExternalOutput").ap()
    p = Prog(nc)
    dn = Dense(p, nt)
    kch = kdim // 128
    h = p.sbuf([128, kch, nt], BF16, "hin")
    h_r = [Res() for _ in range(kch)]
    ch_in = p.chan("chin")
    src = inT.rearrange("(c q) n -> q c n", q=128)
    g_sb, g_r = load_vec(p, p.chan("chg"), g, D // 128)
    if glu:
        ys = p.sbuf([128, 8, nt], BF16, "ys5")
        ys_r = [Res() for _ in range(8)]
        for c0 in range(0, 8, 4):
            p.dma("pool", ch_in, ys[:, c0:c0 + 4, :], src[:, c0:c0 + 4, :], writes=ys_r[c0:c0 + 4])
        for c0 in range(8, 16, 4):
            p.dma("pool", ch_in, h[:, c0:c0 + 4, :], src[:, c0:c0 + 4, :], writes=h_r[c0:c0 + 4])
        sg = [p.sbuf([128, 512], F32, "sg") for _ in range(2)]
        sg_r = [Res(), Res()]
        cg = [0]

        def epi_glu(f0, fsz, tb, bk, bkr):
            s_, sr_ = sg[cg[0] % 2], sg_r[cg[0] % 2]
            cg[0] += 1
            c = f0 // 128
            sl = slice(tb * 512, (tb + 1) * 512)
            p.op("act", lambda e: e.activation(out=s_[:], in_=bk[:], func=AF.Sigmoid), reads=[bkr], writes=[sr_])
            p.op("dve", lambda e: e.tensor_tensor(out=h[:, c, sl], in0=s_[:], in1=ys[:, c, sl], op=ALU.mult),
                 reads=[sr_, ys_r[c]], writes=[h_r[c]])
        linear_fm(p, dn, ys, ys_r, 8, wg, 1024, epi_glu)
    else:
        q = "sp" if in_bf16 else "pool"
        for c0 in range(0, kch, 4):
            c1 = min(kch, c0 + 4)
            p.dma(q, ch_in, h[:, c0:c1, :], src[:, c0:c1, :], writes=h_r[c0:c1])
    o_sb = p.sbuf([128, D // 128, nt], F32, "osb")
    o_r = [Res() for _ in range(D // 128)]
    ce = [0]

    def epi_o(f0, fsz, tb, bk, bkr):
        c = f0 // 128
        sl = slice(tb * 512, (tb + 1) * 512)
        ce[0] += 1
        if ce[0] % 2 == 0:
            p.op("act", lambda e: e.copy(out=o_sb[:, c, sl], in_=bk[:]), reads=[bkr], writes=[o_r[c]])
        else:
            p.op("dve", lambda e: e.tensor_copy(out=o_sb[:, c, sl], in_=bk[:]), reads=[bkr], writes=[o_r[c]])
    linear_fm(p, dn, h, h_r, kch, w, D, epi_o)
    hn, hn_r = rmsnorm_fm(p, dn, o_sb, o_r, D // 128, g_sb, g_r, out_dt=F32, inplace=True)
    xsrc = xT.rearrange("(c q) n -> q c n", q=128)
    odst = oT.rearrange("(c q) n -> q c n", q=128)
    xst = [p.sbuf([128, nt], F32, "xst") for _ in range(3)]
    xst_r = [Res() for _ in range(3)]
    xch = [p.chan("xch") for _ in range(3)]
    for c in range(D // 128):
        i = c % 3
        p.dma("sp", xch[i], xst[i][:], xsrc[:, c, :], writes=[xst_r[i]])
        p.op("dve" if c % 2 == 0 else "pool", lambda e, i=i, c=c: e.tensor_tensor(
            out=xst[i][:], in0=xst[i][:], in1=hn[:, c, :], op=ALU.add),
            reads=[hn_r[c], xst_r[i]], writes=[xst_r[i]])
        p.dma("sp", xch[i], odst[:, c, :], xst[i][:], reads=[xst_r[i]], is_output=True)
    p.finish()
    return nc


class TPool:
    def __init__(self, p, shape, dt, n, name="tp", psum=False):
        mk = p.psum if psum else p.sbuf
        self.t = [mk(shape, dt, name) for _ in range(n)]
        self.r = [Res() for _ in range(n)]
        self.i = 0

    def get(self):
        i = self.i
        self.i = (i + 1) % len(self.t)
        return self.t[i], self.r[i]


def o_tt(p, eng, out, in0, in1, op, reads, writes):
    return p.op(eng, lambda e: e.tensor_tensor(out=out, in0=in0, in1=in1, op=op), reads=reads, writes=writes)


def o_ts(p, eng, out, in0, s1, s2, op0, op1, reads, writes):
    if s2 is None:
        return p.op(eng, lambda e: e.tensor_scalar(out, in0, s1, None, op0), reads=reads, writes=writes)
    return p.op(eng, lambda e: e.tensor_scalar(out, in0, s1, s2, op0, op1), reads=reads, writes=writes)


def o_stt(p, out, in0, scalar, in1, op0, op1, reads, writes):
    return p.op("dve", lambda e: e.scalar_tensor_tensor(out=out, in0=in0, scalar=scalar, in1=in1, op0=op0, op1=op1),
                reads=reads, writes=writes)


def o_act(p, out, in_, func, reads, writes, scale=1.0, bias=None):
    if bias is None:
        return p.op("act", lambda e: e.activation(out=out, in_=in_, func=func, scale=scale), reads=reads, writes=writes)
    return p.op("act", lambda e: e.activation(out=out, in_=in_, func=func, scale=scale, bias=bias),
                reads=reads, writes=writes)


def gelu_tanh(p, tp, out, x, xr, outr, eng="pool"):
    t1, r1 = tp.get()
    o_tt(p, eng, t1[:], x, x, ALU.mult, [xr], [r1])
    o_ts(p, eng, t1[:], t1[:], 0.044715, 1.0, ALU.mult, ALU.add, [r1], [r1])
    o_tt(p, eng, t1[:], t1[:], x, ALU.mult, [r1, xr], [r1])
    o_act(p, t1[:], t1[:], AF.Sigmoid, [r1], [r1], scale=1.5957691216057308)
    o_tt(p, eng, out, t1[:], x, ALU.mult, [r1, xr], [outr])


def build_lru(tlen=T):
    nc = bass.Bass("TRN2", target_bir_lowering=False)
    CH = 512
    gateT = nc.dram_tensor("gateT", [CH, tlen], F32, kind="ExternalInput").ap()
    xbT = nc.dram_tensor("xbT", [CH, tlen], F32, kind="ExternalInput").ap()
    vecs = nc.dram_tensor("vecs", [128, 4, 8], F32, kind="ExternalInput").ap()
    wr = nc.dram_tensor("wr", [2, 256, 256], F32, kind="ExternalInput").ap()
    wi = nc.dram_tensor("wi", [2, 256, 256], F32, kind="ExternalInput").ap()
    yT = nc.dram_tensor("yT", [CH, tlen], F32, kind="ExternalOutput").ap()
    p = Prog(nc)
    ntb = tlen // 512
    vec = p.sbuf([128, 4, 8], F32, "vec")
    vec_r = Res()
    p.dma("sp", p.chan(), vec[:], vecs, writes=[vec_r])
    c8 = p.sbuf([128, 4], F32, "c8")
    c8_r = Res()
    o_act(p, c8[:], vec[:, :, 7], AF.Exp, [vec_r], [c8_r], scale=-1.0)
    o_ts(p, "dve", c8[:], c8[:], 1.0, None, ALU.add, None, [c8_r], [c8_r])
    o_act(p, c8[:], c8[:], AF.Ln, [c8_r], [c8_r])
    o_ts(p, "dve", c8[:], c8[:], -8.0, None, ALU.mult, None, [c8_r], [c8_r])
    wsb = {}
    wch = p.chan()
    for nm, wd in (("r", wr), ("i", wi)):
        t = p.sbuf([128, 2, 2, 256], BF16, "w" + nm)
        r = Res()
        wch = p.chan()
        for n in range(2):
            p.dma("pool", wch, t[:, n, :, :], wd[n].rearrange("(c q) d -> q c d", q=128), writes=[r])
        wsb[nm] = (t, r)
    xin = TPool(p, [128, 515], F32, 4, "xin")
    gin = TPool(p, [128, 512], F32, 3, "gin")
    xch = [p.chan() for _ in range(4)]
    gch = [p.chan() for _ in range(3)]
    och = [p.chan() for _ in range(3)]
    xcp = TPool(p, [128, 512], F32, 4, "xc")
    xcb = TPool(p, [128, 512], BF16, 4, "xcb")
    tmp = TPool(p, [128, 512], F32, 8, "tmp")
    hp = TPool(p, [128, 512], F32, 8, "h")
    op_ = TPool(p, [128, 512], F32, 3, "o")
    psp = TPool(p, [128, 512], F32, 8, "ps", psum=True)
    hprev = {}
    xsrc = xbT.rearrange("(c q) n -> q c n", q=128)
    gsrc = gateT.rearrange("(c q) n -> q c n", q=128)
    ydst = yT.rearrange("(c q) n -> q c n", q=128)
    for tb in range(ntb):
        t0 = tb * 512
        for n in range(2):
            xcs = []
            for cc in range(2):
                c = 2 * n + cc
                i = xin.i
                xt, xr = xin.get()
                if tb == 0:
                    p.op("pool", lambda e, xt=xt: e.memset(xt[:, 0:3], 0.0), writes=[xr])
                    p.dma("sp", xch[i], xt[:, 3:515], xsrc[:, c, 0:512], writes=[xr])
                else:
                    p.dma("sp", xch[i], xt[:, 0:515], xsrc[:, c, t0 - 3:t0 + 512], writes=[xr])
                xc, xcr = xcp.get()
                o_ts(p, "dve", xc[:], xt[:, 3:515], vec[:, c, 3:4], vec[:, c, 4:5], ALU.mult, ALU.add, [xr, vec_r], [xcr])
                for j in range(3):
                    o_stt(p, xc[:], xt[:, j:j + 512], vec[:, c, j:j + 1], xc[:], ALU.mult, ALU.add, [xr, vec_r, xcr], [xcr])
                xb_, xbr = xcb.get()
                p.op("act", lambda e, xb_=xb_, xc=xc: e.copy(out=xb_[:], in_=xc[:]), reads=[xcr], writes=[xbr])
                xcs.append((xc, xcr, xb_, xbr))
            for dc in range(2):
                c = 2 * n + dc
                xc, xcr = xcs[dc][0], xcs[dc][1]
                pr, prr = psp.get()
                pi, pir = psp.get()
                for (pt, ptr, nm) in ((pr, prr, "r"), (pi, pir, "i")):
                    wt, wtr = wsb[nm]
                    for cc in range(2):
                        p.op("pe", lambda e, pt=pt, wt=wt, cc=cc, dc=dc, n=n, xb_=xcs[cc][2]: e.matmul(
                            pt[:], wt[:, n, cc, dc * 128:(dc + 1) * 128], xb_[:], start=(cc == 0), stop=(cc == 1)),
                            reads=[wtr, xcs[cc][3]], writes=[ptr])
                sr, srr = tmp.get()
                si, sir = tmp.get()
                o_act(p, sr[:], pr[:], AF.Sigmoid, [prr, vec_r], [srr], bias=vec[:, c, 5:6])
                o_act(p, si[:], pi[:], AF.Sigmoid, [pir, vec_r], [sir], bias=vec[:, c, 6:7])
                a_, ar = tmp.get()
                o_act(p, a_[:], sr[:], AF.Exp, [srr, c8_r], [ar], scale=c8[:, c:c + 1])
                m_, mr = tmp.get()
                o_tt(p, "pool", m_[:], a_[:], a_[:], ALU.mult, [ar], [mr])
                o_ts(p, "pool", m_[:], m_[:], -1.0, 1.0, ALU.mult, ALU.add, [mr], [mr])
                o_act(p, m_[:], m_[:], AF.Sqrt, [mr], [mr])
                o_tt(p, "pool", m_[:], m_[:], si[:], ALU.mult, [mr, sir], [mr])
                o_tt(p, "dve", m_[:], m_[:], xc[:], ALU.mult, [mr, xcr], [mr])
                h_, hr = hp.get()
                if c in hprev:
                    ph, phr = hprev[c]
                    p.op("dve", lambda e, h_=h_, a_=a_, m_=m_, ph=ph: e.tensor_tensor_scan(
                        out=h_[:], data0=a_[:], data1=m_[:], initial=ph[:, 511:512], op0=ALU.mult, op1=ALU.add),
                        reads=[ar, mr, phr], writes=[hr])
                else:
                    p.op("dve", lambda e, h_=h_, a_=a_, m_=m_: e.tensor_tensor_scan(
                        out=h_[:], data0=a_[:], data1=m_[:], initial=0.0, op0=ALU.mult, op1=ALU.add),
                        reads=[ar, mr], writes=[hr])
                hprev[c] = (h_, hr)
                gi_ = gin.i
                gt, gr_ = gin.get()
                p.dma("sp", gch[gi_], gt[:], gsrc[:, c, t0:t0 + 512], writes=[gr_])
                oi = op_.i
                ot, otr = op_.get()
                gelu_tanh(p, tmp, ot[:], gt[:], gr_, otr, eng="pool")
                o_tt(p, "dve", ot[:], ot[:], h_[:], ALU.mult, [otr, hr], [otr])
                p.dma("sp", och[oi], ydst[:, c, t0:t0 + 512], ot[:], reads=[otr], is_output=True)
    p.finish()
    return nc


TWO_PI = 6.283185307179586


def frac_wrap(p, x, ti, tf, reads, r):
    p.op("dve", lambda e: e.tensor_copy(out=ti, in_=x), reads=reads, writes=[r])
    p.op("dve", lambda e: e.tensor_copy(out=tf, in_=ti), reads=reads, writes=[r])
    o_tt(p, "dve", x, x, tf, ALU.subtract, reads, [r])
    o_stt(p, tf, x, 0.5, x, ALU.is_gt, ALU.subtract, reads, [r])
    o_stt(p, x, tf, 0.5, tf, ALU.is_gt, ALU.subtract, reads, [r])


def sincos_turns(p, S, C, ph, tabs, halfpi_ap, reads, rS, rC, rt):
    o_act(p, S, ph, AF.Sin, reads, [rS], scale=TWO_PI)
    o_act(p, tabs, ph, AF.Abs, reads, [rt])
    o_act(p, C, tabs, AF.Sin, [rt], [rC], scale=-TWO_PI, bias=halfpi_ap)


def s5_pre(p, lr, li, ldt, shape, pi_ap, tagr):
    r = Res()
    mk = lambda nm: p.sbuf(shape, F32, "s5" + nm)
    dt, mag, f0, f0c, sn, cs = mk("dt"), mk("mag"), mk("f0"), mk("f0c"), mk("sn"), mk("cs")
    are, aim, den, f1 = mk("are"), mk("aim"), mk("den"), mk("f1")
    t1, t2, gre, gim = f0c, sn, cs, dt
    R = [tagr, r]
    o_act(p, dt[:], ldt, AF.Exp, R, [r])
    o_tt(p, "dve", mag[:], lr, dt[:], ALU.mult, R, [r])
    o_act(p, mag[:], mag[:], AF.Exp, R, [r])
    o_tt(p, "dve", f0[:], li, dt[:], ALU.mult, R, [r])
    o_ts(p, "dve", f0[:], f0[:], 1.0 / TWO_PI, None, ALU.mult, None, R, [r])
    ti = p.sbuf(shape, mybir.dt.int32, "s5ti")
    frac_wrap(p, f0[:], ti[:], f0c[:], R, r)
    sincos_turns(p, sn[:], cs[:], f0[:], f0c[:], pi_ap, R, r, r, r)
    o_tt(p, "dve", are[:], mag[:], cs[:], ALU.mult, R, [r])
    o_tt(p, "dve", aim[:], mag[:], sn[:], ALU.mult, R, [r])
    o_tt(p, "dve", den[:], lr, lr, ALU.mult, R, [r])
    o_tt(p, "dve", t1[:], li, li, ALU.mult, R, [r])
    o_tt(p, "dve", den[:], den[:], t1[:], ALU.add, R, [r])
    p.op("dve", lambda e: e.reciprocal(out=den[:], in_=den[:]), reads=R, writes=[r])
    o_ts(p, "dve", t1[:], are[:], -1.0, None, ALU.add, None, R, [r])
    o_tt(p, "dve", gre[:], t1[:], lr, ALU.mult, R, [r])
    o_tt(p, "dve", t2[:], aim[:], li, ALU.mult, R, [r])
    o_tt(p, "dve", gre[:], gre[:], t2[:], ALU.add, R, [r])
    o_tt(p, "dve", gre[:], gre[:], den[:], ALU.mult, R, [r])
    o_tt(p, "dve", gim[:], aim[:], lr, ALU.mult, R, [r])
    o_tt(p, "dve", t2[:], t1[:], li, ALU.mult, R, [r])
    o_tt(p, "dve", gim[:], gim[:], t2[:], ALU.subtract, R, [r])
    o_tt(p, "dve", gim[:], gim[:], den[:], ALU.mult, R, [r])
    o_ts(p, "dve", f1[:], f0[:], 64.0, None, ALU.mult, None, R, [r])
    frac_wrap(p, f1[:], ti[:], den[:], R, r)
    return dict(mag=mag, f0=f0, f1=f1, gre=gre, gim=gim), r


def emit_s5(p, uT, lamP, lamR, bmat, cmat, dsk, yT, tlen):
    nc = p.nc
    ntb = tlen // 512
    pi_t = p.sbuf([128, 1], F32, "pi")
    pi_r = Res()
    p.op("pool", lambda e: e.memset(pi_t[:], 1.5707963267948966), writes=[pi_r])
    ch0 = p.chan()
    lp = p.sbuf([128, 3, 8], F32, "lamP")
    lp_r = Res()
    p.dma("sp", p.chan(), lp[:], lamP, writes=[lp_r])
    lrw = p.sbuf([32, 3, 1024], F32, "lamR")
    lrw_r = Res()
    p.dma("sp", p.chan(), lrw[:], lamR, writes=[lrw_r])
    p.op("dve", lambda e: e.tensor_copy(out=lp[:, 0, 0:1], in_=lp[:, 0, 0:1]), reads=[pi_r, lp_r], writes=[lp_r])
    P, Pr = s5_pre(p, lp[:, 0, :], lp[:, 1, :], lp[:, 2, :], [128, 8], pi_t[:, 0:1], lp_r)
    p.op("dve", lambda e: e.tensor_copy(out=lrw[:, 0, 0:1], in_=lrw[:, 0, 0:1]), reads=[pi_r, lrw_r], writes=[lrw_r])
    Rw, Rr = s5_pre(p, lrw[:, 0, :], lrw[:, 1, :], lrw[:, 2, :], [32, 1024], pi_t[0:32, 0:1], lrw_r)
    bsb = p.sbuf([32, 2, 1024], F32, "bsb")
    b_r = Res()
    p.dma("sp", p.chan(), bsb[:], bmat, writes=[b_r])
    BT = p.sbuf([32, 2, 1024], F32, "BT")
    BT_r = Res()
    tb1 = p.sbuf([32, 1024], F32, "tb1")
    tb1_r = Res()
    o_tt(p, "dve", tb1[:], bsb[:, 1, :], Rw["gim"][:], ALU.mult, [b_r, Rr], [tb1_r])
    o_tt(p, "dve", BT[:, 0, :], bsb[:, 0, :], Rw["gre"][:], ALU.mult, [b_r, Rr], [BT_r])
    o_tt(p, "dve", BT[:, 0, :], BT[:, 0, :], tb1[:], ALU.subtract, [BT_r, tb1_r], [BT_r])
    o_tt(p, "dve", tb1[:], bsb[:, 1, :], Rw["gre"][:], ALU.mult, [b_r, Rr, BT_r], [tb1_r])
    o_tt(p, "dve", BT[:, 1, :], bsb[:, 0, :], Rw["gim"][:], ALU.mult, [b_r, Rr], [BT_r])
    o_tt(p, "dve", BT[:, 1, :], BT[:, 1, :], tb1[:], ALU.add, [BT_r, tb1_r], [BT_r])
    csb = p.sbuf([128, 2, 8, 32], F32, "csb")
    c_r = Res()
    p.dma("sp", p.chan(), csb[:], cmat, writes=[c_r])
    o_ts(p, "dve", csb[:, 1, :, :], csb[:, 1, :, :], -1.0, None, ALU.mult, None, [c_r], [c_r])
    dsb = p.sbuf([32, 8], F32, "dsb")
    d_r = Res()
    p.dma("sp", p.chan(), dsb[:], dsk, writes=[d_r])
    ia = p.sbuf([128, 512], mybir.dt.int32, "ia")
    ib = p.sbuf([128, 512], mybir.dt.int32, "ib")
    A0 = p.sbuf([128, 512], F32, "A0")
    B0 = p.sbuf([128, 512], F32, "B0")
    io_r = Res()
    p.op("pool", lambda e: e.iota(ia[:], [[1, 8], [0, 64]], base=0, channel_multiplier=0), writes=[io_r])
    p.op("pool", lambda e: e.iota(ib[:], [[0, 8], [1, 64]], base=0, channel_multiplier=0), writes=[io_r])
    p.op("dve", lambda e: e.tensor_copy(out=A0[:], in_=ia[:]), reads=[io_r], writes=[io_r])
    p.op("dve", lambda e: e.tensor_copy(out=B0[:], in_=ib[:]), reads=[io_r], writes=[io_r])
    ones = p.sbuf([128, 512], F32, "ones5")
    p.op("pool", lambda e: e.memset(ones[:], 1.0), writes=[io_r])

    usrc = uT.rearrange("(m q) n -> q m n", q=32)
    ydst = yT.rearrange("(m q) n -> q m n", q=32)
    upool = TPool(p, [32, tlen], F32, 2, "u")
    uch = [p.chan(), p.chan()]
    rho_p = TPool(p, [128, 512], F32, 2, "rho")
    psb = TPool(p, [128, 512], F32, 4, "psbu", psum=True)
    psy = TPool(p, [32, 512], F32, 2, "psy", psum=True)
    tp = TPool(p, [128, 512], F32, 16, "s5t")
    tip = TPool(p, [128, 512], mybir.dt.int32, 2, "s5ti")
    zp = TPool(p, [128, 512], F32, 6, "s5z")
    t32 = TPool(p, [32, 512], F32, 6, "s5o")
    och = [p.chan() for _ in range(6)]
    for m in range(8):
        ui = upool.i
        u, u_r = upool.get()
        p.dma("sp", uch[ui], u[:], usrc[:, m, :], writes=[u_r])
        rho, rho_r = rho_p.get()
        o_ts(p, "dve", rho[:], ones[:], P["mag"][:, m:m + 1], None, ALU.mult, None, [io_r, Pr], [rho_r])
        zprev = None
        for tb in range(ntb):
            sl = slice(tb * 512, (tb + 1) * 512)
            pre, prer = psb.get()
            pim, pimr = psb.get()
            p.op("pe", lambda e, pre=pre, u=u, sl=sl, m=m: e.matmul(pre[:], BT[:, 0, m * 128:(m + 1) * 128], u[:, sl], start=True, stop=True),
                 reads=[BT_r, u_r], writes=[prer])
            p.op("pe", lambda e, pim=pim, u=u, sl=sl, m=m: e.matmul(pim[:], BT[:, 1, m * 128:(m + 1) * 128], u[:, sl], start=True, stop=True),
                 reads=[BT_r, u_r], writes=[pimr])
            bre, brer = tp.get()
            bim, bimr = tp.get()
            p.op("act", lambda e, bre=bre, pre=pre: e.copy(out=bre[:], in_=pre[:]), reads=[prer], writes=[brer])
            p.op("act", lambda e, bim=bim, pim=pim: e.copy(out=bim[:], in_=pim[:]), reads=[pimr], writes=[bimr])
            ph, phr = tp.get()
            phc, phcr = tp.get()
            o_ts(p, "dve", ph[:], A0[:], float(8 * tb), P["f1"][:, m:m + 1], ALU.add, ALU.mult, [io_r, Pr], [phr])
            o_stt(p, ph[:], B0[:], P["f0"][:, m:m + 1], ph[:], ALU.mult, ALU.add, [io_r, Pr, phr], [phr])
            ti, tir = tip.get()
            frac_wrap(p, ph[:], ti[:], phc[:], [phr, phcr, tir], phr)
            S, Sr = tp.get()
            C, Cr = tp.get()
            sincos_turns(p, S[:], C[:], ph[:], phc[:], pi_t[:, 0:1], [phr, pi_r], Sr, Cr, phcr)
            wre, wrer = tp.get()
            wim, wimr = tp.get()
            t1, t1r = tp.get()
            t2, t2r = tp.get()
            o_tt(p, "pool", wre[:], C[:], bre[:], ALU.mult, [Cr, brer], [wrer])
            o_tt(p, "pool", t1[:], S[:], bim[:], ALU.mult, [Sr, bimr], [t1r])
            o_tt(p, "pool", wre[:], wre[:], t1[:], ALU.add, [wrer, t1r], [wrer])
            o_tt(p, "dve", wim[:], C[:], bim[:], ALU.mult, [Cr, bimr], [wimr])
            o_tt(p, "dve", t2[:], S[:], bre[:], ALU.mult, [Sr, brer], [t2r])
            o_tt(p, "dve", wim[:], wim[:], t2[:], ALU.subtract, [wimr, t2r], [wimr])
            zre, zrer = zp.get()
            zim, zimr = zp.get()
            for (z, zr, w_, wr_, k) in ((zre, zrer, wre, wrer, 0), (zim, zimr, wim, wimr, 1)):
                if zprev is None:
                    p.op("dve", lambda e, z=z, w_=w_, rho=rho: e.tensor_tensor_scan(
                        out=z[:], data0=rho[:], data1=w_[:], initial=0.0, op0=ALU.mult, op1=ALU.add),
                        reads=[rho_r, wr_], writes=[zr])
                else:
                    pz, pzr = zprev[k]
                    p.op("dve", lambda e, z=z, w_=w_, rho=rho, pz=pz: e.tensor_tensor_scan(
                        out=z[:], data0=rho[:], data1=w_[:], initial=pz[:, 511:512], op0=ALU.mult, op1=ALU.add),
                        reads=[rho_r, wr_, pzr], writes=[zr])
            zprev = ((zre, zrer), (zim, zimr))
            sre, srer = tp.get()
            sim, simr = tp.get()
            t3, t3r = tp.get()
            t4, t4r = tp.get()
            o_tt(p, "pool", sre[:], C[:], zre[:], ALU.mult, [Cr, zrer], [srer])
            o_tt(p, "pool", t3[:], S[:], zim[:], ALU.mult, [Sr, zimr], [t3r])
            o_tt(p, "pool", sre[:], sre[:], t3[:], ALU.subtract, [srer, t3r], [srer])
            o_tt(p, "dve", sim[:], S[:], zre[:], ALU.mult, [Sr, zrer], [simr])
            o_tt(p, "pool", t4[:], C[:], zim[:], ALU.mult, [Cr, zimr], [t4r])
            o_tt(p, "dve", sim[:], sim[:], t4[:], ALU.add, [simr, t4r], [simr])
            py, pyr = psy.get()
            p.op("pe", lambda e, py=py, sre=sre, m=m: e.matmul(py[:], csb[:, 0, m, :], sre[:], start=True, stop=False),
                 reads=[c_r, srer], writes=[pyr])
            p.op("pe", lambda e, py=py, sim=sim, m=m: e.matmul(py[:], csb[:, 1, m, :], sim[:], start=False, stop=True),
                 reads=[c_r, simr], writes=[pyr])
            y2, y2r = t32.get()
            o_stt(p, y2[:], u[:, sl], dsb[:, m:m + 1], py[:], ALU.mult, ALU.add, [u_r, d_r, pyr], [y2r])
            oi = t32.i
            yo, yor = t32.get()
            gelu_tanh(p, t32, yo[:], y2[:], y2r, yor, eng="pool")
            p.dma("sp", och[oi], ydst[:, m, sl], yo[:], reads=[yor], is_output=True)


def build_s5(tlen=T):
    nc = bass.Bass("TRN2", target_bir_lowering=False)
    uT = nc.dram_tensor("uT", [256, tlen], F32, kind="ExternalInput").ap()
    lamP = nc.dram_tensor("lamP", [128, 3, 8], F32, kind="ExternalInput").ap()
    lamR = nc.dram_tensor("lamR", [32, 3, 1024], F32, kind="ExternalInput").ap()
    bmat = nc.dram_tensor("bmat", [32, 2, 1024], F32, kind="ExternalInput").ap()
    cmat = nc.dram_tensor("cmat", [128, 2, 8, 32], F32, kind="ExternalInput").ap()
    dsk = nc.dram_tensor("dsk", [32, 8], F32, kind="ExternalInput").ap()
    yT = nc.dram_tensor("yT", [256, tlen], F32, kind="ExternalOutput").ap()
    p = Prog(nc)
    emit_s5(p, uT, lamP, lamR, bmat, cmat, dsk, yT, tlen)
    p.finish()
    return nc


def s5_host_layout(lam_re, lam_im, log_dt, b_re, b_im, c_re, c_im, d_skip, q):
    g0 = 16 * q
    lr = lam_re[g0:g0 + 16]
    li = lam_im[g0:g0 + 16]
    ld = np.broadcast_to(log_dt[g0:g0 + 16, None], (16, 64))
    st = np.stack([lr, li, ld], 0).reshape(3, 8, 2, 64)
    lamP = np.ascontiguousarray(st.transpose(2, 3, 0, 1).reshape(128, 3, 8))
    lamR = np.ascontiguousarray(np.broadcast_to(st.reshape(1, 3, 1024), (32, 3, 1024)))
    bmat = np.zeros((32, 2, 8, 2, 64), np.float32)
    cmat = np.zeros((2, 64, 2, 8, 2, 16), np.float32)
    for k, (bb, cc) in enumerate(((b_re, c_re), (b_im, c_im))):
        bq = bb[g0:g0 + 16].reshape(8, 2, 64, 16)
        cq = cc[g0:g0 + 16].reshape(8, 2, 16, 64)
        for gl in range(2):
            bmat[gl * 16:(gl + 1) * 16, k, :, gl, :] = bq[:, gl].transpose(2, 0, 1)
            cmat[gl, :, k, :, gl, :] = cq[:, gl].transpose(2, 0, 1)
    bmat = bmat.reshape(32, 2, 1024)
    cmat = cmat.reshape(128, 2, 8, 32)
    dsk = np.ascontiguousarray(d_skip[256 * q:256 * (q + 1)].reshape(8, 32).T)
    return dict(lamP=lamP, lamR=lamR, bmat=bmat, cmat=cmat, dsk=dsk)


RTB = 128
_DBG = [99]


def build_rw1(tlen=T):
    nc = bass.Bass("TRN2", target_bir_lowering=False)
    di = lambda n, sh: nc.dram_tensor(n, sh, F32, kind="ExternalInput").ap()
    do = lambda n, sh: nc.dram_tensor(n, sh, F32, kind="ExternalOutput").ap()
    zrkv = di("zrkv", [64, 3, 4, tlen + 1])
    zw = di("zw", [64, tlen + 1])
    za = di("za", [64, tlen + 1])
    zg0 = di("zg0", [128, tlen + 1])
    zg1 = di("zg1", [128, tlen + 1])
    mu_rkv = di("mu_rkv", [64, 3, 4, RTB])
    mu_l = di("mu_l", [128, 4])
    w2 = di("w2", [64, 256])
    a2 = di("a2", [64, 256])
    g2a = di("g2a", [128, 256])
    g2b = di("g2b", [128, 256])
    cvec = di("cvec", [64, 5, 4, RTB])
    outs = {n: do(n, [64, 4, tlen]) for n in ("at", "rt", "bt", "kt", "xv", "gg", "bs", "ec")}
    p = Prog(nc)
    ch0 = p.chan()
    N = 4 * RTB

    def const(ap, shape, nm):
        t = p.sbuf(shape, F32, nm)
        r = Res()
        p.dma("sp", p.chan(), t[:], ap, writes=[r])
        return t, r
    mu_t, mu_r = const(mu_rkv, [64, 3, 4, RTB], "mu")
    mul_t, mul_r = const(mu_l, [128, 4], "mul")
    w2_t, w2_r = const(w2, [64, 256], "w2")
    a2_t, a2_r = const(a2, [64, 256], "a2")
    g2a_t, g2a_r = const(g2a, [128, 256], "g2a")
    g2b_t, g2b_r = const(g2b, [128, 256], "g2b")
    cv_t, cv_r = const(cvec, [64, 5, 4, RTB], "cv")
    cst = p.sbuf([128, 4], F32, "cst")
    cst_r = Res()
    p.op("pool", lambda e: e.memset(cst[:, 0:1], 1.0), writes=[cst_r])
    p.op("pool", lambda e: e.memset(cst[:, 1:2], -0.5), writes=[cst_r])
    ones_in = di("ones_in", [64, 256])
    ones_f32, ones_r = const(ones_in, [64, 256], "ones_f32")
    ones = ones_f32[:, 0:64]
    cmask = p.sbuf([64, RTB], F32, "cmask")
    p.op("pool", lambda e: e.memset(cmask[:], 1.0), writes=[cst_r])
    for c in range(RTB // 64):
        p.op("pool", lambda e, c=c: e.memset(cmask[:, c * 64:c * 64 + 1], 0.0), writes=[cst_r])

    zin = TPool(p, [64, 3, 4, RTB + 1], F32, 2, "zin")
    zch = [p.chan(), p.chan()]
    lin = TPool(p, [128, 4, RTB + 1], F32, 2, "lin")
    lch = [p.chan(), p.chan()]
    tp = TPool(p, [64, 4, RTB], F32, 30, "r1t")
    tl = TPool(p, [128, RTB], F32, 8, "r1l")
    ps = TPool(p, [64, 4, RTB], F32, 6, "r1ps", psum=True)
    psg = TPool(p, [64, 4, RTB], F32, 2, "r1pg", psum=True)
    op_ = TPool(p, [64, 4, RTB], F32, 16, "r1o")
    och = [p.chan() for _ in range(16)]

    def emit_out(name, t, r, t0):
        i = op_.i
        o, o_r = op_.get()
        p.op("pool", lambda e: e.tensor_copy(out=o[:], in_=t), reads=[r], writes=[o_r])
        p.dma("sp", och[i], outs[name][:, :, t0:t0 + RTB], o[:], reads=[o_r], is_output=True)

    for blk in range(tlen // RTB):
        t0 = blk * RTB
        zi = zin.i
        z, z_r = zin.get()
        p.dma("sp", zch[zi], z[:], zrkv[:, :, :, t0:t0 + RTB + 1], writes=[z_r])
        li_ = lin.i
        l, l_r = lin.get()
        p.dma("sp", lch[li_], l[0:64, 0, :], zw[:, t0:t0 + RTB + 1], writes=[l_r])
        p.dma("sp", lch[li_], l[0:64, 1, :], za[:, t0:t0 + RTB + 1], writes=[l_r])
        p.dma("sp", lch[li_], l[:, 2, :], zg0[:, t0:t0 + RTB + 1], writes=[l_r])
        p.dma("sp", lch[li_], l[:, 3, :], zg1[:, t0:t0 + RTB + 1], writes=[l_r])
        xs = []
        for j in range(3):
            x, x_r = tp.get()
            o_tt(p, "pool", x[:], z[:, j, :, 0:RTB], z[:, j, :, 1:RTB + 1], ALU.subtract, [z_r], [x_r])
            o_tt(p, "pool", x[:], x[:], mu_t[:, j, :, :], ALU.mult, [x_r, mu_r], [x_r])
            o_tt(p, "pool", x[:], x[:], z[:, j, :, 1:RTB + 1], ALU.add, [x_r, z_r], [x_r])
            xs.append((x, x_r))
        (xr, xr_r), (xk, xk_r), (xv, xv_r) = xs
        if _DBG[0] == 1:
            emit_out('xv', xv[:], xv_r, t0)
            continue
        ls = []
        for j, np_ in ((0, 64), (1, 64), (2, 128), (3, 128)):
            x, x_r = tl.get()
            o_tt(p, "dve", x[0:np_, :], l[0:np_, j, 0:RTB], l[0:np_, j, 1:RTB + 1], ALU.subtract, [l_r], [x_r])
            o_stt(p, x[0:np_, :], x[0:np_, :], mul_t[0:np_, j:j + 1], l[0:np_, j, 1:RTB + 1], ALU.mult, ALU.add,
                  [x_r, l_r, mul_r], [x_r])
            ls.append((x, x_r, np_))
        xw, xw_r, _ = ls[0]
        xa, xa_r, _ = ls[1]
        o_act(p, xw[0:64, :], xw[0:64, :], AF.Tanh, [xw_r], [xw_r])
        for (x, x_r, np_) in ls[2:]:
            o_act(p, x[0:np_, :], x[0:np_, :], AF.Sigmoid, [x_r], [x_r])
        if _DBG[0] == 2:
            emit_out('xv', xv[:], xv_r, t0)
            continue
        pw, pw_r = ps.get()
        pa, pa_r = ps.get()
        pg, pg_r = psg.get()
        for h in range(4):
            hs = slice(h * 64, (h + 1) * 64)
            p.op("pe", lambda e, h=h, hs=hs: e.matmul(pw[:, h, :], w2_t[:, hs], xw[0:64, :], start=True, stop=True),
                 reads=[w2_r, xw_r], writes=[pw_r])
            p.op("pe", lambda e, h=h, hs=hs: e.matmul(pa[:, h, :], a2_t[:, hs], xa[0:64, :], start=True, stop=True),
                 reads=[a2_r, xa_r], writes=[pa_r])
            p.op("pe", lambda e, h=h, hs=hs: e.matmul(pg[:, h, :], g2a_t[:, hs], ls[2][0][:, :], start=True, stop=False),
                 reads=[g2a_r, ls[2][1]], writes=[pg_r])
            p.op("pe", lambda e, h=h, hs=hs: e.matmul(pg[:, h, :], g2b_t[:, hs], ls[3][0][:, :], start=False, stop=True),
                 reads=[g2b_r, ls[3][1]], writes=[pg_r])
        if _DBG[0] == 3:
            emit_out('xv', xv[:], xv_r, t0)
            continue
        e2, e2_r = tp.get()
        o_tt(p, "dve", e2[:], pw[:], cv_t[:, 0, :, :], ALU.add, [pw_r, cv_r], [e2_r])
        o_act(p, e2[:], e2[:], AF.Exp, [e2_r], [e2_r], scale=-1.0)
        o_act(p, e2[:], e2[:], AF.Ln, [e2_r, cst_r], [e2_r], bias=cst[0:64, 0:1])
        o_act(p, e2[:], e2[:], AF.Exp, [e2_r, cst_r], [e2_r], scale=-1.0, bias=cst[0:64, 1:2])
        a_, a_r = tp.get()
        o_tt(p, "dve", a_[:], pa[:], cv_t[:, 1, :, :], ALU.add, [pa_r, cv_r], [a_r])
        o_act(p, a_[:], a_[:], AF.Sigmoid, [a_r], [a_r])
        gg, gg_r = tp.get()
        p.op("act", lambda e, gg=gg, pg=pg: e.copy(out=gg[:], in_=pg[:]), reads=[pg_r], writes=[gg_r])
        if _DBG[0] == 4:
            emit_out('xv', xv[:], xv_r, t0)
            continue
        kk, kk_r = tp.get()
        sq, sq_r = tp.get()
        o_tt(p, "dve", kk[:], xk[:], cv_t[:, 2, :, :], ALU.mult, [xk_r, cv_r], [kk_r])
        o_tt(p, "dve", sq[:], kk[:], kk[:], ALU.mult, [kk_r], [sq_r])
        if _DBG[0] == 411:
            emit_out('xv', sq[:], sq_r, t0)
            continue
        pss, pss_r = ps.get()
        for h in range(4):
            p.op("pe", lambda e, pss=pss, sq=sq, h=h: e.matmul(pss[:, h, :], ones, sq[:, h, :], start=True, stop=True),
                 reads=[ones_r, sq_r], writes=[pss_r])
        o_ts(p, "dve", sq[:], pss[:], 1.0, 1e-24, ALU.mult, ALU.max, [pss_r, sq_r], [sq_r])
        if _DBG[0] == 412:
            emit_out('xv', sq[:], sq_r, t0)
            continue
        o_act(p, sq[:], sq[:], AF.Sqrt, [sq_r], [sq_r])
        p.op("dve", lambda e, sq=sq: e.reciprocal(out=sq[:], in_=sq[:]), reads=[sq_r], writes=[sq_r])
        o_tt(p, "dve", kk[:], kk[:], sq[:], ALU.mult, [kk_r, sq_r], [kk_r])
        if _DBG[0] == 41:
            emit_out('xv', kk[:], kk_r, t0)
            continue
        km, km_r = tp.get()
        o_ts(p, "pool", km[:], a_[:], -1.0, None, ALU.add, None, [a_r], [km_r])
        o_tt(p, "pool", km[:], km[:], cv_t[:, 3, :, :], ALU.mult, [km_r, cv_r], [km_r])
        o_stt(p, km[:], km[:], 1.0, xk[:], ALU.add, ALU.mult, [km_r, xk_r], [km_r])
        vb, vb_r = tp.get()
        o_tt(p, "pool", vb[:], kk[:], a_[:], ALU.mult, [kk_r, a_r], [vb_r])
        if _DBG[0] == 42:
            emit_out('xv', vb[:], vb_r, t0)
            continue
        rk, rk_r = tp.get()
        o_tt(p, "pool", rk[:], xr[:], km[:], ALU.mult, [xr_r, km_r], [rk_r])
        o_tt(p, "pool", rk[:], rk[:], cv_t[:, 4, :, :], ALU.mult, [rk_r, cv_r], [rk_r])
        pb, pb_r = ps.get()
        for h in range(4):
            p.op("pe", lambda e, pb=pb, rk=rk, h=h: e.matmul(pb[:, h, :], ones, rk[:, h, :], start=True, stop=True),
                 reads=[ones_r, rk_r], writes=[pb_r])
        bsv, bsv_r = tp.get()
        p.op("act", lambda e, bsv=bsv, pb=pb: e.copy(out=bsv[:], in_=pb[:]), reads=[pb_r], writes=[bsv_r])
        if _DBG[0] == 5:
            emit_out('xv', xv[:], xv_r, t0)
            continue
        cu, cu_r = tp.get()
        for h in range(4):
            p.op("dve", lambda e, h=h, cu=cu, e2=e2: e.tensor_tensor_scan(
                out=cu[:, h, :], data0=cmask[:], data1=e2[:, h, :], initial=0.0, op0=ALU.mult, op1=ALU.add),
                reads=[cst_r, e2_r], writes=[cu_r])
        cex, cex_r = tp.get()
        o_tt(p, "pool", cex[:], cu[:], e2[:], ALU.subtract, [cu_r, e2_r], [cex_r])
        ec, ec_r = tp.get()
        en, en_r = tp.get()
        o_act(p, ec[:], cu[:], AF.Exp, [cu_r], [ec_r], scale=-1.0)
        o_act(p, en[:], cu[:], AF.Exp, [cu_r], [en_r])
        o_act(p, cex[:], cex[:], AF.Exp, [cex_r], [cex_r], scale=-1.0)
        at, at_r = tp.get()
        o_stt(p, at[:], kk[:], -1.0, cex[:], ALU.mult, ALU.mult, [kk_r, cex_r], [at_r])
        rt, rt_r = tp.get()
        o_tt(p, "dve", rt[:], xr[:], ec[:], ALU.mult, [xr_r, ec_r], [rt_r])
        bt, bt_r = tp.get()
        o_tt(p, "pool", bt[:], vb[:], en[:], ALU.mult, [vb_r, en_r], [bt_r])
        kt, kt_r = tp.get()
        o_tt(p, "dve", kt[:], km[:], en[:], ALU.mult, [km_r, en_r], [kt_r])
        for nm, (t_, r_) in (("at", (at, at_r)), ("rt", (rt, rt_r)), ("bt", (bt, bt_r)), ("kt", (kt, kt_r)),
                             ("xv", (xv, xv_r)), ("gg", (gg, gg_r)), ("bs", (bsv, bsv_r)), ("ec", (ec, ec_r))):
            emit_out(nm, t_[:], r_, t0)
    p.finish()
    return nc


def build_rw2(tlen=T):
    nc = bass.Bass("TRN2", target_bir_lowering=False)
    nch = tlen // 64
    di = lambda n, sh: nc.dram_tensor(n, sh, F32, kind="ExternalInput").ap()
    AR = di("AR", [64, nch, 4, 2, 64])
    BK = di("BK", [64, nch, 4, 2, 64])
    TM = di("TM", [64, nch, 4, 5, 64])
    BS = di("BS", [64, nch, 4])
    GL = di("GL", [64, nch, 4])
    LN = di("LN", [64, 2, 4, 64])
    yo = nc.dram_tensor("yo", [64, nch, 4, 64], F32, kind="ExternalOutput").ap()
    p = Prog(nc)
    ch0 = p.chan()
    ln_t = p.sbuf([64, 2, 4, 64], F32, "ln")
    ln_r = Res()
    p.dma("sp", p.chan(), ln_t[:], LN, writes=[ln_r])
    gl_t = p.sbuf([64, nch, 4], F32, "gl")
    gl_r = Res()
    p.dma("sp", p.chan(), gl_t[:], GL, writes=[gl_r])
    bs_t = p.sbuf([64, nch, 4], F32, "bs")
    bs_r = Res()
    p.dma("sp", p.chan(), bs_t[:], BS, writes=[bs_r])
    ii = p.sbuf([64, 64], mybir.dt.int32, "ii")
    dif = p.sbuf([64, 64], F32, "dif")
    mk_r = Res()
    p.op("pool", lambda e: e.iota(ii[:], [[1, 64]], base=0, channel_multiplier=-1), writes=[mk_r])
    p.op("dve", lambda e: e.tensor_copy(out=dif[:], in_=ii[:]), reads=[mk_r], writes=[mk_r])
    msk = p.sbuf([64, 4, 4, 64], F32, "msk")
    mlow = p.sbuf([64, 4, 64], F32, "mlow")
    ident = p.sbuf([64, 4, 64], F32, "ident")
    for h in range(4):
        for q in range(4):
            op = ALU.is_gt if q % 2 == 0 else ALU.is_ge
            o_ts(p, "dve", msk[:, h, q, :], dif[:], 0.0, None, op, None, [mk_r], [mk_r])
        o_ts(p, "dve", mlow[:, h, :], dif[:], 0.0, None, ALU.is_lt, None, [mk_r], [mk_r])
        o_ts(p, "dve", ident[:, h, :], dif[:], 0.0, None, ALU.is_equal, None, [mk_r], [mk_r])

    arp = TPool(p, [64, 4, 2, 64], F32, 2, "ar")
    bkp = TPool(p, [64, 4, 2, 64], F32, 2, "bk")
    tmp_ = TPool(p, [64, 4, 5, 64], F32, 2, "tm")
    inch = [p.chan(), p.chan()]
    inch2 = [p.chan(), p.chan()]
    inch3 = [p.chan(), p.chan()]
    psA = TPool(p, [64, 4, 4, 64], F32, 1, "psA", psum=True)
    ps4 = TPool(p, [64, 4, 64], F32, 5, "ps4", psum=True)
    psX = TPool(p, [64, 4, 128], F32, 1, "psX", psum=True)
    sb4 = TPool(p, [64, 4, 64], F32, 24, "sb4")
    AT_p = TPool(p, [64, 4, 4, 64], F32, 2, "AT")
    X_p = TPool(p, [64, 4, 128], F32, 2, "X")
    WU_p = TPool(p, [64, 4, 128], F32, 2, "WU")
    S_p = TPool(p, [64, 4, 64], F32, 3, "S")
    st4 = TPool(p, [64, 4], F32, 8, "st4")
    outp = TPool(p, [64, 4, 64], F32, 3, "yo")
    och = [p.chan() for _ in range(3)]
    S, S_r = S_p.get()
    p.op("pool", lambda e: e.memset(S[:], 0.0), writes=[S_r])

    def mm4(ps, ps_r, lhs_fn, rhs_fn, reads):
        mm4g(ps, ps_r, [(lhs_fn, rhs_fn, reads)])

    def mm4g(ps, ps_r, terms):
        for h in range(4):
            for ti, (lhs_fn, rhs_fn, reads) in enumerate(terms):
                p.op("pe", lambda e, h=h, lhs_fn=lhs_fn, rhs_fn=rhs_fn, ti=ti: e.matmul(
                    ps[:, h, :], lhs_fn(h), rhs_fn(h), start=(ti == 0), stop=(ti == len(terms) - 1)),
                    reads=reads, writes=[ps_r])

    for n in range(nch):
        i_in = arp.i
        ar, ar_r = arp.get()
        bk, bk_r = bkp.get()
        tm, tm_r = tmp_.get()
        p.dma("sp", inch[i_in], ar[:], AR[:, n], writes=[ar_r])
        p.dma("sp", inch2[i_in], bk[:], BK[:, n], writes=[bk_r])
        p.dma("sp", inch3[i_in], tm[:], TM[:, n], writes=[tm_r])
        pA, pA_r = psA.get()
        for h in range(4):
            p.op("pe", lambda e, h=h: e.matmul(pA[:, h, 0:2, :], bk[:, h, 0, :], ar[:, h, :, :], start=True, stop=True),
                 reads=[bk_r, ar_r], writes=[pA_r])
            p.op("pe", lambda e, h=h: e.matmul(pA[:, h, 2:4, :], bk[:, h, 1, :], ar[:, h, :, :], start=True, stop=True),
                 reads=[bk_r, ar_r], writes=[pA_r])
        AT, AT_r = AT_p.get()
        o_tt(p, "dve", AT[:], pA[:], msk[:], ALU.mult, [pA_r, mk_r], [AT_r])
        pN, pN_r = ps4.get()
        mm4(pN, pN_r, lambda h: ar[:, h, 0, :], lambda h: bk[:, h, 0, :], [ar_r, bk_r])
        PT, PT_r = sb4.get()
        o_tt(p, "dve", PT[:], pN[:], mlow[:], ALU.mult, [pN_r, mk_r], [PT_r])
        P_, P_r = sb4.get()
        p.op("act", lambda e, P_=P_, AT=AT: e.copy(out=P_[:], in_=AT[:, :, 0, :]), reads=[AT_r], writes=[P_r])
        M, M_r = sb4.get()
        o_tt(p, "pool", M[:], AT[:, :, 0, :], ident[:], ALU.add, [AT_r, mk_r], [M_r])
        for lev in range(1, 6):
            pq, pq_r = ps4.get()
            mm4(pq, pq_r, lambda h, P_=P_: P_[:, h, :], lambda h, PT=PT: PT[:, h, :], [P_r, PT_r])
            PT2, PT2_r = sb4.get()
            p.op("act", lambda e, PT2=PT2, pq=pq: e.copy(out=PT2[:], in_=pq[:]), reads=[pq_r], writes=[PT2_r])
            if lev < 5:
                pq2, pq2_r = ps4.get()
                mm4(pq2, pq2_r, lambda h, PT=PT: PT[:, h, :], lambda h, P_=P_: P_[:, h, :], [P_r, PT_r])
                P2, P2_r = sb4.get()
                p.op("dve", lambda e, P2=P2, pq2=pq2: e.tensor_copy(out=P2[:], in_=pq2[:]), reads=[pq2_r], writes=[P2_r])
            pm, pm_r = ps4.get()
            mm4(pm, pm_r, lambda h, PT2=PT2: PT2[:, h, :], lambda h, M=M: M[:, h, :], [PT2_r, M_r])
            M2, M2_r = sb4.get()
            o_tt(p, "dve", M2[:], pm[:], M[:], ALU.add, [pm_r, M_r], [M2_r])
            M, M_r = M2, M2_r
            PT, PT_r = PT2, PT2_r
            if lev < 5:
                P_, P_r = P2, P2_r
        pv, pv_r = ps4.get()
        mm4(pv, pv_r, lambda h: AT[:, h, 2, :], lambda h: tm[:, h, 3, :], [AT_r, tm_r])
        X, X_r = X_p.get()
        p.op("act", lambda e, X=X, pv=pv: e.copy(out=X[:, :, 64:128], in_=pv[:]), reads=[pv_r], writes=[X_r])
        p.op("pool", lambda e, X=X, tm=tm: e.tensor_copy(out=X[:, :, 0:64], in_=tm[:, :, 0, :]), reads=[tm_r], writes=[X_r])
        pX, pX_r = psX.get()
        mm4(pX, pX_r, lambda h, M=M: M[:, h, :], lambda h, X=X: X[:, h, :], [M_r, X_r])
        WU, WU_r = WU_p.get()
        p.op("dve", lambda e, WU=WU, pX=pX: e.tensor_copy(out=WU[:], in_=pX[:]), reads=[pX_r], writes=[WU_r])
        pG, pG_r = ps4.get()
        mm4(pG, pG_r, lambda h, WU=WU: WU[:, h, 0:64], lambda h: tm[:, h, 1, :], [WU_r, tm_r])
        GT, GT_r = sb4.get()
        o_tt(p, "dve", GT[:], pG[:], ident[:], ALU.add, [pG_r, mk_r], [GT_r])
        pH, pH_r = ps4.get()
        mm4g(pH, pH_r, [(lambda h: tm[:, h, 1, :], lambda h, WU=WU: WU[:, h, 64:128], [WU_r, tm_r]),
                        (lambda h: tm[:, h, 2, :], lambda h: tm[:, h, 3, :], [tm_r])])
        HG, HG_r = sb4.get()
        o_tt(p, "dve", HG[:], pH[:], gl_t[:, n, :].unsqueeze(2).to_broadcast([64, 4, 64]), ALU.mult, [pH_r, gl_r], [HG_r])
        pQ, pQ_r = ps4.get()
        mm4(pQ, pQ_r, lambda h, WU=WU: WU[:, h, 0:64], lambda h: AT[:, h, 1, :], [WU_r, AT_r])
        QT, QT_r = sb4.get()
        o_tt(p, "dve", QT[:], pQ[:], ar[:, :, 1, :], ALU.add, [pQ_r, ar_r], [QT_r])
        pY, pY_r = ps4.get()
        mm4g(pY, pY_r, [(lambda h: AT[:, h, 1, :], lambda h, WU=WU: WU[:, h, 64:128], [AT_r, WU_r]),
                        (lambda h: AT[:, h, 3, :], lambda h: tm[:, h, 3, :], [AT_r, tm_r]),
                        (lambda h, QT=QT: QT[:, h, :], lambda h, S=S: S[:, h, :], [QT_r, S_r])])
        pS, pS_r = ps4.get()
        mm4(pS, pS_r, lambda h, GT=GT: GT[:, h, :], lambda h, S=S: S[:, h, :], [GT_r, S_r])
        S2, S2_r = S_p.get()
        sg, sg_r = sb4.get()
        o_tt(p, "dve", sg[:], pS[:], gl_t[:, n, :].unsqueeze(2).to_broadcast([64, 4, 64]), ALU.mult, [pS_r, gl_r], [sg_r])
        o_tt(p, "dve", S2[:], sg[:], HG[:], ALU.add, [sg_r, HG_r], [S2_r])
        S, S_r = S2, S2_r
        y, y_r = sb4.get()
        p.op("act", lambda e, y=y, pY=pY: e.copy(out=y[:], in_=pY[:]), reads=[pY_r], writes=[y_r])
        s1, s1_r = st4.get()
        s2, s2_r = st4.get()
        ysq, ysq_r = sb4.get()
        p.op("dve", lambda e, s1=s1, y=y: e.reduce_sum(out=s1[:], in_=y[:], axis=AX.X), reads=[y_r], writes=[s1_r])
        o_tt(p, "pool", ysq[:], y[:], y[:], ALU.mult, [y_r], [ysq_r])
        p.op("dve", lambda e, s2=s2, ysq=ysq: e.reduce_sum(out=s2[:], in_=ysq[:], axis=AX.X), reads=[ysq_r], writes=[s2_r])
        o_ts(p, "dve", s1[:], s1[:], 1.0 / 64, None, ALU.mult, None, [s1_r], [s1_r])
        m2, m2_r = st4.get()
        o_tt(p, "dve", m2[:], s1[:], s1[:], ALU.mult, [s1_r], [m2_r])
        o_stt(p, s2[:], s2[:], 1.0 / 64, m2[:], ALU.mult, ALU.subtract, [s2_r, m2_r], [s2_r])
        o_ts(p, "dve", s2[:], s2[:], 64e-5, None, ALU.add, None, [s2_r], [s2_r])
        o_act(p, s2[:], s2[:], AF.Sqrt, [s2_r], [s2_r])
        p.op("dve", lambda e, s2=s2: e.reciprocal(out=s2[:], in_=s2[:]), reads=[s2_r], writes=[s2_r])
        yn, yn_r = sb4.get()
        o_tt(p, "dve", yn[:], y[:], s1[:].unsqueeze(2).to_broadcast([64, 4, 64]), ALU.subtract, [y_r, s1_r], [yn_r])
        o_tt(p, "dve", yn[:], yn[:], s2[:].unsqueeze(2).to_broadcast([64, 4, 64]), ALU.mult, [yn_r, s2_r], [yn_r])
        o_tt(p, "pool", yn[:], yn[:], ln_t[:, 0, :, :], ALU.mult, [yn_r, ln_r], [yn_r])
        o_tt(p, "pool", yn[:], yn[:], ln_t[:, 1, :, :], ALU.add, [yn_r, ln_r], [yn_r])
        bv, bv_r = sb4.get()
        o_tt(p, "dve", bv[:], tm[:, :, 3, :], bs_t[:, n, :].unsqueeze(2).to_broadcast([64, 4, 64]), ALU.mult, [tm_r, bs_r], [bv_r])
        o_tt(p, "pool", yn[:], yn[:], bv[:], ALU.add, [yn_r, bv_r], [yn_r])
        oi = outp.i
        o, o_r = outp.get()
        o_tt(p, "pool", o[:], yn[:], tm[:, :, 4, :], ALU.mult, [yn_r, tm_r], [o_r])
        p.dma("sp", och[oi], yo[:, n], o[:], reads=[o_r], is_output=True)
    p.finish()
    return nc


def rw1_host_layout(z, mu, w2, a2, g2, w0, a0, k_k, k_a, r_k, q):
    tl = z.shape[0]
    cs = slice(256 * q, 256 * (q + 1))
    zrkv = np.zeros((64, 3, 4, tl + 1), np.float32)
    mu_rkv = np.zeros((64, 3, 4, RTB), np.float32)
    for j in range(3):
        blk = z[:, j * 1024:(j + 1) * 1024][:, cs]
        zrkv[:, j, :, 1:] = blk.reshape(tl, 4, 64).transpose(2, 1, 0)
        mu_rkv[:, j, :, :] = mu[j * 1024:(j + 1) * 1024][cs].reshape(4, 64).T[:, :, None]

    def rows(c0, c1):
        o = np.zeros((c1 - c0, tl + 1), np.float32)
        o[:, 1:] = z[:, c0:c1].T
        return o
    mu_l = np.zeros((128, 4), np.float32)
    mu_l[0:64, 0] = mu[3072:3136]
    mu_l[0:64, 1] = mu[3136:3200]
    mu_l[0:128, 2] = mu[3200:3328]
    mu_l[0:32, 3] = mu[3328:3360]
    cvec = np.zeros((64, 5, 4, RTB), np.float32)
    for i, v in enumerate((w0, a0, k_k, k_a)):
        cvec[:, i, :, :] = v[cs].reshape(4, 64).T[:, :, None]
    cvec[:, 4, :, :] = r_k[4 * q:4 * q + 4].T[:, :, None]
    return dict(zrkv=zrkv, zw=rows(3072, 3136), za=rows(3136, 3200), zg0=rows(3200, 3328), zg1=np.concatenate([rows(3328, 3360), np.zeros((96, tl + 1), np.float32)], 0),
                mu_rkv=mu_rkv, mu_l=mu_l, w2=np.ascontiguousarray(w2[:, cs]), a2=np.ascontiguousarray(a2[:, cs]),
                g2a=np.ascontiguousarray(g2[0:128, cs]), g2b=np.concatenate([g2[128:160, cs], np.zeros((96, 256), np.float32)], 0), cvec=cvec,
                ones_in=np.ones((64, 256), np.float32))


def rw2_host_layout(o, lnx_w, lnx_b, q):
    tl = o["at"].shape[2]
    nch = tl // 64
    c5 = lambda a: np.asarray(a).reshape(64, 4, nch, 64)
    at, rt, bt, kt, xv, gg = (c5(o[k]) for k in ("at", "rt", "bt", "kt", "xv", "gg"))
    AR = np.stack([at, rt], 3).transpose(0, 2, 1, 3, 4)
    BK = np.stack([bt, kt], 3).transpose(0, 2, 1, 3, 4)
    TM = np.stack([at, bt, kt, xv, gg], 0).transpose(4, 3, 2, 0, 1)
    BS = c5(o["bs"])[0].transpose(2, 1, 0)
    GL = c5(o["ec"])[:, :, :, 63].transpose(0, 2, 1)
    cs = slice(256 * q, 256 * (q + 1))
    LN = np.broadcast_to(np.stack([lnx_w[cs].reshape(4, 64), lnx_b[cs].reshape(4, 64)], 0)[None], (64, 2, 4, 64))
    f = lambda a: np.ascontiguousarray(a, dtype=np.float32)
    return dict(AR=f(AR), BK=f(BK), TM=f(TM), BS=f(BS), GL=f(GL), LN=f(LN))


_PROGS = {}


def _prog(key, fn):
    if key not in _PROGS:
        _PROGS[key] = fn()
    return _PROGS[key]


def _run(nc, maps):
    return run_bass_kernel_spmd(nc, maps, core_ids=list(range(NCORES))).results


def _tok_shards(a):
    return [np.ascontiguousarray(a[c // 4, (c % 4) * NT:(c % 4 + 1) * NT, :].T) for c in range(NCORES)]


def _from_tok_shards(res, key, ncols):
    out = np.empty((B, T, ncols), np.float32)
    for c in range(NCORES):
        out[c // 4, (c % 4) * NT:(c % 4 + 1) * NT, :] = np.asarray(res[c][key]).T
    return out


def kernel(**inputs):
    I = {k: np.asarray(v) for k, v in inputs.items()}
    f32 = lambda a: np.ascontiguousarray(a, dtype=np.float32)
    xs = _tok_shards(f32(I["x"]))
    for layer in range(4):
        i = layer // 2
        if layer % 2 == 0:
            nc = _prog("k1e", lambda: build_k1(EVEN_IN))
            r = _run(nc, [{"xT": xs[c], "g": f32(I["norm_mix_pre"][layer]), "w": f32(I["ev_w_in"][i])} for c in range(NCORES)])
            pfull = _from_tok_shards(r, "oT", EVEN_IN)
            maps = []
            for c in range(NCORES):
                b, q = c // 4, c % 4
                d = s5_host_layout(f32(I["s5_lam_re"][i]), f32(I["s5_lam_im"][i]), f32(I["s5_log_dt"][i]),
                                   f32(I["s5_b_re"][i]), f32(I["s5_b_im"][i]), f32(I["s5_c_re"][i]),
                                   f32(I["s5_c_im"][i]), f32(I["s5_d"][i]), q)
                d["uT"] = np.ascontiguousarray(pfull[b, :, 256 * q:256 * (q + 1)].T)
                maps.append(d)
            rs = _run(_prog("s5", build_s5), maps)
            ycat = np.empty((B, T, D), np.float32)
            for c in range(NCORES):
                b, q = c // 4, c % 4
                ycat[b, :, 256 * q:256 * (q + 1)] = np.asarray(rs[c]["yT"]).T
            maps = [rw1_host_layout(pfull[c // 4, :, 1024:], f32(I["ev_shift_mu"][i]), f32(I["rw_w2"][i]),
                                    f32(I["rw_a2"][i]), f32(I["rw_g2"][i]), f32(I["rw_w0"][i]), f32(I["rw_a0"][i]),
                                    f32(I["rw_k_k"][i]), f32(I["rw_k_a"][i]), f32(I["rw_r_k"][i]), c % 4)
                    for c in range(NCORES)]
            r1 = _run(_prog("rw1", build_rw1), maps)
            maps = [rw2_host_layout(r1[c], f32(I["rw_lnx_w"][i]), f32(I["rw_lnx_b"][i]), c % 4) for c in range(NCORES)]
            r2 = _run(_prog("rw2", build_rw2), maps)
            for c in range(NCORES):
                b, q = c // 4, c % 4
                yo = np.asarray(r2[c]["yo"])
                ycat[b, :, 1024 + 256 * q:1024 + 256 * (q + 1)] = yo.transpose(1, 0, 2, 3).reshape(T, 256)
            ys = _tok_shards(ycat)
            nc = _prog("k2g", lambda: build_k2(D, glu=True))
            r = _run(nc, [{"inT": ys[c], "xT": xs[c], "g": f32(I["norm_mix_post"][layer]), "w": f32(I["ev_w_out"][i]),
                           "wglu": f32(I["s5_w_glu"][i])} for c in range(NCORES)])
        else:
            nc = _prog("k1o", lambda: build_k1(2 * D))
            r = _run(nc, [{"xT": xs[c], "g": f32(I["norm_mix_pre"][layer]), "w": f32(I["od_w_in"][i])} for c in range(NCORES)])
            pfull = _from_tok_shards(r, "oT", 2 * D)
            maps = []
            for c in range(NCORES):
                b, q = c // 4, c % 4
                cs = slice(512 * q, 512 * (q + 1))
                v = np.stack([I["od_conv_w"][i][0][cs], I["od_conv_w"][i][1][cs], I["od_conv_w"][i][2][cs],
                              I["od_conv_w"][i][3][cs], I["od_conv_b"][i][cs], I["lru_b_r"][i][cs],
                              I["lru_b_i"][i][cs], I["lru_lam"][i][cs]], -1)
                maps.append({"gateT": np.ascontiguousarray(pfull[b, :, cs].T),
                             "xbT": np.ascontiguousarray(pfull[b, :, D + 512 * q:D + 512 * (q + 1)].T),
                             "vecs": f32(v.reshape(4, 128, 8).transpose(1, 0, 2)),
                             "wr": f32(I["lru_w_r"][i][2 * q:2 * q + 2]), "wi": f32(I["lru_w_i"][i][2 * q:2 * q + 2])})
            rl = _run(_prog("lru", build_lru), maps)
            ycat = np.empty((B, T, D), np.float32)
            for c in range(NCORES):
                b, q = c // 4, c % 4
                ycat[b, :, 512 * q:512 * (q + 1)] = np.asarray(rl[c]["yT"]).T
            ys = _tok_shards(ycat)
            nc = _prog("k2p", lambda: build_k2(D))
            r = _run(nc, [{"inT": ys[c], "xT": xs[c], "g": f32(I["norm_mix_post"][layer]), "w": f32(I["od_w_out"][i])}
                          for c in range(NCORES)])
        xs = [f32(r[c]["oT"]) for c in range(NCORES)]
        nc = _prog("k1f", lambda: build_k1(DFF, swiglu=True))
        r = _run(nc, [{"xT": xs[c], "g": f32(I["norm_ffn_pre"][layer]), "w": f32(I["ffn_w_gate"][layer]),
                       "w2": f32(I["ffn_w_up"][layer])} for c in range(NCORES)])
        aT = [np.asarray(r[c]["oT"]) for c in range(NCORES)]
        nc = _prog("k2f", lambda: build_k2(DFF, in_bf16=True))
        r = _run(nc, [{"inT": aT[c], "xT": xs[c], "g": f32(I["norm_ffn_post"][layer]), "w": f32(I["ffn_w_down"][layer])}
                      for c in range(NCORES)])
        xs = [f32(r[c]["oT"]) for c in range(NCORES)]
    out = np.empty((B, T, D), np.float32)
    for c in range(NCORES):
        out[c // 4, (c % 4) * NT:(c % 4 + 1) * NT, :] = xs[c].T
    return out
```

```python
import contextlib
import numpy as np
import concourse.bass as bass
import concourse.mybir as mybir
from concourse.bass_utils import run_bass_kernel_spmd

F32 = mybir.dt.float32
BF16 = mybir.dt.bfloat16
AF = mybir.ActivationFunctionType
ALU = mybir.AluOpType
AX = mybir.AxisListType

NCORES = 8
D = 2048
B = 2
T = 4096
NT = 1024
DFF = 5632
EVEN_IN = 4384
NORM_EPS = 1e-6

ENGS = ("pe", "act", "dve", "pool", "sp")


class Res:
    __slots__ = ("lw", "rd")

    def __init__(self):
        self.lw = None
        self.rd = []


class Chan:
    __slots__ = ("sem", "cnt")

    def __init__(self, sem):
        self.sem = sem
        self.cnt = 0


class _Rec:
    def __init__(self):
        self.call = None

    def __getattr__(self, name):
        def f(*a, **k):
            assert self.call is None, "op closure must emit exactly one instruction"
            self.call = (name, a, k)
            return self
        return f


class Prog:
    def __init__(self, nc):
        self.nc = nc
        self.stack = contextlib.ExitStack()
        self.q = {e: [] for e in ENGS}
        self.esem = {e: self.stack.enter_context(nc.semaphore("es_" + e)) for e in ENGS}
        self.ecnt = {e: 0 for e in ENGS}
        self.seen = {e: {} for e in ENGS}
        self.out_events = []
        self._n = 0

    def name(self, base):
        self._n += 1
        return "%s_%d" % (base, self._n)

    def sbuf(self, shape, dt, name="sb"):
        return self.stack.enter_context(self.nc.sbuf_tensor(self.name(name), list(shape), dt))

    def psum(self, shape, dt=F32, name="ps"):
        return self.stack.enter_context(self.nc.psum_tensor(self.name(name), list(shape), dt))

    def chan(self, name="ch"):
        return Chan(self.stack.enter_context(self.nc.semaphore(self.name(name))))

    def _deps(self, eng, reads, writes, pe_group_start=True):
        evs = []
        for r in reads:
            if r.lw is not None and not (eng == "pe" and r.lw[2] == "pe"):
                evs.append(r.lw)
        for w in writes:
            if w.lw is not None and not (eng == "pe" and w.lw[2] == "pe" and not pe_group_start):
                evs.append(w.lw)
            for ev in w.rd:
                if ev[2] != eng:
                    evs.append(ev)
        waits = {}
        seen = self.seen[eng]
        for (sem, val, _e) in evs:
            k = id(sem)
            if seen.get(k, 0) >= val:
                continue
            if k not in waits or waits[k][1] < val:
                waits[k] = (sem, val)
        for k, (sem, val) in waits.items():
            seen[k] = val
        return list(waits.values())

    def _commit(self, ev, reads, writes):
        for w in writes:
            w.lw = ev
            w.rd = []
        for r in reads:
            r.rd.append(ev)

    def op(self, eng, fn, reads=(), writes=(), same_bank_cont=False):
        rec = _Rec()
        fn(rec)
        gs = True
        if eng == "pe" and rec.call[0] == "matmul":
            gs = bool(rec.call[2].get("start", True)) and not same_bank_cont
        waits = self._deps(eng, reads, writes, pe_group_start=gs)
        self.ecnt[eng] += 1
        ev = (self.esem[eng], self.ecnt[eng], eng)
        self.q[eng].append((waits, rec.call, (self.esem[eng], 1)))
        self._commit(ev, reads, writes)
        return ev

    def dma(self, queue, chan, out, in_, reads=(), writes=(), is_output=False, **kw):
        waits = self._deps(queue, reads, writes)
        chan.cnt += 1
        ev = (chan.sem, 16 * chan.cnt, "dma")
        self.q[queue].append((waits, ("dma_start", (), dict(out=out, in_=in_, **kw)), (chan.sem, 16)))
        self._commit(ev, reads, writes)
        if is_output:
            self.out_events.append(ev)
        return ev

    def finish(self):
        fin = {}
        for (sem, val, _e) in self.out_events:
            k = id(sem)
            if k not in fin or fin[k][1] < val:
                fin[k] = (sem, val)
        nc = self.nc
        q = self.q
        fin_waits = list(fin.values())

        def replay(engobj, items, extra_waits=()):
            for (waits, fn, inc) in items:
                for (sem, val) in waits:
                    engobj.wait_ge(sem, val)
                name, a, k = fn
                ins = getattr(engobj, name)(*a, **k)
                if inc is not None:
                    ins.then_inc(inc[0], inc[1])
            for (sem, val) in extra_waits:
                engobj.wait_ge(sem, val)

        with nc.Block() as block:
            @block.sync
            def _(e):
                replay(e, q["sp"], fin_waits)

            @block.tensor
            def _(e):
                replay(e, q["pe"])

            @block.scalar
            def _(e):
                replay(e, q["act"])

            @block.vector
            def _(e):
                replay(e, q["dve"])

            @block.gpsimd
            def _(e):
                replay(e, q["pool"])
        self.stack.close()


class Dense:
    def __init__(self, p, nt=NT):
        self.p = p
        nc = p.nc
        self.nt = nt
        self.ntb = nt // 512
        self.ones = p.sbuf([128, 128], BF16, "ones")
        self.ones_r = Res()
        p.op("pool", lambda e: e.memset(self.ones[:], 1.0), writes=[self.ones_r])
        self.banks = [p.psum([128, 512], F32, "bank") for _ in range(8)]
        self.bank_r = [Res() for _ in range(8)]
        self._bk = 0
        self.wslots = [p.sbuf([128, 16, 256], BF16, "wslot") for _ in range(3)]
        self.wslot_r = [Res() for _ in range(3)]
        self.wchan = [p.chan("wch") for _ in range(3)]
        self._ws = 0
        self.ostg = [p.sbuf([128, 512], F32, "ostg") for _ in range(4)]
        self.ostg_r = [Res() for _ in range(4)]
        self.ochan = [p.chan("och") for _ in range(4)]
        self._os = 0

    def bank(self):
        i = self._bk
        self._bk = (self._bk + 1) % 8
        return self.banks[i], self.bank_r[i]

    def wslot(self):
        i = self._ws
        self._ws = (self._ws + 1) % 3
        return self.wslots[i], self.wslot_r[i], self.wchan[i]

    def ostage(self):
        i = self._os
        self._os = (self._os + 1) % 4
        return self.ostg[i], self.ostg_r[i], self.ochan[i]


def load_fm(p, chan, dram_ap, nchunks, nt, dt=F32, name="act", queue="sp"):
    t = p.sbuf([128, nchunks, nt], dt, name)
    rs = [Res() for _ in range(nchunks)]
    src = dram_ap.rearrange("(c q) n -> q c n", q=128)
    step = max(1, 4)
    for c0 in range(0, nchunks, step):
        c1 = min(nchunks, c0 + step)
        p.dma(queue, chan, t[:, c0:c1, :], src[:, c0:c1, :], writes=rs[c0:c1])
    return t, rs


def load_vec(p, chan, dram_ap, nchunks, name="vec"):
    t = p.sbuf([128, nchunks], F32, name)
    r = Res()
    src = dram_ap.rearrange("(c q) -> q c", q=128)
    p.dma("sp", chan, t[:], src, writes=[r], allow_slow_non_contiguous=True)
    return t, r


def rmsnorm_fm(p, dn, x_sb, x_r, nchunks, g_sb, g_r, out_dt=BF16, name="h", inplace=False):
    nt = dn.nt
    dmodel = nchunks * 128
    if inplace:
        h, h_r = x_sb, x_r
    else:
        h = p.sbuf([128, nchunks, nt], out_dt, name)
        h_r = [Res() for _ in range(nchunks)]
    sq = [p.sbuf([128, nt], BF16, "sq") for _ in range(2)]
    sq_r = [Res(), Res()]
    rstd = p.sbuf([128, nt], F32, "rstd")
    rstd_r = [Res() for _ in range(dn.ntb)]
    banks = [dn.bank() for _ in range(dn.ntb)]
    for c in range(nchunks):
        s, sr = sq[c % 2], sq_r[c % 2]
        p.op("act", lambda e, s=s, c=c: e.activation(out=s[:], in_=x_sb[:, c, :], func=AF.Square),
             reads=[x_r[c]], writes=[sr])
        for tb in range(dn.ntb):
            bk, bkr = banks[tb]
            p.op("pe", lambda e, bk=bk, s=s, tb=tb, c=c: e.matmul(
                bk[:], dn.ones[:], s[:, tb * 512:(tb + 1) * 512],
                start=(c == 0), stop=(c == nchunks - 1)),
                reads=[sr, dn.ones_r], writes=[bkr])
    for tb in range(dn.ntb):
        bk, bkr = banks[tb]
        sl = slice(tb * 512, (tb + 1) * 512)
        p.op("dve", lambda e, bk=bk, sl=sl: e.tensor_scalar(
            rstd[:, sl], bk[:], 1.0 / dmodel, NORM_EPS, ALU.mult, ALU.add),
            reads=[bkr], writes=[rstd_r[tb]])
        p.op("act", lambda e, sl=sl: e.activation(out=rstd[:, sl], in_=rstd[:, sl], func=AF.Sqrt),
             reads=[rstd_r[tb]], writes=[rstd_r[tb]])
        p.op("dve", lambda e, sl=sl: e.reciprocal(out=rstd[:, sl], in_=rstd[:, sl]),
             reads=[rstd_r[tb]], writes=[rstd_r[tb]])
    for c in range(nchunks):
        for tb in range(dn.ntb):
            sl = slice(tb * 512, (tb + 1) * 512)
            p.op("dve", lambda e, c=c, sl=sl: e.scalar_tensor_tensor(
                out=h[:, c, sl], in0=x_sb[:, c, sl], scalar=g_sb[:, c:c + 1], in1=rstd[:, sl],
                op0=ALU.mult, op1=ALU.mult),
                reads=[x_r[c], g_r, rstd_r[tb]], writes=[h_r[c]])
    return h, h_r


def linear_fm(p, dn, h, h_r, kchunks, w_dram, fdim, epilogue, w2_dram=None):
    nt = dn.nt
    FB = 256
    kstep = 16
    for f0 in range(0, fdim, FB):
        fb = min(FB, fdim - f0)
        slots = []
        for wd in ([w_dram] if w2_dram is None else [w_dram, w2_dram]):
            parts = []
            for k0 in range(0, kchunks, kstep):
                k1 = min(kchunks, k0 + kstep)
                ws, wr, wc = dn.wslot()
                src = wd[k0 * 128:k1 * 128, f0:f0 + fb].rearrange("(c q) f -> q c f", q=128)
                p.dma("pool", wc, ws[:, 0:k1 - k0, 0:fb], src, writes=[wr])
                parts.append((ws, wr, k0, k1))
            slots.append(parts)
        for fs in range(0, fb, 128):
            fsz = min(128, fb - fs)
            for tb in range(dn.ntb):
                outs = []
                for parts in slots:
                    bk, bkr = dn.bank()
                    nk = kchunks
                    for (ws, wr, k0, k1) in parts:
                        for k in range(k0, k1):
                            p.op("pe", lambda e, bk=bk, ws=ws, k=k, k0=k0, fs=fs, fsz=fsz, tb=tb: e.matmul(
                                bk[0:fsz, :], ws[:, k - k0, fs:fs + fsz], h[:, k, tb * 512:(tb + 1) * 512],
                                start=(k == 0), stop=(k == nk - 1)),
                                reads=[wr, h_r[k]], writes=[bkr])
                    outs += [bk, bkr]
                epilogue(f0 + fs, fsz, tb, *outs)


def build_k1(fdim, swiglu=False, nt=NT):
    nc = bass.Bass("TRN2", target_bir_lowering=False)
    xT = nc.dram_tensor("xT", [D, nt], F32, kind="ExternalInput").ap()
    g = nc.dram_tensor("g", [D], F32, kind="ExternalInput").ap()
    w = nc.dram_tensor("w", [D, fdim], F32, kind="ExternalInput").ap()
    w2 = nc.dram_tensor("w2", [D, fdim], F32, kind="ExternalInput").ap() if swiglu else None
    odt = BF16 if swiglu else F32
    oT = nc.dram_tensor("oT", [fdim, nt], odt, kind="ExternalOutput").ap()
    p = Prog(nc)
    dn = Dense(p, nt)
    ch_in = p.chan("chin")
    x_sb, x_r = load_fm(p, ch_in, xT, D // 128, nt, name="x")
    g_sb, g_r = load_vec(p, p.chan("chg"), g, D // 128)
    h, h_r = rmsnorm_fm(p, dn, x_sb, x_r, D // 128, g_sb, g_r)
    ostg_bf = [p.sbuf([128, 512], BF16, "ostgb") for _ in range(4)]
    sil = [p.sbuf([128, 512], F32, "sil") for _ in range(2)]
    sil_r = [Res(), Res()]
    cnt = [0]

    def epi_plain(f0, fsz, tb, bk, bkr):
        st, sr, sc = dn.ostage()
        eng = "act" if cnt[0] % 2 == 0 else "dve"
        cnt[0] += 1
        if eng == "act":
            p.op("act", lambda e: e.copy(out=st[0:fsz, :], in_=bk[0:fsz, :]), reads=[bkr], writes=[sr])
        else:
            p.op("dve", lambda e: e.tensor_copy(out=st[0:fsz, :], in_=bk[0:fsz, :]), reads=[bkr], writes=[sr])
        p.dma("sp", sc, oT[f0:f0 + fsz, tb * 512:(tb + 1) * 512], st[0:fsz, :], reads=[sr], is_output=True)

    def epi_swiglu(f0, fsz, tb, bg, bgr, bu, bur):
        i = dn._os
        st, sr, sc = dn.ostage()
        stb = ostg_bf[i]
        s, s_r = sil[cnt[0] % 2], sil_r[cnt[0] % 2]
        cnt[0] += 1
        p.op("act", lambda e: e.activation(out=s[0:fsz, :], in_=bg[0:fsz, :], func=AF.Silu),
             reads=[bgr], writes=[s_r])
        p.op("dve", lambda e: e.tensor_tensor(out=stb[0:fsz, :], in0=s[0:fsz, :], in1=bu[0:fsz, :], op=ALU.mult),
             reads=[s_r, bur], writes=[sr])
        p.dma("sp", sc, oT[f0:f0 + fsz, tb * 512:(tb + 1) * 512], stb[0:fsz, :], reads=[sr], is_output=True)

    linear_fm(p, dn, h, h_r, D // 128, w, fdim, epi_swiglu if swiglu else epi_plain, w2_dram=w2)
    p.finish()
    return nc


def build_k2(kdim, in_bf16=False, glu=False, nt=NT):
    nc = bass.Bass("TRN2", target_bir_lowering=False)
    in_dt = BF16 if in_bf16 else F32
    inT = nc.dram_tensor("inT", [kdim, nt], in_dt, kind="ExternalInput").ap()
    xT = nc.dram_tensor("xT", [D, nt], F32, kind="ExternalInput").ap()
    g = nc.dram_tensor("g", [D], F32, kind="ExternalInput").ap()
    w = nc.dram_tensor("w", [kdim, D], F32, kind="ExternalInput").ap()
    wg = nc.dram_tensor("wglu", [1024, 1024], F32, kind="ExternalInput").ap() if glu else None
    oT = nc.dram_tensor("oT", [D, nt], F32, kind="ExternalOutput").ap()
    p = Prog(nc)
    dn = Dense(p, nt)
    kch = kdim // 128
    h = p.sbuf([128, kch, nt], BF16, "hin")
    h_r = [Res() for _ in range(kch)]
    ch_in = p.chan("chin")
    src = inT.rearrange("(c q) n -> q c n", q=128)
    g_sb, g_r = load_vec(p, p.chan("chg"), g, D // 128)
    if glu:
        ys = p.sbuf([128, 8, nt], BF16, "ys5")
        ys_r = [Res() for _ in range(8)]
        for c0 in range(0, 8, 4):
            p.dma("pool", ch_in, ys[:, c0:c0 + 4, :], src[:, c0:c0 + 4, :], writes=ys_r[c0:c0 + 4])
        for c0 in range(8, 16, 4):
            p.dma("pool", ch_in, h[:, c0:c0 + 4, :], src[:, c0:c0 + 4, :], writes=h_r[c0:c0 + 4])
        sg = [p.sbuf([128, 512], F32, "sg") for _ in range(2)]
        sg_r = [Res(), Res()]
        cg = [0]

        def epi_glu(f0, fsz, tb, bk, bkr):
            s_, sr_ = sg[cg[0] % 2], sg_r[cg[0] % 2]
            cg[0] += 1
            c = f0 // 128
            sl = slice(tb * 512, (tb + 1) * 512)
            p.op("act", lambda e: e.activation(out=s_[:], in_=bk[:], func=AF.Sigmoid), reads=[bkr], writes=[sr_])
            p.op("dve", lambda e: e.tensor_tensor(out=h[:, c, sl], in0=s_[:], in1=ys[:, c, sl], op=ALU.mult),
                 reads=[sr_, ys_r[c]], writes=[h_r[c]])
        linear_fm(p, dn, ys, ys_r, 8, wg, 1024, epi_glu)
    else:
        q = "sp" if in_bf16 else "pool"
        for c0 in range(0, kch, 4):
            c1 = min(kch, c0 + 4)
            p.dma(q, ch_in, h[:, c0:c1, :], src[:, c0:c1, :], writes=h_r[c0:c1])
    o_sb = p.sbuf([128, D // 128, nt], F32, "osb")
    o_r = [Res() for _ in range(D // 128)]
    ce = [0]

    def epi_o(f0, fsz, tb, bk, bkr):
        c = f0 // 128
        sl = slice(tb * 512, (tb + 1) * 512)
        ce[0] += 1
        if ce[0] % 2 == 0:
            p.op("act", lambda e: e.copy(out=o_sb[:, c, sl], in_=bk[:]), reads=[bkr], writes=[o_r[c]])
        else:
            p.op("dve", lambda e: e.tensor_copy(out=o_sb[:, c, sl], in_=bk[:]), reads=[bkr], writes=[o_r[c]])
    linear_fm(p, dn, h, h_r, kch, w, D, epi_o)
    hn, hn_r = rmsnorm_fm(p, dn, o_sb, o_r, D // 128, g_sb, g_r, out_dt=F32, inplace=True)
    xsrc = xT.rearrange("(c q) n -> q c n", q=128)
    odst = oT.rearrange("(c q) n -> q c n", q=128)
    xst = [p.sbuf([128, nt], F32, "xst") for _ in range(3)]
    xst_r = [Res() for _ in range(3)]
    xch = [p.chan("xch") for _ in range(3)]
    for c in range(D // 128):
        i = c % 3
        p.dma("sp", xch[i], xst[i][:], xsrc[:, c, :], writes=[xst_r[i]])
        p.op("dve" if c % 2 == 0 else "pool", lambda e, i=i, c=c: e.tensor_tensor(
            out=xst[i][:], in0=xst[i][:], in1=hn[:, c, :], op=ALU.add),
            reads=[hn_r[c], xst_r[i]], writes=[xst_r[i]])
        p.dma("sp", xch[i], odst[:, c, :], xst[i][:], reads=[xst_r[i]], is_output=True)
    p.finish()
    return nc


class TPool:
    def __init__(self, p, shape, dt, n, name="tp", psum=False):
        mk = p.psum if psum else p.sbuf
        self.t = [mk(shape, dt, name) for _ in range(n)]
        self.r = [Res() for _ in range(n)]
        self.i = 0

    def get(self):
        i = self.i
        self.i = (i + 1) % len(self.t)
        return self.t[i], self.r[i]


def o_tt(p, eng, out, in0, in1, op, reads, writes):
    return p.op(eng, lambda e: e.tensor_tensor(out=out, in0=in0, in1=in1, op=op), reads=reads, writes=writes)


def o_ts(p, eng, out, in0, s1, s2, op0, op1, reads, writes):
    if s2 is None:
        return p.op(eng, lambda e: e.tensor_scalar(out, in0, s1, None, op0), reads=reads, writes=writes)
    return p.op(eng, lambda e: e.tensor_scalar(out, in0, s1, s2, op0, op1), reads=reads, writes=writes)


def o_stt(p, out, in0, scalar, in1, op0, op1, reads, writes):
    return p.op("dve", lambda e: e.scalar_tensor_tensor(out=out, in0=in0, scalar=scalar, in1=in1, op0=op0, op1=op1),
                reads=reads, writes=writes)


def o_act(p, out, in_, func, reads, writes, scale=1.0, bias=None):
    if bias is None:
        return p.op("act", lambda e: e.activation(out=out, in_=in_, func=func, scale=scale), reads=reads, writes=writes)
    return p.op("act", lambda e: e.activation(out=out, in_=in_, func=func, scale=scale, bias=bias),
                reads=reads, writes=writes)


def gelu_tanh(p, tp, out, x, xr, outr, eng="pool"):
    t1, r1 = tp.get()
    o_tt(p, eng, t1[:], x, x, ALU.mult, [xr], [r1])
    o_ts(p, eng, t1[:], t1[:], 0.044715, 1.0, ALU.mult, ALU.add, [r1], [r1])
    o_tt(p, eng, t1[:], t1[:], x, ALU.mult, [r1, xr], [r1])
    o_act(p, t1[:], t1[:], AF.Sigmoid, [r1], [r1], scale=1.5957691216057308)
    o_tt(p, eng, out, t1[:], x, ALU.mult, [r1, xr], [outr])


def build_lru(tlen=T):
    nc = bass.Bass("TRN2", target_bir_lowering=False)
    CH = 512
    gateT = nc.dram_tensor("gateT", [CH, tlen], F32, kind="ExternalInput").ap()
    xbT = nc.dram_tensor("xbT", [CH, tlen], F32, kind="ExternalInput").ap()
    vecs = nc.dram_tensor("vecs", [128, 4, 8], F32, kind="ExternalInput").ap()
    wr = nc.dram_tensor("wr", [2, 256, 256], F32, kind="ExternalInput").ap()
    wi = nc.dram_tensor("wi", [2, 256, 256], F32, kind="ExternalInput").ap()
    yT = nc.dram_tensor("yT", [CH, tlen], F32, kind="ExternalOutput").ap()
    p = Prog(nc)
    ntb = tlen // 512
    vec = p.sbuf([128, 4, 8], F32, "vec")
    vec_r = Res()
    p.dma("sp", p.chan(), vec[:], vecs, writes=[vec_r])
    c8 = p.sbuf([128, 4], F32, "c8")
    c8_r = Res()
    o_act(p, c8[:], vec[:, :, 7], AF.Exp, [vec_r], [c8_r], scale=-1.0)
    o_ts(p, "dve", c8[:], c8[:], 1.0, None, ALU.add, None, [c8_r], [c8_r])
    o_act(p, c8[:], c8[:], AF.Ln, [c8_r], [c8_r])
    o_ts(p, "dve", c8[:], c8[:], -8.0, None, ALU.mult, None, [c8_r], [c8_r])
    wsb = {}
    wch = p.chan()
    for nm, wd in (("r", wr), ("i", wi)):
        t = p.sbuf([128, 2, 2, 256], BF16, "w" + nm)
        r = Res()
        wch = p.chan()
        for n in range(2):
            p.dma("pool", wch, t[:, n, :, :], wd[n].rearrange("(c q) d -> q c d", q=128), writes=[r])
        wsb[nm] = (t, r)
    xin = TPool(p, [128, 515], F32, 4, "xin")
    gin = TPool(p, [128, 512], F32, 3, "gin")
    xch = [p.chan() for _ in range(4)]
    gch = [p.chan() for _ in range(3)]
    och = [p.chan() for _ in range(3)]
    xcp = TPool(p, [128, 512], F32, 4, "xc")
    xcb = TPool(p, [128, 512], BF16, 4, "xcb")
    tmp = TPool(p, [128, 512], F32, 8, "tmp")
    hp = TPool(p, [128, 512], F32, 8, "h")
    op_ = TPool(p, [128, 512], F32, 3, "o")
    psp = TPool(p, [128, 512], F32, 8, "ps", psum=True)
    hprev = {}
    xsrc = xbT.rearrange("(c q) n -> q c n", q=128)
    gsrc = gateT.rearrange("(c q) n -> q c n", q=128)
    ydst = yT.rearrange("(c q) n -> q c n", q=128)
    for tb in range(ntb):
        t0 = tb * 512
        for n in range(2):
            xcs = []
            for cc in range(2):
                c = 2 * n + cc
                i = xin.i
                xt, xr = xin.get()
                if tb == 0:
                    p.op("pool", lambda e, xt=xt: e.memset(xt[:, 0:3], 0.0), writes=[xr])
                    p.dma("sp", xch[i], xt[:, 3:515], xsrc[:, c, 0:512], writes=[xr])
                else:
                    p.dma("sp", xch[i], xt[:, 0:515], xsrc[:, c, t0 - 3:t0 + 512], writes=[xr])
                xc, xcr = xcp.get()
                o_ts(p, "dve", xc[:], xt[:, 3:515], vec[:, c, 3:4], vec[:, c, 4:5], ALU.mult, ALU.add, [xr, vec_r], [xcr])
                for j in range(3):
                    o_stt(p, xc[:], xt[:, j:j + 512], vec[:, c, j:j + 1], xc[:], ALU.mult, ALU.add, [xr, vec_r, xcr], [xcr])
                xb_, xbr = xcb.get()
                p.op("act", lambda e, xb_=xb_, xc=xc: e.copy(out=xb_[:], in_=xc[:]), reads=[xcr], writes=[xbr])
                xcs.append((xc, xcr, xb_, xbr))
            for dc in range(2):
                c = 2 * n + dc
                xc, xcr = xcs[dc][0], xcs[dc][1]
                pr, prr = psp.get()
                pi, pir = psp.get()
                for (pt, ptr, nm) in ((pr, prr, "r"), (pi, pir, "i")):
                    wt, wtr = wsb[nm]
                    for cc in range(2):
                        p.op("pe", lambda e, pt=pt, wt=wt, cc=cc, dc=dc, n=n, xb_=xcs[cc][2]: e.matmul(
                            pt[:], wt[:, n, cc, dc * 128:(dc + 1) * 128], xb_[:], start=(cc == 0), stop=(cc == 1)),
                            reads=[wtr, xcs[cc][3]], writes=[ptr])
                sr, srr = tmp.get()
                si, sir = tmp.get()
                o_act(p, sr[:], pr[:], AF.Sigmoid, [prr, vec_r], [srr], bias=vec[:, c, 5:6])
                o_act(p, si[:], pi[:], AF.Sigmoid, [pir, vec_r], [sir], bias=vec[:, c, 6:7])
                a_, ar = tmp.get()
                o_act(p, a_[:], sr[:], AF.Exp, [srr, c8_r], [ar], scale=c8[:, c:c + 1])
                m_, mr = tmp.get()
                o_tt(p, "pool", m_[:], a_[:], a_[:], ALU.mult, [ar], [mr])
                o_ts(p, "pool", m_[:], m_[:], -1.0, 1.0, ALU.mult, ALU.add, [mr], [mr])
                o_act(p, m_[:], m_[:], AF.Sqrt, [mr], [mr])
                o_tt(p, "pool", m_[:], m_[:], si[:], ALU.mult, [mr, sir], [mr])
                o_tt(p, "dve", m_[:], m_[:], xc[:], ALU.mult, [mr, xcr], [mr])
                h_, hr = hp.get()
                if c in hprev:
                    ph, phr = hprev[c]
                    p.op("dve", lambda e, h_=h_, a_=a_, m_=m_, ph=ph: e.tensor_tensor_scan(
                        out=h_[:], data0=a_[:], data1=m_[:], initial=ph[:, 511:512], op0=ALU.mult, op1=ALU.add),
                        reads=[ar, mr, phr], writes=[hr])
                else:
                    p.op("dve", lambda e, h_=h_, a_=a_, m_=m_: e.tensor_tensor_scan(
                        out=h_[:], data0=a_[:], data1=m_[:], initial=0.0, op0=ALU.mult, op1=ALU.add),
                        reads=[ar, mr], writes=[hr])
                hprev[c] = (h_, hr)
                gi_ = gin.i
                gt, gr_ = gin.get()
                p.dma("sp", gch[gi_], gt[:], gsrc[:, c, t0:t0 + 512], writes=[gr_])
                oi = op_.i
                ot, otr = op_.get()
                gelu_tanh(p, tmp, ot[:], gt[:], gr_, otr, eng="pool")
                o_tt(p, "dve", ot[:], ot[:], h_[:], ALU.mult, [otr, hr], [otr])
                p.dma("sp", och[oi], ydst[:, c, t0:t0 + 512], ot[:], reads=[otr], is_output=True)
    p.finish()
    return nc


TWO_PI = 6.283185307179586


def frac_wrap(p, x, ti, tf, reads, r):
    p.op("dve", lambda e: e.tensor_copy(out=ti, in_=x), reads=reads, writes=[r])
    p.op("dve", lambda e: e.tensor_copy(out=tf, in_=ti), reads=reads, writes=[r])
    o_tt(p, "dve", x, x, tf, ALU.subtract, reads, [r])
    o_stt(p, tf, x, 0.5, x, ALU.is_gt, ALU.subtract, reads, [r])
    o_stt(p, x, tf, 0.5, tf, ALU.is_gt, ALU.subtract, reads, [r])


def sincos_turns(p, S, C, ph, tabs, halfpi_ap, reads, rS, rC, rt):
    o_act(p, S, ph, AF.Sin, reads, [rS], scale=TWO_PI)
    o_act(p, tabs, ph, AF.Abs, reads, [rt])
    o_act(p, C, tabs, AF.Sin, [rt], [rC], scale=-TWO_PI, bias=halfpi_ap)


def s5_pre(p, lr, li, ldt, shape, pi_ap, tagr):
    r = Res()
    mk = lambda nm: p.sbuf(shape, F32, "s5" + nm)
    dt, mag, f0, f0c, sn, cs = mk("dt"), mk("mag"), mk("f0"), mk("f0c"), mk("sn"), mk("cs")
    are, aim, den, f1 = mk("are"), mk("aim"), mk("den"), mk("f1")
    t1, t2, gre, gim = f0c, sn, cs, dt
    R = [tagr, r]
    o_act(p, dt[:], ldt, AF.Exp, R, [r])
    o_tt(p, "dve", mag[:], lr, dt[:], ALU.mult, R, [r])
    o_act(p, mag[:], mag[:], AF.Exp, R, [r])
    o_tt(p, "dve", f0[:], li, dt[:], ALU.mult, R, [r])
    o_ts(p, "dve", f0[:], f0[:], 1.0 / TWO_PI, None, ALU.mult, None, R, [r])
    ti = p.sbuf(shape, mybir.dt.int32, "s5ti")
    frac_wrap(p, f0[:], ti[:], f0c[:], R, r)
    sincos_turns(p, sn[:], cs[:], f0[:], f0c[:], pi_ap, R, r, r, r)
    o_tt(p, "dve", are[:], mag[:], cs[:], ALU.mult, R, [r])
    o_tt(p, "dve", aim[:], mag[:], sn[:], ALU.mult, R, [r])
    o_tt(p, "dve", den[:], lr, lr, ALU.mult, R, [r])
    o_tt(p, "dve", t1[:], li, li, ALU.mult, R, [r])
    o_tt(p, "dve", den[:], den[:], t1[:], ALU.add, R, [r])
    p.op("dve", lambda e: e.reciprocal(out=den[:], in_=den[:]), reads=R, writes=[r])
    o_ts(p, "dve", t1[:], are[:], -1.0, None, ALU.add, None, R, [r])
    o_tt(p, "dve", gre[:], t1[:], lr, ALU.mult, R, [r])
    o_tt(p, "dve", t2[:], aim[:], li, ALU.mult, R, [r])
    o_tt(p, "dve", gre[:], gre[:], t2[:], ALU.add, R, [r])
    o_tt(p, "dve", gre[:], gre[:], den[:], ALU.mult, R, [r])
    o_tt(p, "dve", gim[:], aim[:], lr, ALU.mult, R, [r])
    o_tt(p, "dve", t2[:], t1[:], li, ALU.mult, R, [r])
    o_tt(p, "dve", gim[:], gim[:], t2[:], ALU.subtract, R, [r])
    o_tt(p, "dve", gim[:], gim[:], den[:], ALU.mult, R, [r])
    o_ts(p, "dve", f1[:], f0[:], 64.0, None, ALU.mult, None, R, [r])
    frac_wrap(p, f1[:], ti[:], den[:], R, r)
    return dict(mag=mag, f0=f0, f1=f1, gre=gre, gim=gim), r


def emit_s5(p, uT, lamP, lamR, bmat, cmat, dsk, yT, tlen):
    nc = p.nc
    ntb = tlen // 512
    pi_t = p.sbuf([128, 1], F32, "pi")
    pi_r = Res()
    p.op("pool", lambda e: e.memset(pi_t[:], 1.5707963267948966), writes=[pi_r])
    ch0 = p.chan()
    lp = p.sbuf([128, 3, 8], F32, "lamP")
    lp_r = Res()
    p.dma("sp", p.chan(), lp[:], lamP, writes=[lp_r])
    lrw = p.sbuf([32, 3, 1024], F32, "lamR")
    lrw_r = Res()
    p.dma("sp", p.chan(), lrw[:], lamR, writes=[lrw_r])
    p.op("dve", lambda e: e.tensor_copy(out=lp[:, 0, 0:1], in_=lp[:, 0, 0:1]), reads=[pi_r, lp_r], writes=[lp_r])
    P, Pr = s5_pre(p, lp[:, 0, :], lp[:, 1, :], lp[:, 2, :], [128, 8], pi_t[:, 0:1], lp_r)
    p.op("dve", lambda e: e.tensor_copy(out=lrw[:, 0, 0:1], in_=lrw[:, 0, 0:1]), reads=[pi_r, lrw_r], writes=[lrw_r])
    Rw, Rr = s5_pre(p, lrw[:, 0, :], lrw[:, 1, :], lrw[:, 2, :], [32, 1024], pi_t[0:32, 0:1], lrw_r)
    bsb = p.sbuf([32, 2, 1024], F32, "bsb")
    b_r = Res()
    p.dma("sp", p.chan(), bsb[:], bmat, writes=[b_r])
    BT = p.sbuf([32, 2, 1024], F32, "BT")
    BT_r = Res()
    tb1 = p.sbuf([32, 1024], F32, "tb1")
    tb1_r = Res()
    o_tt(p, "dve", tb1[:], bsb[:, 1, :], Rw["gim"][:], ALU.mult, [b_r, Rr], [tb1_r])
    o_tt(p, "dve", BT[:, 0, :], bsb[:, 0, :], Rw["gre"][:], ALU.mult, [b_r, Rr], [BT_r])
    o_tt(p, "dve", BT[:, 0, :], BT[:, 0, :], tb1[:], ALU.subtract, [BT_r, tb1_r], [BT_r])
    o_tt(p, "dve", tb1[:], bsb[:, 1, :], Rw["gre"][:], ALU.mult, [b_r, Rr, BT_r], [tb1_r])
    o_tt(p, "dve", BT[:, 1, :], bsb[:, 0, :], Rw["gim"][:], ALU.mult, [b_r, Rr], [BT_r])
    o_tt(p, "dve", BT[:, 1, :], BT[:, 1, :], tb1[:], ALU.add, [BT_r, tb1_r], [BT_r])
    csb = p.sbuf([128, 2, 8, 32], F32, "csb")
    c_r = Res()
    p.dma("sp", p.chan(), csb[:], cmat, writes=[c_r])
    o_ts(p, "dve", csb[:, 1, :, :], csb[:, 1, :, :], -1.0, None, ALU.mult, None, [c_r], [c_r])
    dsb = p.sbuf([32, 8], F32, "dsb")
    d_r = Res()
    p.dma("sp", p.chan(), dsb[:], dsk, writes=[d_r])
    ia = p.sbuf([128, 512], mybir.dt.int32, "ia")
    ib = p.sbuf([128, 512], mybir.dt.int32, "ib")
    A0 = p.sbuf([128, 512], F32, "A0")
    B0 = p.sbuf([128, 512], F32, "B0")
    io_r = Res()
    p.op("pool", lambda e: e.iota(ia[:], [[1, 8], [0, 64]], base=0, channel_multiplier=0), writes=[io_r])
    p.op("pool", lambda e: e.iota(ib[:], [[0, 8], [1, 64]], base=0, channel_multiplier=0), writes=[io_r])
    p.op("dve", lambda e: e.tensor_copy(out=A0[:], in_=ia[:]), reads=[io_r], writes=[io_r])
    p.op("dve", lambda e: e.tensor_copy(out=B0[:], in_=ib[:]), reads=[io_r], writes=[io_r])
    ones = p.sbuf([128, 512], F32, "ones5")
    p.op("pool", lambda e: e.memset(ones[:], 1.0), writes=[io_r])

    usrc = uT.rearrange("(m q) n -> q m n", q=32)
    ydst = yT.rearrange("(m q) n -> q m n", q=32)
    upool = TPool(p, [32, tlen], F32, 2, "u")
    uch = [p.chan(), p.chan()]
    rho_p = TPool(p, [128, 512], F32, 2, "rho")
    psb = TPool(p, [128, 512], F32, 4, "psbu", psum=True)
    psy = TPool(p, [32, 512], F32, 2, "psy", psum=True)
    tp = TPool(p, [128, 512], F32, 16, "s5t")
    tip = TPool(p, [128, 512], mybir.dt.int32, 2, "s5ti")
    zp = TPool(p, [128, 512], F32, 6, "s5z")
    t32 = TPool(p, [32, 512], F32, 6, "s5o")
    och = [p.chan() for _ in range(6)]
    for m in range(8):
        ui = upool.i
        u, u_r = upool.get()
        p.dma("sp", uch[ui], u[:], usrc[:, m, :], writes=[u_r])
        rho, rho_r = rho_p.get()
        o_ts(p, "dve", rho[:], ones[:], P["mag"][:, m:m + 1], None, ALU.mult, None, [io_r, Pr], [rho_r])
        zprev = None
        for tb in range(ntb):
            sl = slice(tb * 512, (tb + 1) * 512)
            pre, prer = psb.get()
            pim, pimr = psb.get()
            p.op("pe", lambda e, pre=pre, u=u, sl=sl, m=m: e.matmul(pre[:], BT[:, 0, m * 128:(m + 1) * 128], u[:, sl], start=True, stop=True),
                 reads=[BT_r, u_r], writes=[prer])
            p.op("pe", lambda e, pim=pim, u=u, sl=sl, m=m: e.matmul(pim[:], BT[:, 1, m * 128:(m + 1) * 128], u[:, sl], start=True, stop=True),
                 reads=[BT_r, u_r], writes=[pimr])
            bre, brer = tp.get()
            bim, bimr = tp.get()
            p.op("act", lambda e, bre=bre, pre=pre: e.copy(out=bre[:], in_=pre[:]), reads=[prer], writes=[brer])
            p.op("act", lambda e, bim=bim, pim=pim: e.copy(out=bim[:], in_=pim[:]), reads=[pimr], writes=[bimr])
            ph, phr = tp.get()
            phc, phcr = tp.get()
            o_ts(p, "dve", ph[:], A0[:], float(8 * tb), P["f1"][:, m:m + 1], ALU.add, ALU.mult, [io_r, Pr], [phr])
            o_stt(p, ph[:], B0[:], P["f0"][:, m:m + 1], ph[:], ALU.mult, ALU.add, [io_r, Pr, phr], [phr])
            ti, tir = tip.get()
            frac_wrap(p, ph[:], ti[:], phc[:], [phr, phcr, tir], phr)
            S, Sr = tp.get()
            C, Cr = tp.get()
            sincos_turns(p, S[:], C[:], ph[:], phc[:], pi_t[:, 0:1], [phr, pi_r], Sr, Cr, phcr)
            wre, wrer = tp.get()
            wim, wimr = tp.get()
            t1, t1r = tp.get()
            t2, t2r = tp.get()
            o_tt(p, "pool", wre[:], C[:], bre[:], ALU.mult, [Cr, brer], [wrer])
            o_tt(p, "pool", t1[:], S[:], bim[:], ALU.mult, [Sr, bimr], [t1r])
            o_tt(p, "pool", wre[:], wre[:], t1[:], ALU.add, [wrer, t1r], [wrer])
            o_tt(p, "dve", wim[:], C[:], bim[:], ALU.mult, [Cr, bimr], [wimr])
            o_tt(p, "dve", t2[:], S[:], bre[:], ALU.mult, [Sr, brer], [t2r])
            o_tt(p, "dve", wim[:], wim[:], t2[:], ALU.subtract, [wimr, t2r], [wimr])
            zre, zrer = zp.get()
            zim, zimr = zp.get()
            for (z, zr, w_, wr_, k) in ((zre, zrer, wre, wrer, 0), (zim, zimr, wim, wimr, 1)):
                if zprev is None:
                    p.op("dve", lambda e, z=z, w_=w_, rho=rho: e.tensor_tensor_scan(
                        out=z[:], data0=rho[:], data1=w_[:], initial=0.0, op0=ALU.mult, op1=ALU.add),
                        reads=[rho_r, wr_], writes=[zr])
                else:
                    pz, pzr = zprev[k]
                    p.op("dve", lambda e, z=z, w_=w_, rho=rho, pz=pz: e.tensor_tensor_scan(
                        out=z[:], data0=rho[:], data1=w_[:], initial=pz[:, 511:512], op0=ALU.mult, op1=ALU.add),
                        reads=[rho_r, wr_, pzr], writes=[zr])
            zprev = ((zre, zrer), (zim, zimr))
            sre, srer = tp.get()
            sim, simr = tp.get()
            t3, t3r = tp.get()
            t4, t4r = tp.get()
            o_tt(p, "pool", sre[:], C[:], zre[:], ALU.mult, [Cr, zrer], [srer])
            o_tt(p, "pool", t3[:], S[:], zim[:], ALU.mult, [Sr, zimr], [t3r])
            o_tt(p, "pool", sre[:], sre[:], t3[:], ALU.subtract, [srer, t3r], [srer])
            o_tt(p, "dve", sim[:], S[:], zre[:], ALU.mult, [Sr, zrer], [simr])
            o_tt(p, "pool", t4[:], C[:], zim[:], ALU.mult, [Cr, zimr], [t4r])
            o_tt(p, "dve", sim[:], sim[:], t4[:], ALU.add, [simr, t4r], [simr])
            py, pyr = psy.get()
            p.op("pe", lambda e, py=py, sre=sre, m=m: e.matmul(py[:], csb[:, 0, m, :], sre[:], start=True, stop=False),
                 reads=[c_r, srer], writes=[pyr])
            p.op("pe", lambda e, py=py, sim=sim, m=m: e.matmul(py[:], csb[:, 1, m, :], sim[:], start=False, stop=True),
                 reads=[c_r, simr], writes=[pyr])
            y2, y2r = t32.get()
            o_stt(p, y2[:], u[:, sl], dsb[:, m:m + 1], py[:], ALU.mult, ALU.add, [u_r, d_r, pyr], [y2r])
            oi = t32.i
            yo, yor = t32.get()
            gelu_tanh(p, t32, yo[:], y2[:], y2r, yor, eng="pool")
            p.dma("sp", och[oi], ydst[:, m, sl], yo[:], reads=[yor], is_output=True)


def build_s5(tlen=T):
    nc = bass.Bass("TRN2", target_bir_lowering=False)
    uT = nc.dram_tensor("uT", [256, tlen], F32, kind="ExternalInput").ap()
    lamP = nc.dram_tensor("lamP", [128, 3, 8], F32, kind="ExternalInput").ap()
    lamR = nc.dram_tensor("lamR", [32, 3, 1024], F32, kind="ExternalInput").ap()
    bmat = nc.dram_tensor("bmat", [32, 2, 1024], F32, kind="ExternalInput").ap()
    cmat = nc.dram_tensor("cmat", [128, 2, 8, 32], F32, kind="ExternalInput").ap()
    dsk = nc.dram_tensor("dsk", [32, 8], F32, kind="ExternalInput").ap()
    yT = nc.dram_tensor("yT", [256, tlen], F32, kind="ExternalOutput").ap()
    p = Prog(nc)
    emit_s5(p, uT, lamP, lamR, bmat, cmat, dsk, yT, tlen)
    p.finish()
    return nc


def s5_host_layout(lam_re, lam_im, log_dt, b_re, b_im, c_re, c_im, d_skip, q):
    g0 = 16 * q
    lr = lam_re[g0:g0 + 16]
    li = lam_im[g0:g0 + 16]
    ld = np.broadcast_to(log_dt[g0:g0 + 16, None], (16, 64))
    st = np.stack([lr, li, ld], 0).reshape(3, 8, 2, 64)
    lamP = np.ascontiguousarray(st.transpose(2, 3, 0, 1).reshape(128, 3, 8))
    lamR = np.ascontiguousarray(np.broadcast_to(st.reshape(1, 3, 1024), (32, 3, 1024)))
    bmat = np.zeros((32, 2, 8, 2, 64), np.float32)
    cmat = np.zeros((2, 64, 2, 8, 2, 16), np.float32)
    for k, (bb, cc) in enumerate(((b_re, c_re), (b_im, c_im))):
        bq = bb[g0:g0 + 16].reshape(8, 2, 64, 16)
        cq = cc[g0:g0 + 16].reshape(8, 2, 16, 64)
        for gl in range(2):
            bmat[gl * 16:(gl + 1) * 16, k, :, gl, :] = bq[:, gl].transpose(2, 0, 1)
            cmat[gl, :, k, :, gl, :] = cq[:, gl].transpose(2, 0, 1)
    bmat = bmat.reshape(32, 2, 1024)
    cmat = cmat.reshape(128, 2, 8, 32)
    dsk = np.ascontiguousarray(d_skip[256 * q:256 * (q + 1)].reshape(8, 32).T)
    return dict(lamP=lamP, lamR=lamR, bmat=bmat, cmat=cmat, dsk=dsk)


RTB = 128
_DBG = [99]


def build_rw1(tlen=T):
    nc = bass.Bass("TRN2", target_bir_lowering=False)
    di = lambda n, sh: nc.dram_tensor(n, sh, F32, kind="ExternalInput").ap()
    do = lambda n, sh: nc.dram_tensor(n, sh, F32, kind="ExternalOutput").ap()
    zrkv = di("zrkv", [64, 3, 4, tlen + 1])
    zw = di("zw", [64, tlen + 1])
    za = di("za", [64, tlen + 1])
    zg0 = di("zg0", [128, tlen + 1])
    zg1 = di("zg1", [128, tlen + 1])
    mu_rkv = di("mu_rkv", [64, 3, 4, RTB])
    mu_l = di("mu_l", [128, 4])
    w2 = di("w2", [64, 256])
    a2 = di("a2", [64, 256])
    g2a = di("g2a", [128, 256])
    g2b = di("g2b", [128, 256])
    cvec = di("cvec", [64, 5, 4, RTB])
    outs = {n: do(n, [64, 4, tlen]) for n in ("at", "rt", "bt", "kt", "xv", "gg", "bs", "ec")}
    p = Prog(nc)
    ch0 = p.chan()
    N = 4 * RTB

    def const(ap, shape, nm):
        t = p.sbuf(shape, F32, nm)
        r = Res()
        p.dma("sp", p.chan(), t[:], ap, writes=[r])
        return t, r
    mu_t, mu_r = const(mu_rkv, [64, 3, 4, RTB], "mu")
    mul_t, mul_r = const(mu_l, [128, 4], "mul")
    w2_t, w2_r = const(w2, [64, 256], "w2")
    a2_t, a2_r = const(a2, [64, 256], "a2")
    g2a_t, g2a_r = const(g2a, [128, 256], "g2a")
    g2b_t, g2b_r = const(g2b, [128, 256], "g2b")
    cv_t, cv_r = const(cvec, [64, 5, 4, RTB], "cv")
    cst = p.sbuf([128, 4], F32, "cst")
    cst_r = Res()
    p.op("pool", lambda e: e.memset(cst[:, 0:1], 1.0), writes=[cst_r])
    p.op("pool", lambda e: e.memset(cst[:, 1:2], -0.5), writes=[cst_r])
    ones_in = di("ones_in", [64, 256])
    ones_f32, ones_r = const(ones_in, [64, 256], "ones_f32")
    ones = ones_f32[:, 0:64]
    cmask = p.sbuf([64, RTB], F32, "cmask")
    p.op("pool", lambda e: e.memset(cmask[:], 1.0), writes=[cst_r])
    for c in range(RTB // 64):
        p.op("pool", lambda e, c=c: e.memset(cmask[:, c * 64:c * 64 + 1], 0.0), writes=[cst_r])

    zin = TPool(p, [64, 3, 4, RTB + 1], F32, 2, "zin")
    zch = [p.chan(), p.chan()]
    lin = TPool(p, [128, 4, RTB + 1], F32, 2, "lin")
    lch = [p.chan(), p.chan()]
    tp = TPool(p, [64, 4, RTB], F32, 30, "r1t")
    tl = TPool(p, [128, RTB], F32, 8, "r1l")
    ps = TPool(p, [64, 4, RTB], F32, 6, "r1ps", psum=True)
    psg = TPool(p, [64, 4, RTB], F32, 2, "r1pg", psum=True)
    op_ = TPool(p, [64, 4, RTB], F32, 16, "r1o")
    och = [p.chan() for _ in range(16)]

    def emit_out(name, t, r, t0):
        i = op_.i
        o, o_r = op_.get()
        p.op("pool", lambda e: e.tensor_copy(out=o[:], in_=t), reads=[r], writes=[o_r])
        p.dma("sp", och[i], outs[name][:, :, t0:t0 + RTB], o[:], reads=[o_r], is_output=True)

    for blk in range(tlen // RTB):
        t0 = blk * RTB
        zi = zin.i
        z, z_r = zin.get()
        p.dma("sp", zch[zi], z[:], zrkv[:, :, :, t0:t0 + RTB + 1], writes=[z_r])
        li_ = lin.i
        l, l_r = lin.get()
        p.dma("sp", lch[li_], l[0:64, 0, :], zw[:, t0:t0 + RTB + 1], writes=[l_r])
        p.dma("sp", lch[li_], l[0:64, 1, :], za[:, t0:t0 + RTB + 1], writes=[l_r])
        p.dma("sp", lch[li_], l[:, 2, :], zg0[:, t0:t0 + RTB + 1], writes=[l_r])
        p.dma("sp", lch[li_], l[:, 3, :], zg1[:, t0:t0 + RTB + 1], writes=[l_r])
        xs = []
        for j in range(3):
            x, x_r = tp.get()
            o_tt(p, "pool", x[:], z[:, j, :, 0:RTB], z[:, j, :, 1:RTB + 1], ALU.subtract, [z_r], [x_r])
            o_tt(p, "pool", x[:], x[:], mu_t[:, j, :, :], ALU.mult, [x_r, mu_r], [x_r])
            o_tt(p, "pool", x[:], x[:], z[:, j, :, 1:RTB + 1], ALU.add, [x_r, z_r], [x_r])
            xs.append((x, x_r))
        (xr, xr_r), (xk, xk_r), (xv, xv_r) = xs
        if _DBG[0] == 1:
            emit_out('xv', xv[:], xv_r, t0)
            continue
        ls = []
        for j, np_ in ((0, 64), (1, 64), (2, 128), (3, 128)):
            x, x_r = tl.get()
            o_tt(p, "dve", x[0:np_, :], l[0:np_, j, 0:RTB], l[0:np_, j, 1:RTB + 1], ALU.subtract, [l_r], [x_r])
            o_stt(p, x[0:np_, :], x[0:np_, :], mul_t[0:np_, j:j + 1], l[0:np_, j, 1:RTB + 1], ALU.mult, ALU.add,
                  [x_r, l_r, mul_r], [x_r])
            ls.append((x, x_r, np_))
        xw, xw_r, _ = ls[0]
        xa, xa_r, _ = ls[1]
        o_act(p, xw[0:64, :], xw[0:64, :], AF.Tanh, [xw_r], [xw_r])
        for (x, x_r, np_) in ls[2:]:
            o_act(p, x[0:np_, :], x[0:np_, :], AF.Sigmoid, [x_r], [x_r])
        if _DBG[0] == 2:
            emit_out('xv', xv[:], xv_r, t0)
            continue
        pw, pw_r = ps.get()
        pa, pa_r = ps.get()
        pg, pg_r = psg.get()
        for h in range(4):
            hs = slice(h * 64, (h + 1) * 64)
            p.op("pe", lambda e, h=h, hs=hs: e.matmul(pw[:, h, :], w2_t[:, hs], xw[0:64, :], start=True, stop=True),
                 reads=[w2_r, xw_r], writes=[pw_r], same_bank_cont=(h > 0))
            p.op("pe", lambda e, h=h, hs=hs: e.matmul(pa[:, h, :], a2_t[:, hs], xa[0:64, :], start=True, stop=True),
                 reads=[a2_r, xa_r], writes=[pa_r], same_bank_cont=(h > 0))
            p.op("pe", lambda e, h=h, hs=hs: e.matmul(pg[:, h, :], g2a_t[:, hs], ls[2][0][:, :], start=True, stop=False),
                 reads=[g2a_r, ls[2][1]], writes=[pg_r], same_bank_cont=(h > 0))
            p.op("pe", lambda e, h=h, hs=hs: e.matmul(pg[:, h, :], g2b_t[:, hs], ls[3][0][:, :], start=False, stop=True),
                 reads=[g2b_r, ls[3][1]], writes=[pg_r])
        if _DBG[0] == 3:
            emit_out('xv', xv[:], xv_r, t0)
            continue
        e2, e2_r = tp.get()
        o_tt(p, "dve", e2[:], pw[:], cv_t[:, 0, :, :], ALU.add, [pw_r, cv_r], [e2_r])
        o_act(p, e2[:], e2[:], AF.Exp, [e2_r], [e2_r], scale=-1.0)
        o_act(p, e2[:], e2[:], AF.Ln, [e2_r, cst_r], [e2_r], bias=cst[0:64, 0:1])
        o_act(p, e2[:], e2[:], AF.Exp, [e2_r, cst_r], [e2_r], scale=-1.0, bias=cst[0:64, 1:2])
        a_, a_r = tp.get()
        o_tt(p, "dve", a_[:], pa[:], cv_t[:, 1, :, :], ALU.add, [pa_r, cv_r], [a_r])
        o_act(p, a_[:], a_[:], AF.Sigmoid, [a_r], [a_r])
        gg, gg_r = tp.get()
        p.op("act", lambda e, gg=gg, pg=pg: e.copy(out=gg[:], in_=pg[:]), reads=[pg_r], writes=[gg_r])
        if _DBG[0] == 4:
            emit_out('xv', xv[:], xv_r, t0)
            continue
        kk, kk_r = tp.get()
        sq, sq_r = tp.get()
        o_tt(p, "dve", kk[:], xk[:], cv_t[:, 2, :, :], ALU.mult, [xk_r, cv_r], [kk_r])
        o_tt(p, "dve", sq[:], kk[:], kk[:], ALU.mult, [kk_r], [sq_r])
        if _DBG[0] == 411:
            emit_out('xv', sq[:], sq_r, t0)
            continue
        pss, pss_r = ps.get()
        for h in range(4):
            p.op("pe", lambda e, pss=pss, sq=sq, h=h: e.matmul(pss[:, h, :], ones, sq[:, h, :], start=True, stop=True),
                 reads=[ones_r, sq_r], writes=[pss_r], same_bank_cont=(h > 0))
        o_ts(p, "dve", sq[:], pss[:], 1.0, 1e-24, ALU.mult, ALU.max, [pss_r, sq_r], [sq_r])
        if _DBG[0] == 412:
            emit_out('xv', sq[:], sq_r, t0)
            continue
        o_act(p, sq[:], sq[:], AF.Sqrt, [sq_r], [sq_r])
        p.op("dve", lambda e, sq=sq: e.reciprocal(out=sq[:], in_=sq[:]), reads=[sq_r], writes=[sq_r])
        o_tt(p, "dve", kk[:], kk[:], sq[:], ALU.mult, [kk_r, sq_r], [kk_r])
        if _DBG[0] == 41:
            emit_out('xv', kk[:], kk_r, t0)
            continue
        km, km_r = tp.get()
        o_ts(p, "pool", km[:], a_[:], -1.0, None, ALU.add, None, [a_r], [km_r])
        o_tt(p, "pool", km[:], km[:], cv_t[:, 3, :, :], ALU.mult, [km_r, cv_r], [km_r])
        o_stt(p, km[:], km[:], 1.0, xk[:], ALU.add, ALU.mult, [km_r, xk_r], [km_r])
        vb, vb_r = tp.get()
        o_tt(p, "pool", vb[:], kk[:], a_[:], ALU.mult, [kk_r, a_r], [vb_r])
        if _DBG[0] == 42:
            emit_out('xv', vb[:], vb_r, t0)
            continue
        rk, rk_r = tp.get()
        o_tt(p, "pool", rk[:], xr[:], km[:], ALU.mult, [xr_r, km_r], [rk_r])
        o_tt(p, "pool", rk[:], rk[:], cv_t[:, 4, :, :], ALU.mult, [rk_r, cv_r], [rk_r])
        pb, pb_r = ps.get()
        for h in range(4):
            p.op("pe", lambda e, pb=pb, rk=rk, h=h: e.matmul(pb[:, h, :], ones, rk[:, h, :], start=True, stop=True),
                 reads=[ones_r, rk_r], writes=[pb_r], same_bank_cont=(h > 0))
        bsv, bsv_r = tp.get()
        p.op("act", lambda e, bsv=bsv, pb=pb: e.copy(out=bsv[:], in_=pb[:]), reads=[pb_r], writes=[bsv_r])
        if _DBG[0] == 5:
            emit_out('xv', xv[:], xv_r, t0)
            continue
        cu, cu_r = tp.get()
        for h in range(4):
            p.op("dve", lambda e, h=h, cu=cu, e2=e2: e.tensor_tensor_scan(
                out=cu[:, h, :], data0=cmask[:], data1=e2[:, h, :], initial=0.0, op0=ALU.mult, op1=ALU.add),
                reads=[cst_r, e2_r], writes=[cu_r])
        cex, cex_r = tp.get()
        o_tt(p, "pool", cex[:], cu[:], e2[:], ALU.subtract, [cu_r, e2_r], [cex_r])
        ec, ec_r = tp.get()
        en, en_r = tp.get()
        o_act(p, ec[:], cu[:], AF.Exp, [cu_r], [ec_r], scale=-1.0)
        o_act(p, en[:], cu[:], AF.Exp, [cu_r], [en_r])
        o_act(p, cex[:], cex[:], AF.Exp, [cex_r], [cex_r], scale=-1.0)
        at, at_r = tp.get()
        o_stt(p, at[:], kk[:], -1.0, cex[:], ALU.mult, ALU.mult, [kk_r, cex_r], [at_r])
        rt, rt_r = tp.get()
        o_tt(p, "dve", rt[:], xr[:], ec[:], ALU.mult, [xr_r, ec_r], [rt_r])
        bt, bt_r = tp.get()
        o_tt(p, "pool", bt[:], vb[:], en[:], ALU.mult, [vb_r, en_r], [bt_r])
        kt, kt_r = tp.get()
        o_tt(p, "dve", kt[:], km[:], en[:], ALU.mult, [km_r, en_r], [kt_r])
        for nm, (t_, r_) in (("at", (at, at_r)), ("rt", (rt, rt_r)), ("bt", (bt, bt_r)), ("kt", (kt, kt_r)),
                             ("xv", (xv, xv_r)), ("gg", (gg, gg_r)), ("bs", (bsv, bsv_r)), ("ec", (ec, ec_r))):
            emit_out(nm, t_[:], r_, t0)
    p.finish()
    return nc


def build_rw2(tlen=T):
    nc = bass.Bass("TRN2", target_bir_lowering=False)
    nch = tlen // 64
    di = lambda n, sh: nc.dram_tensor(n, sh, F32, kind="ExternalInput").ap()
    AR = di("AR", [64, nch, 4, 2, 64])
    BK = di("BK", [64, nch, 4, 2, 64])
    TM = di("TM", [64, nch, 4, 5, 64])
    BS = di("BS", [64, nch, 4])
    GL = di("GL", [64, nch, 4])
    LN = di("LN", [64, 2, 4, 64])
    yo = nc.dram_tensor("yo", [64, nch, 4, 64], F32, kind="ExternalOutput").ap()
    p = Prog(nc)
    ch0 = p.chan()
    ln_t = p.sbuf([64, 2, 4, 64], F32, "ln")
    ln_r = Res()
    p.dma("sp", p.chan(), ln_t[:], LN, writes=[ln_r])
    gl_t = p.sbuf([64, nch, 4], F32, "gl")
    gl_r = Res()
    p.dma("sp", p.chan(), gl_t[:], GL, writes=[gl_r])
    bs_t = p.sbuf([64, nch, 4], F32, "bs")
    bs_r = Res()
    p.dma("sp", p.chan(), bs_t[:], BS, writes=[bs_r])
    ii = p.sbuf([64, 64], mybir.dt.int32, "ii")
    dif = p.sbuf([64, 64], F32, "dif")
    mk_r = Res()
    p.op("pool", lambda e: e.iota(ii[:], [[1, 64]], base=0, channel_multiplier=-1), writes=[mk_r])
    p.op("dve", lambda e: e.tensor_copy(out=dif[:], in_=ii[:]), reads=[mk_r], writes=[mk_r])
    msk = p.sbuf([64, 4, 4, 64], F32, "msk")
    mlow = p.sbuf([64, 4, 64], F32, "mlow")
    ident = p.sbuf([64, 4, 64], F32, "ident")
    for h in range(4):
        for q in range(4):
            op = ALU.is_gt if q % 2 == 0 else ALU.is_ge
            o_ts(p, "dve", msk[:, h, q, :], dif[:], 0.0, None, op, None, [mk_r], [mk_r])
        o_ts(p, "dve", mlow[:, h, :], dif[:], 0.0, None, ALU.is_lt, None, [mk_r], [mk_r])
        o_ts(p, "dve", ident[:, h, :], dif[:], 0.0, None, ALU.is_equal, None, [mk_r], [mk_r])

    arp = TPool(p, [64, 4, 2, 64], F32, 2, "ar")
    bkp = TPool(p, [64, 4, 2, 64], F32, 2, "bk")
    tmp_ = TPool(p, [64, 4, 5, 64], F32, 2, "tm")
    inch = [p.chan(), p.chan()]
    inch2 = [p.chan(), p.chan()]
    inch3 = [p.chan(), p.chan()]
    psA = TPool(p, [64, 4, 4, 64], F32, 1, "psA", psum=True)
    ps4 = TPool(p, [64, 4, 64], F32, 5, "ps4", psum=True)
    psX = TPool(p, [64, 4, 128], F32, 1, "psX", psum=True)
    sb4 = TPool(p, [64, 4, 64], F32, 24, "sb4")
    AT_p = TPool(p, [64, 4, 4, 64], F32, 2, "AT")
    X_p = TPool(p, [64, 4, 128], F32, 2, "X")
    WU_p = TPool(p, [64, 4, 128], F32, 2, "WU")
    S_p = TPool(p, [64, 4, 64], F32, 3, "S")
    st4 = TPool(p, [64, 4], F32, 8, "st4")
    outp = TPool(p, [64, 4, 64], F32, 3, "yo")
    och = [p.chan() for _ in range(3)]
    S, S_r = S_p.get()
    p.op("pool", lambda e: e.memset(S[:], 0.0), writes=[S_r])

    def mm4(ps, ps_r, lhs_fn, rhs_fn, reads):
        mm4g(ps, ps_r, [(lhs_fn, rhs_fn, reads)])

    def mm4g(ps, ps_r, terms):
        for h in range(4):
            for ti, (lhs_fn, rhs_fn, reads) in enumerate(terms):
                p.op("pe", lambda e, h=h, lhs_fn=lhs_fn, rhs_fn=rhs_fn, ti=ti: e.matmul(
                    ps[:, h, :], lhs_fn(h), rhs_fn(h), start=(ti == 0), stop=(ti == len(terms) - 1)),
                    reads=reads, writes=[ps_r], same_bank_cont=(h > 0))

    for n in range(nch):
        i_in = arp.i
        ar, ar_r = arp.get()
        bk, bk_r = bkp.get()
        tm, tm_r = tmp_.get()
        p.dma("sp", inch[i_in], ar[:], AR[:, n], writes=[ar_r])
        p.dma("sp", inch2[i_in], bk[:], BK[:, n], writes=[bk_r])
        p.dma("sp", inch3[i_in], tm[:], TM[:, n], writes=[tm_r])
        pA, pA_r = psA.get()
        for h in range(4):
            p.op("pe", lambda e, h=h: e.matmul(pA[:, h, 0:2, :], bk[:, h, 0, :], ar[:, h, :, :], start=True, stop=True),
                 reads=[bk_r, ar_r], writes=[pA_r], same_bank_cont=(h % 2 == 1))
            p.op("pe", lambda e, h=h: e.matmul(pA[:, h, 2:4, :], bk[:, h, 1, :], ar[:, h, :, :], start=True, stop=True),
                 reads=[bk_r, ar_r], writes=[pA_r], same_bank_cont=True)
        AT, AT_r = AT_p.get()
        o_tt(p, "dve", AT[:], pA[:], msk[:], ALU.mult, [pA_r, mk_r], [AT_r])
        pN, pN_r = ps4.get()
        mm4(pN, pN_r, lambda h: ar[:, h, 0, :], lambda h: bk[:, h, 0, :], [ar_r, bk_r])
        PT, PT_r = sb4.get()
        o_tt(p, "dve", PT[:], pN[:], mlow[:], ALU.mult, [pN_r, mk_r], [PT_r])
        P_, P_r = sb4.get()
        p.op("act", lambda e, P_=P_, AT=AT: e.copy(out=P_[:], in_=AT[:, :, 0, :]), reads=[AT_r], writes=[P_r])
        M, M_r = sb4.get()
        o_tt(p, "pool", M[:], AT[:, :, 0, :], ident[:], ALU.add, [AT_r, mk_r], [M_r])
        for lev in range(1, 6):
            pq, pq_r = ps4.get()
            mm4(pq, pq_r, lambda h, P_=P_: P_[:, h, :], lambda h, PT=PT: PT[:, h, :], [P_r, PT_r])
            PT2, PT2_r = sb4.get()
            p.op("act", lambda e, PT2=PT2, pq=pq: e.copy(out=PT2[:], in_=pq[:]), reads=[pq_r], writes=[PT2_r])
            if lev < 5:
                pq2, pq2_r = ps4.get()
                mm4(pq2, pq2_r, lambda h, PT=PT: PT[:, h, :], lambda h, P_=P_: P_[:, h, :], [P_r, PT_r])
                P2, P2_r = sb4.get()
                p.op("dve", lambda e, P2=P2, pq2=pq2: e.tensor_copy(out=P2[:], in_=pq2[:]), reads=[pq2_r], writes=[P2_r])
            pm, pm_r = ps4.get()
            mm4(pm, pm_r, lambda h, PT2=PT2: PT2[:, h, :], lambda h, M=M: M[:, h, :], [PT2_r, M_r])
            M2, M2_r = sb4.get()
            o_tt(p, "dve", M2[:], pm[:], M[:], ALU.add, [pm_r, M_r], [M2_r])
            M, M_r = M2, M2_r
            PT, PT_r = PT2, PT2_r
            if lev < 5:
                P_, P_r = P2, P2_r
        pv, pv_r = ps4.get()
        mm4(pv, pv_r, lambda h: AT[:, h, 2, :], lambda h: tm[:, h, 3, :], [AT_r, tm_r])
        X, X_r = X_p.get()
        p.op("act", lambda e, X=X, pv=pv: e.copy(out=X[:, :, 64:128], in_=pv[:]), reads=[pv_r], writes=[X_r])
        p.op("pool", lambda e, X=X, tm=tm: e.tensor_copy(out=X[:, :, 0:64], in_=tm[:, :, 0, :]), reads=[tm_r], writes=[X_r])
        pX, pX_r = psX.get()
        mm4(pX, pX_r, lambda h, M=M: M[:, h, :], lambda h, X=X: X[:, h, :], [M_r, X_r])
        WU, WU_r = WU_p.get()
        p.op("dve", lambda e, WU=WU, pX=pX: e.tensor_copy(out=WU[:], in_=pX[:]), reads=[pX_r], writes=[WU_r])
        pG, pG_r = ps4.get()
        mm4(pG, pG_r, lambda h, WU=WU: WU[:, h, 0:64], lambda h: tm[:, h, 1, :], [WU_r, tm_r])
        GT, GT_r = sb4.get()
        o_tt(p, "dve", GT[:], pG[:], ident[:], ALU.add, [pG_r, mk_r], [GT_r])
        pH, pH_r = ps4.get()
        mm4g(pH, pH_r, [(lambda h: tm[:, h, 1, :], lambda h, WU=WU: WU[:, h, 64:128], [WU_r, tm_r]),
                        (lambda h: tm[:, h, 2, :], lambda h: tm[:, h, 3, :], [tm_r])])
        HG, HG_r = sb4.get()
        o_tt(p, "dve", HG[:], pH[:], gl_t[:, n, :].unsqueeze(2).to_broadcast([64, 4, 64]), ALU.mult, [pH_r, gl_r], [HG_r])
        pQ, pQ_r = ps4.get()
        mm4(pQ, pQ_r, lambda h, WU=WU: WU[:, h, 0:64], lambda h: AT[:, h, 1, :], [WU_r, AT_r])
        QT, QT_r = sb4.get()
        o_tt(p, "dve", QT[:], pQ[:], ar[:, :, 1, :], ALU.add, [pQ_r, ar_r], [QT_r])
        pY, pY_r = ps4.get()
        mm4g(pY, pY_r, [(lambda h: AT[:, h, 1, :], lambda h, WU=WU: WU[:, h, 64:128], [AT_r, WU_r]),
                        (lambda h: AT[:, h, 3, :], lambda h: tm[:, h, 3, :], [AT_r, tm_r]),
                        (lambda h, QT=QT: QT[:, h, :], lambda h, S=S: S[:, h, :], [QT_r, S_r])])
        pS, pS_r = ps4.get()
        mm4(pS, pS_r, lambda h, GT=GT: GT[:, h, :], lambda h, S=S: S[:, h, :], [GT_r, S_r])
        S2, S2_r = S_p.get()
        sg, sg_r = sb4.get()
        o_tt(p, "dve", sg[:], pS[:], gl_t[:, n, :].unsqueeze(2).to_broadcast([64, 4, 64]), ALU.mult, [pS_r, gl_r], [sg_r])
        o_tt(p, "dve", S2[:], sg[:], HG[:], ALU.add, [sg_r, HG_r], [S2_r])
        S, S_r = S2, S2_r
        y, y_r = sb4.get()
        p.op("act", lambda e, y=y, pY=pY: e.copy(out=y[:], in_=pY[:]), reads=[pY_r], writes=[y_r])
        s1, s1_r = st4.get()
        s2, s2_r = st4.get()
        ysq, ysq_r = sb4.get()
        p.op("dve", lambda e, s1=s1, y=y: e.reduce_sum(out=s1[:], in_=y[:], axis=AX.X), reads=[y_r], writes=[s1_r])
        o_tt(p, "pool", ysq[:], y[:], y[:], ALU.mult, [y_r], [ysq_r])
        p.op("dve", lambda e, s2=s2, ysq=ysq: e.reduce_sum(out=s2[:], in_=ysq[:], axis=AX.X), reads=[ysq_r], writes=[s2_r])
        o_ts(p, "dve", s1[:], s1[:], 1.0 / 64, None, ALU.mult, None, [s1_r], [s1_r])
        m2, m2_r = st4.get()
        o_tt(p, "dve", m2[:], s1[:], s1[:], ALU.mult, [s1_r], [m2_r])
        o_stt(p, s2[:], s2[:], 1.0 / 64, m2[:], ALU.mult, ALU.subtract, [s2_r, m2_r], [s2_r])
        o_ts(p, "dve", s2[:], s2[:], 64e-5, None, ALU.add, None, [s2_r], [s2_r])
        o_act(p, s2[:], s2[:], AF.Sqrt, [s2_r], [s2_r])
        p.op("dve", lambda e, s2=s2: e.reciprocal(out=s2[:], in_=s2[:]), reads=[s2_r], writes=[s2_r])
        yn, yn_r = sb4.get()
        o_tt(p, "dve", yn[:], y[:], s1[:].unsqueeze(2).to_broadcast([64, 4, 64]), ALU.subtract, [y_r, s1_r], [yn_r])
        o_tt(p, "dve", yn[:], yn[:], s2[:].unsqueeze(2).to_broadcast([64, 4, 64]), ALU.mult, [yn_r, s2_r], [yn_r])
        o_tt(p, "pool", yn[:], yn[:], ln_t[:, 0, :, :], ALU.mult, [yn_r, ln_r], [yn_r])
        o_tt(p, "pool", yn[:], yn[:], ln_t[:, 1, :, :], ALU.add, [yn_r, ln_r], [yn_r])
        bv, bv_r = sb4.get()
        o_tt(p, "dve", bv[:], tm[:, :, 3, :], bs_t[:, n, :].unsqueeze(2).to_broadcast([64, 4, 64]), ALU.mult, [tm_r, bs_r], [bv_r])
        o_tt(p, "pool", yn[:], yn[:], bv[:], ALU.add, [yn_r, bv_r], [yn_r])
        oi = outp.i
        o, o_r = outp.get()
        o_tt(p, "pool", o[:], yn[:], tm[:, :, 4, :], ALU.mult, [yn_r, tm_r], [o_r])
        p.dma("sp", och[oi], yo[:, n], o[:], reads=[o_r], is_output=True)
    p.finish()
    return nc


def rw1_host_layout(z, mu, w2, a2, g2, w0, a0, k_k, k_a, r_k, q):
    tl = z.shape[0]
    cs = slice(256 * q, 256 * (q + 1))
    zrkv = np.zeros((64, 3, 4, tl + 1), np.float32)
    mu_rkv = np.zeros((64, 3, 4, RTB), np.float32)
    for j in range(3):
        blk = z[:, j * 1024:(j + 1) * 1024][:, cs]
        zrkv[:, j, :, 1:] = blk.reshape(tl, 4, 64).transpose(2, 1, 0)
        mu_rkv[:, j, :, :] = mu[j * 1024:(j + 1) * 1024][cs].reshape(4, 64).T[:, :, None]

    def rows(c0, c1):
        o = np.zeros((c1 - c0, tl + 1), np.float32)
        o[:, 1:] = z[:, c0:c1].T
        return o
    mu_l = np.zeros((128, 4), np.float32)
    mu_l[0:64, 0] = mu[3072:3136]
    mu_l[0:64, 1] = mu[3136:3200]
    mu_l[0:128, 2] = mu[3200:3328]
    mu_l[0:32, 3] = mu[3328:3360]
    cvec = np.zeros((64, 5, 4, RTB), np.float32)
    for i, v in enumerate((w0, a0, k_k, k_a)):
        cvec[:, i, :, :] = v[cs].reshape(4, 64).T[:, :, None]
    cvec[:, 4, :, :] = r_k[4 * q:4 * q + 4].T[:, :, None]
    return dict(zrkv=zrkv, zw=rows(3072, 3136), za=rows(3136, 3200), zg0=rows(3200, 3328), zg1=np.concatenate([rows(3328, 3360), np.zeros((96, tl + 1), np.float32)], 0),
                mu_rkv=mu_rkv, mu_l=mu_l, w2=np.ascontiguousarray(w2[:, cs]), a2=np.ascontiguousarray(a2[:, cs]),
                g2a=np.ascontiguousarray(g2[0:128, cs]), g2b=np.concatenate([g2[128:160, cs], np.zeros((96, 256), np.float32)], 0), cvec=cvec,
                ones_in=np.ones((64, 256), np.float32))


def rw2_host_layout(o, lnx_w, lnx_b, q):
    tl = o["at"].shape[2]
    nch = tl // 64
    c5 = lambda a: np.asarray(a).reshape(64, 4, nch, 64)
    at, rt, bt, kt, xv, gg = (c5(o[k]) for k in ("at", "rt", "bt", "kt", "xv", "gg"))
    AR = np.stack([at, rt], 3).transpose(0, 2, 1, 3, 4)
    BK = np.stack([bt, kt], 3).transpose(0, 2, 1, 3, 4)
    TM = np.stack([at, bt, kt, xv, gg], 0).transpose(4, 3, 2, 0, 1)
    BS = c5(o["bs"])[0].transpose(2, 1, 0)
    GL = c5(o["ec"])[:, :, :, 63].transpose(0, 2, 1)
    cs = slice(256 * q, 256 * (q + 1))
    LN = np.broadcast_to(np.stack([lnx_w[cs].reshape(4, 64), lnx_b[cs].reshape(4, 64)], 0)[None], (64, 2, 4, 64))
    f = lambda a: np.ascontiguousarray(a, dtype=np.float32)
    return dict(AR=f(AR), BK=f(BK), TM=f(TM), BS=f(BS), GL=f(GL), LN=f(LN))


_PROGS = {}


def _prog(key, fn):
    if key not in _PROGS:
        _PROGS[key] = fn()
    return _PROGS[key]


def _run(nc, maps):
    return run_bass_kernel_spmd(nc, maps, core_ids=list(range(NCORES))).results


def _tok_shards(a):
    return [np.ascontiguousarray(a[c // 4, (c % 4) * NT:(c % 4 + 1) * NT, :].T) for c in range(NCORES)]


def _from_tok_shards(res, key, ncols):
    out = np.empty((B, T, ncols), np.float32)
    for c in range(NCORES):
        out[c // 4, (c % 4) * NT:(c % 4 + 1) * NT, :] = np.asarray(res[c][key]).T
    return out


def kernel(**inputs):
    I = {k: np.asarray(v) for k, v in inputs.items()}
    f32 = lambda a: np.ascontiguousarray(a, dtype=np.float32)
    xs = _tok_shards(f32(I["x"]))
    for layer in range(4):
        i = layer // 2
        if layer % 2 == 0:
            nc = _prog("k1e", lambda: build_k1(EVEN_IN))
            r = _run(nc, [{"xT": xs[c], "g": f32(I["norm_mix_pre"][layer]), "w": f32(I["ev_w_in"][i])} for c in range(NCORES)])
            pfull = _from_tok_shards(r, "oT", EVEN_IN)
            maps = []
            for c in range(NCORES):
                b, q = c // 4, c % 4
                d = s5_host_layout(f32(I["s5_lam_re"][i]), f32(I["s5_lam_im"][i]), f32(I["s5_log_dt"][i]),
                                   f32(I["s5_b_re"][i]), f32(I["s5_b_im"][i]), f32(I["s5_c_re"][i]),
                                   f32(I["s5_c_im"][i]), f32(I["s5_d"][i]), q)
                d["uT"] = np.ascontiguousarray(pfull[b, :, 256 * q:256 * (q + 1)].T)
                maps.append(d)
            rs = _run(_prog("s5", build_s5), maps)
            ycat = np.empty((B, T, D), np.float32)
            for c in range(NCORES):
                b, q = c // 4, c % 4
                ycat[b, :, 256 * q:256 * (q + 1)] = np.asarray(rs[c]["yT"]).T
            maps = [rw1_host_layout(pfull[c // 4, :, 1024:], f32(I["ev_shift_mu"][i]), f32(I["rw_w2"][i]),
                                    f32(I["rw_a2"][i]), f32(I["rw_g2"][i]), f32(I["rw_w0"][i]), f32(I["rw_a0"][i]),
                                    f32(I["rw_k_k"][i]), f32(I["rw_k_a"][i]), f32(I["rw_r_k"][i]), c % 4)
                    for c in range(NCORES)]
            r1 = _run(_prog("rw1", build_rw1), maps)
            maps = [rw2_host_layout(r1[c], f32(I["rw_lnx_w"][i]), f32(I["rw_lnx_b"][i]), c % 4) for c in range(NCORES)]
            r2 = _run(_prog("rw2", build_rw2), maps)
            for c in range(NCORES):
                b, q = c // 4, c % 4
                yo = np.asarray(r2[c]["yo"])
                ycat[b, :, 1024 + 256 * q:1024 + 256 * (q + 1)] = yo.transpose(1, 0, 2, 3).reshape(T, 256)
            ys = _tok_shards(ycat)
            nc = _prog("k2g", lambda: build_k2(D, glu=True))
            r = _run(nc, [{"inT": ys[c], "xT": xs[c], "g": f32(I["norm_mix_post"][layer]), "w": f32(I["ev_w_out"][i]),
                           "wglu": f32(I["s5_w_glu"][i])} for c in range(NCORES)])
        else:
            nc = _prog("k1o", lambda: build_k1(2 * D))
            r = _run(nc, [{"xT": xs[c], "g": f32(I["norm_mix_pre"][layer]), "w": f32(I["od_w_in"][i])} for c in range(NCORES)])
            pfull = _from_tok_shards(r, "oT", 2 * D)
            maps = []
            for c in range(NCORES):
                b, q = c // 4, c % 4
                cs = slice(512 * q, 512 * (q + 1))
                v = np.stack([I["od_conv_w"][i][0][cs], I["od_conv_w"][i][1][cs], I["od_conv_w"][i][2][cs],
                              I["od_conv_w"][i][3][cs], I["od_conv_b"][i][cs], I["lru_b_r"][i][cs],
                              I["lru_b_i"][i][cs], I["lru_lam"][i][cs]], -1)
                maps.append({"gateT": np.ascontiguousarray(pfull[b, :, cs].T),
                             "xbT": np.ascontiguousarray(pfull[b, :, D + 512 * q:D + 512 * (q + 1)].T),
                             "vecs": f32(v.reshape(4, 128, 8).transpose(1, 0, 2)),
                             "wr": f32(I["lru_w_r"][i][2 * q:2 * q + 2]), "wi": f32(I["lru_w_i"][i][2 * q:2 * q + 2])})
            rl = _run(_prog("lru", build_lru), maps)
            ycat = np.empty((B, T, D), np.float32)
            for c in range(NCORES):
                b, q = c // 4, c % 4
                ycat[b, :, 512 * q:512 * (q + 1)] = np.asarray(rl[c]["yT"]).T
            ys = _tok_shards(ycat)
            nc = _prog("k2p", lambda: build_k2(D))
            r = _run(nc, [{"inT": ys[c], "xT": xs[c], "g": f32(I["norm_mix_post"][layer]), "w": f32(I["od_w_out"][i])}
                          for c in range(NCORES)])
        xs = [f32(r[c]["oT"]) for c in range(NCORES)]
        nc = _prog("k1f", lambda: build_k1(DFF, swiglu=True))
        r = _run(nc, [{"xT": xs[c], "g": f32(I["norm_ffn_pre"][layer]), "w": f32(I["ffn_w_gate"][layer]),
                       "w2": f32(I["ffn_w_up"][layer])} for c in range(NCORES)])
        aT = [np.asarray(r[c]["oT"]) for c in range(NCORES)]
        nc = _prog("k2f", lambda: build_k2(DFF, in_bf16=True))
        r = _run(nc, [{"inT": aT[c], "xT": xs[c], "g": f32(I["norm_ffn_post"][layer]), "w": f32(I["ffn_w_down"][layer])}
                      for c in range(NCORES)])
        xs = [f32(r[c]["oT"]) for c in range(NCORES)]
    out = np.empty((B, T, D), np.float32)
    for c in range(NCORES):
        out[c // 4, (c % 4) * NT:(c % 4 + 1) * NT, :] = xs[c].T
    return out
```

```python
import contextlib
import numpy as np
import concourse.bass as bass
import concourse.mybir as mybir
from concourse.bass_utils import run_bass_kernel_spmd

F32 = mybir.dt.float32
BF16 = mybir.dt.bfloat16
AF = mybir.ActivationFunctionType
ALU = mybir.AluOpType
AX = mybir.AxisListType

NCORES = 8
D = 2048
B = 2
T = 4096
NT = 1024
DFF = 5632
EVEN_IN = 4384
NORM_EPS = 1e-6

ENGS = ("pe", "act", "dve", "pool", "sp")


class Res:
    __slots__ = ("lw", "rd")

    def __init__(self):
        self.lw = None
        self.rd = []


class Chan:
    __slots__ = ("sem", "cnt")

    def __init__(self, sem):
        self.sem = sem
        self.cnt = 0


class _Rec:
    def __init__(self):
        self.call = None

    def __getattr__(self, name):
        def f(*a, **k):
            assert self.call is None, "op closure must emit exactly one instruction"
            self.call = (name, a, k)
            return self
        return f


class Prog:
    def __init__(self, nc):
        self.nc = nc
        self.stack = contextlib.ExitStack()
        self.q = {e: [] for e in ENGS}
        self.esem = {e: self.stack.enter_context(nc.semaphore("es_" + e)) for e in ENGS}
        self.ecnt = {e: 0 for e in ENGS}
        self.seen = {e: {} for e in ENGS}
        self.out_events = []
        self._n = 0

    def name(self, base):
        self._n += 1
        return "%s_%d" % (base, self._n)

    def sbuf(self, shape, dt, name="sb"):
        return self.stack.enter_context(self.nc.sbuf_tensor(self.name(name), list(shape), dt))

    def psum(self, shape, dt=F32, name="ps"):
        return self.stack.enter_context(self.nc.psum_tensor(self.name(name), list(shape), dt))

    def chan(self, name="ch"):
        return Chan(self.stack.enter_context(self.nc.semaphore(self.name(name))))

    def _deps(self, eng, reads, writes, pe_group_start=True):
        evs = []
        for r in reads:
            if r.lw is not None and not (eng == "pe" and r.lw[2] == "pe"):
                evs.append(r.lw)
        for w in writes:
            if w.lw is not None and not (eng == "pe" and w.lw[2] == "pe" and not pe_group_start):
                evs.append(w.lw)
            for ev in w.rd:
                if ev[2] != eng:
                    evs.append(ev)
        waits = {}
        seen = self.seen[eng]
        for (sem, val, _e) in evs:
            k = id(sem)
            if seen.get(k, 0) >= val:
                continue
            if k not in waits or waits[k][1] < val:
                waits[k] = (sem, val)
        for k, (sem, val) in waits.items():
            seen[k] = val
        return list(waits.values())

    def _commit(self, ev, reads, writes):
        for w in writes:
            w.lw = ev
            w.rd = []
        for r in reads:
            r.rd.append(ev)

    def op(self, eng, fn, reads=(), writes=(), same_bank_cont=False):
        rec = _Rec()
        fn(rec)
        gs = True
        if eng == "pe" and rec.call[0] == "matmul":
            gs = bool(rec.call[2].get("start", True)) and not same_bank_cont
        waits = self._deps(eng, reads, writes, pe_group_start=gs)
        self.ecnt[eng] += 1
        ev = (self.esem[eng], self.ecnt[eng], eng)
        self.q[eng].append((waits, rec.call, (self.esem[eng], 1)))
        self._commit(ev, reads, writes)
        return ev

    def dma(self, queue, chan, out, in_, reads=(), writes=(), is_output=False, **kw):
        waits = self._deps(queue, reads, writes)
        chan.cnt += 1
        ev = (chan.sem, 16 * chan.cnt, "dma")
        self.q[queue].append((waits, ("dma_start", (), dict(out=out, in_=in_, **kw)), (chan.sem, 16)))
        self._commit(ev, reads, writes)
        if is_output:
            self.out_events.append(ev)
        return ev

    def finish(self):
        fin = {}
        for (sem, val, _e) in self.out_events:
            k = id(sem)
            if k not in fin or fin[k][1] < val:
                fin[k] = (sem, val)
        nc = self.nc
        q = self.q
        fin_waits = list(fin.values())

        def replay(engobj, items, extra_waits=()):
            for (waits, fn, inc) in items:
                for (sem, val) in waits:
                    engobj.wait_ge(sem, val)
                name, a, k = fn
                ins = getattr(engobj, name)(*a, **k)
                if inc is not None:
                    ins.then_inc(inc[0], inc[1])
            for (sem, val) in extra_waits:
                engobj.wait_ge(sem, val)

        with nc.Block() as block:
            @block.sync
            def _(e):
                replay(e, q["sp"], fin_waits)

            @block.tensor
            def _(e):
                replay(e, q["pe"])

            @block.scalar
            def _(e):
                replay(e, q["act"])

            @block.vector
            def _(e):
                replay(e, q["dve"])

            @block.gpsimd
            def _(e):
                replay(e, q["pool"])
        self.stack.close()


class Dense:
    def __init__(self, p, nt=NT):
        self.p = p
        nc = p.nc
        self.nt = nt
        self.ntb = nt // 512
        self.ones = p.sbuf([128, 128], BF16, "ones")
        self.ones_r = Res()
        p.op("pool", lambda e: e.memset(self.ones[:], 1.0), writes=[self.ones_r])
        self.banks = [p.psum([128, 512], F32, "bank") for _ in range(8)]
        self.bank_r = [Res() for _ in range(8)]
        self._bk = 0
        self.wslots = [p.sbuf([128, 16, 256], BF16, "wslot") for _ in range(3)]
        self.wslot_r = [Res() for _ in range(3)]
        self.wchan = [p.chan("wch") for _ in range(3)]
        self._ws = 0
        self.ostg = [p.sbuf([128, 512], F32, "ostg") for _ in range(4)]
        self.ostg_r = [Res() for _ in range(4)]
        self.ochan = [p.chan("och") for _ in range(4)]
        self._os = 0

    def bank(self):
        i = self._bk
        self._bk = (self._bk + 1) % 8
        return self.banks[i], self.bank_r[i]

    def wslot(self):
        i = self._ws
        self._ws = (self._ws + 1) % 3
        return self.wslots[i], self.wslot_r[i], self.wchan[i]

    def ostage(self):
        i = self._os
        self._os = (self._os + 1) % 4
        return self.ostg[i], self.ostg_r[i], self.ochan[i]


def load_fm(p, chan, dram_ap, nchunks, nt, dt=F32, name="act", queue="sp"):
    t = p.sbuf([128, nchunks, nt], dt, name)
    rs = [Res() for _ in range(nchunks)]
    src = dram_ap.rearrange("(c q) n -> q c n", q=128)
    step = max(1, 4)
    for c0 in range(0, nchunks, step):
        c1 = min(nchunks, c0 + step)
        p.dma(queue, chan, t[:, c0:c1, :], src[:, c0:c1, :], writes=rs[c0:c1])
    return t, rs


def load_vec(p, chan, dram_ap, nchunks, name="vec"):
    t = p.sbuf([128, nchunks], F32, name)
    r = Res()
    src = dram_ap.rearrange("(c q) -> q c", q=128)
    p.dma("sp", chan, t[:], src, writes=[r], allow_slow_non_contiguous=True)
    return t, r


def rmsnorm_fm(p, dn, x_sb, x_r, nchunks, g_sb, g_r, out_dt=BF16, name="h", inplace=False):
    nt = dn.nt
    dmodel = nchunks * 128
    if inplace:
        h, h_r = x_sb, x_r
    else:
        h = p.sbuf([128, nchunks, nt], out_dt, name)
        h_r = [Res() for _ in range(nchunks)]
    sq = [p.sbuf([128, nt], BF16, "sq") for _ in range(2)]
    sq_r = [Res(), Res()]
    rstd = p.sbuf([128, nt], F32, "rstd")
    rstd_r = [Res() for _ in range(dn.ntb)]
    banks = [dn.bank() for _ in range(dn.ntb)]
    for c in range(nchunks):
        s, sr = sq[c % 2], sq_r[c % 2]
        p.op("act", lambda e, s=s, c=c: e.activation(out=s[:], in_=x_sb[:, c, :], func=AF.Square),
             reads=[x_r[c]], writes=[sr])
        for tb in range(dn.ntb):
            bk, bkr = banks[tb]
            p.op("pe", lambda e, bk=bk, s=s, tb=tb, c=c: e.matmul(
                bk[:], dn.ones[:], s[:, tb * 512:(tb + 1) * 512],
                start=(c == 0), stop=(c == nchunks - 1)),
                reads=[sr, dn.ones_r], writes=[bkr])
    for tb in range(dn.ntb):
        bk, bkr = banks[tb]
        sl = slice(tb * 512, (tb + 1) * 512)
        p.op("dve", lambda e, bk=bk, sl=sl: e.tensor_scalar(
            rstd[:, sl], bk[:], 1.0 / dmodel, NORM_EPS, ALU.mult, ALU.add),
            reads=[bkr], writes=[rstd_r[tb]])
        p.op("act", lambda e, sl=sl: e.activation(out=rstd[:, sl], in_=rstd[:, sl], func=AF.Sqrt),
             reads=[rstd_r[tb]], writes=[rstd_r[tb]])
        p.op("dve", lambda e, sl=sl: e.reciprocal(out=rstd[:, sl], in_=rstd[:, sl]),
             reads=[rstd_r[tb]], writes=[rstd_r[tb]])
    for c in range(nchunks):
        for tb in range(dn.ntb):
            sl = slice(tb * 512, (tb + 1) * 512)
            p.op("dve", lambda e, c=c, sl=sl: e.scalar_tensor_tensor(
                out=h[:, c, sl], in0=x_sb[:, c, sl], scalar=g_sb[:, c:c + 1], in1=rstd[:, sl],
                op0=ALU.mult, op1=ALU.mult),
                reads=[x_r[c], g_r, rstd_r[tb]], writes=[h_r[c]])
    return h, h_r


def linear_fm(p, dn, h, h_r, kchunks, w_dram, fdim, epilogue, w2_dram=None):
    nt = dn.nt
    FB = 256
    kstep = 16
    for f0 in range(0, fdim, FB):
        fb = min(FB, fdim - f0)
        slots = []
        for wd in ([w_dram] if w2_dram is None else [w_dram, w2_dram]):
            parts = []
            for k0 in range(0, kchunks, kstep):
                k1 = min(kchunks, k0 + kstep)
                ws, wr, wc = dn.wslot()
                src = wd[k0 * 128:k1 * 128, f0:f0 + fb].rearrange("(c q) f -> q c f", q=128)
                p.dma("pool", wc, ws[:, 0:k1 - k0, 0:fb], src, writes=[wr])
                parts.append((ws, wr, k0, k1))
            slots.append(parts)
        for fs in range(0, fb, 128):
            fsz = min(128, fb - fs)
            for tb in range(dn.ntb):
                outs = []
                for parts in slots:
                    bk, bkr = dn.bank()
                    nk = kchunks
                    for (ws, wr, k0, k1) in parts:
                        for k in range(k0, k1):
                            p.op("pe", lambda e, bk=bk, ws=ws, k=k, k0=k0, fs=fs, fsz=fsz, tb=tb: e.matmul(
                                bk[0:fsz, :], ws[:, k - k0, fs:fs + fsz], h[:, k, tb * 512:(tb + 1) * 512],
                                start=(k == 0), stop=(k == nk - 1)),
                                reads=[wr, h_r[k]], writes=[bkr])
                    outs += [bk, bkr]
                epilogue(f0 + fs, fsz, tb, *outs)


def build_k1(fdim, swiglu=False, nt=NT):
    nc = bass.Bass("TRN2", target_bir_lowering=False)
    xT = nc.dram_tensor("xT", [D, nt], F32, kind="ExternalInput").ap()
    g = nc.dram_tensor("g", [D], F32, kind="ExternalInput").ap()
    w = nc.dram_tensor("w", [D, fdim], F32, kind="ExternalInput").ap()
    w2 = nc.dram_tensor("w2", [D, fdim], F32, kind="ExternalInput").ap() if swiglu else None
    odt = BF16 if swiglu else F32
    oT = nc.dram_tensor("oT", [fdim, nt], odt, kind="ExternalOutput").ap()
    p = Prog(nc)
    dn = Dense(p, nt)
    ch_in = p.chan("chin")
    x_sb, x_r = load_fm(p, ch_in, xT, D // 128, nt, name="x")
    g_sb, g_r = load_vec(p, p.chan("chg"), g, D // 128)
    h, h_r = rmsnorm_fm(p, dn, x_sb, x_r, D // 128, g_sb, g_r)
    ostg_bf = [p.sbuf([128, 512], BF16, "ostgb") for _ in range(4)]
    sil = [p.sbuf([128, 512], F32, "sil") for _ in range(2)]
    sil_r = [Res(), Res()]
    cnt = [0]

    def epi_plain(f0, fsz, tb, bk, bkr):
        st, sr, sc = dn.ostage()
        eng = "act" if cnt[0] % 2 == 0 else "dve"
        cnt[0] += 1
        if eng == "act":
            p.op("act", lambda e: e.copy(out=st[0:fsz, :], in_=bk[0:fsz, :]), reads=[bkr], writes=[sr])
        else:
            p.op("dve", lambda e: e.tensor_copy(out=st[0:fsz, :], in_=bk[0:fsz, :]), reads=[bkr], writes=[sr])
        p.dma("sp", sc, oT[f0:f0 + fsz, tb * 512:(tb + 1) * 512], st[0:fsz, :], reads=[sr], is_output=True)

    def epi_swiglu(f0, fsz, tb, bg, bgr, bu, bur):
        i = dn._os
        st, sr, sc = dn.ostage()
        stb = ostg_bf[i]
        s, s_r = sil[cnt[0] % 2], sil_r[cnt[0] % 2]
        cnt[0] += 1
        p.op("act", lambda e: e.activation(out=s[0:fsz, :], in_=bg[0:fsz, :], func=AF.Silu),
             reads=[bgr], writes=[s_r])
        p.op("dve", lambda e: e.tensor_tensor(out=stb[0:fsz, :], in0=s[0:fsz, :], in1=bu[0:fsz, :], op=ALU.mult),
             reads=[s_r, bur], writes=[sr])
        p.dma("sp", sc, oT[f0:f0 + fsz, tb * 512:(tb + 1) * 512], stb[0:fsz, :], reads=[sr], is_output=True)

    linear_fm(p, dn, h, h_r, D // 128, w, fdim, epi_swiglu if swiglu else epi_plain, w2_dram=w2)
    p.finish()
    return nc


def build_k2(kdim, in_bf16=False, glu=False, nt=NT):
    nc = bass.Bass("TRN2", target_bir_lowering=False)
    in_dt = BF16 if in_bf16 else F32
    inT = nc.dram_tensor("inT", [kdim, nt], in_dt, kind="ExternalInput").ap()
    xT = nc.dram_tensor("xT", [D, nt], F32, kind="ExternalInput").ap()
    g = nc.dram_tensor("g", [D], F32, kind="ExternalInput").ap()
    w = nc.dram_tensor("w", [kdim, D], F32, kind="ExternalInput").ap()
    wg = nc.dram_tensor("wglu", [1024, 1024], F32, kind="ExternalInput").ap() if glu else None
    oT = nc.dram_tensor("oT", [D, nt], F32, kind="ExternalOutput").ap()
    p = Prog(nc)
    dn = Dense(p, nt)
    kch = kdim // 128
    h = p.sbuf([128, kch, nt], BF16, "hin")
    h_r = [Res() for _ in range(kch)]
    ch_in = p.chan("chin")
    src = inT.rearrange("(c q) n -> q c n", q=128)
    g_sb, g_r = load_vec(p, p.chan("chg"), g, D // 128)
    if glu:
        ys = p.sbuf([128, 8, nt], BF16, "ys5")
        ys_r = [Res() for _ in range(8)]
        for c0 in range(0, 8, 4):
            p.dma("pool", ch_in, ys[:, c0:c0 + 4, :], src[:, c0:c0 + 4, :], writes=ys_r[c0:c0 + 4])
        for c0 in range(8, 16, 4):
            p.dma("pool", ch_in, h[:, c0:c0 + 4, :], src[:, c0:c0 + 4, :], writes=h_r[c0:c0 + 4])
        sg = [p.sbuf([128, 512], F32, "sg") for _ in range(2)]
        sg_r = [Res(), Res()]
        cg = [0]

        def epi_glu(f0, fsz, tb, bk, bkr):
            s_, sr_ = sg[cg[0] % 2], sg_r[cg[0] % 2]
            cg[0] += 1
            c = f0 // 128
            sl = slice(tb * 512, (tb + 1) * 512)
            p.op("act", lambda e: e.activation(out=s_[:], in_=bk[:], func=AF.Sigmoid), reads=[bkr], writes=[sr_])
            p.op("dve", lambda e: e.tensor_tensor(out=h[:, c, sl], in0=s_[:], in1=ys[:, c, sl], op=ALU.mult),
                 reads=[sr_, ys_r[c]], writes=[h_r[c]])
        linear_fm(p, dn, ys, ys_r, 8, wg, 1024, epi_glu)
    else:
        q = "sp" if in_bf16 else "pool"
        for c0 in range(0, kch, 4):
            c1 = min(kch, c0 + 4)
            p.dma(q, ch_in, h[:, c0:c1, :], src[:, c0:c1, :], writes=h_r[c0:c1])
    o_sb = p.sbuf([128, D // 128, nt], F32, "osb")
    o_r = [Res() for _ in range(D // 128)]
    ce = [0]

    def epi_o(f0, fsz, tb, bk, bkr):
        c = f0 // 128
        sl = slice(tb * 512, (tb + 1) * 512)
        ce[0] += 1
        if ce[0] % 2 == 0:
            p.op("act", lambda e: e.copy(out=o_sb[:, c, sl], in_=bk[:]), reads=[bkr], writes=[o_r[c]])
        else:
            p.op("dve", lambda e: e.tensor_copy(out=o_sb[:, c, sl], in_=bk[:]), reads=[bkr], writes=[o_r[c]])
    linear_fm(p, dn, h, h_r, kch, w, D, epi_o)
    hn, hn_r = rmsnorm_fm(p, dn, o_sb, o_r, D // 128, g_sb, g_r, out_dt=F32, inplace=True)
    xsrc = xT.rearrange("(c q) n -> q c n", q=128)
    odst = oT.rearrange("(c q) n -> q c n", q=128)
    xst = [p.sbuf([128, nt], F32, "xst") for _ in range(3)]
    xst_r = [Res() for _ in range(3)]
    xch = [p.chan("xch") for _ in range(3)]
    for c in range(D // 128):
        i = c % 3
        p.dma("sp", xch[i], xst[i][:], xsrc[:, c, :], writes=[xst_r[i]])
        p.op("dve" if c % 2 == 0 else "pool", lambda e, i=i, c=c: e.tensor_tensor(
            out=xst[i][:], in0=xst[i][:], in1=hn[:, c, :], op=ALU.add),
            reads=[hn_r[c], xst_r[i]], writes=[xst_r[i]])
        p.dma("sp", xch[i], odst[:, c, :], xst[i][:], reads=[xst_r[i]], is_output=True)
    p.finish()
    return nc


class TPool:
    def __init__(self, p, shape, dt, n, name="tp", psum=False):
        mk = p.psum if psum else p.sbuf
        self.t = [mk(shape, dt, name) for _ in range(n)]
        self.r = [Res() for _ in range(n)]
        self.i = 0

    def get(self):
        i = self.i
        self.i = (i + 1) % len(self.t)
        return self.t[i], self.r[i]


def o_tt(p, eng, out, in0, in1, op, reads, writes):
    return p.op(eng, lambda e: e.tensor_tensor(out=out, in0=in0, in1=in1, op=op), reads=reads, writes=writes)


def o_ts(p, eng, out, in0, s1, s2, op0, op1, reads, writes):
    if s2 is None:
        return p.op(eng, lambda e: e.tensor_scalar(out, in0, s1, None, op0), reads=reads, writes=writes)
    return p.op(eng, lambda e: e.tensor_scalar(out, in0, s1, s2, op0, op1), reads=reads, writes=writes)


def o_stt(p, out, in0, scalar, in1, op0, op1, reads, writes):
    return p.op("dve", lambda e: e.scalar_tensor_tensor(out=out, in0=in0, scalar=scalar, in1=in1, op0=op0, op1=op1),
                reads=reads, writes=writes)


def o_act(p, out, in_, func, reads, writes, scale=1.0, bias=None):
    if bias is None:
        return p.op("act", lambda e: e.activation(out=out, in_=in_, func=func, scale=scale), reads=reads, writes=writes)
    return p.op("act", lambda e: e.activation(out=out, in_=in_, func=func, scale=scale, bias=bias),
                reads=reads, writes=writes)


def gelu_tanh(p, tp, out, x, xr, outr, eng="pool"):
    t1, r1 = tp.get()
    o_tt(p, eng, t1[:], x, x, ALU.mult, [xr], [r1])
    o_ts(p, eng, t1[:], t1[:], 0.044715, 1.0, ALU.mult, ALU.add, [r1], [r1])
    o_tt(p, eng, t1[:], t1[:], x, ALU.mult, [r1, xr], [r1])
    o_act(p, t1[:], t1[:], AF.Sigmoid, [r1], [r1], scale=1.5957691216057308)
    o_tt(p, eng, out, t1[:], x, ALU.mult, [r1, xr], [outr])


def build_lru(tlen=T):
    nc = bass.Bass("TRN2", target_bir_lowering=False)
    CH = 512
    gateT = nc.dram_tensor("gateT", [CH, tlen], F32, kind="ExternalInput").ap()
    xbT = nc.dram_tensor("xbT", [CH, tlen], F32, kind="ExternalInput").ap()
    vecs = nc.dram_tensor("vecs", [128, 4, 8], F32, kind="ExternalInput").ap()
    wr = nc.dram_tensor("wr", [2, 256, 256], F32, kind="ExternalInput").ap()
    wi = nc.dram_tensor("wi", [2, 256, 256], F32, kind="ExternalInput").ap()
    yT = nc.dram_tensor("yT", [CH, tlen], F32, kind="ExternalOutput").ap()
    p = Prog(nc)
    ntb = tlen // 512
    vec = p.sbuf([128, 4, 8], F32, "vec")
    vec_r = Res()
    p.dma("sp", p.chan(), vec[:], vecs, writes=[vec_r])
    c8 = p.sbuf([128, 4], F32, "c8")
    c8_r = Res()
    o_act(p, c8[:], vec[:, :, 7], AF.Exp, [vec_r], [c8_r], scale=-1.0)
    o_ts(p, "dve", c8[:], c8[:], 1.0, None, ALU.add, None, [c8_r], [c8_r])
    o_act(p, c8[:], c8[:], AF.Ln, [c8_r], [c8_r])
    o_ts(p, "dve", c8[:], c8[:], -8.0, None, ALU.mult, None, [c8_r], [c8_r])
    wsb = {}
    wch = p.chan()
    for nm, wd in (("r", wr), ("i", wi)):
        t = p.sbuf([128, 2, 2, 256], BF16, "w" + nm)
        r = Res()
        wch = p.chan()
        for n in range(2):
            p.dma("pool", wch, t[:, n, :, :], wd[n].rearrange("(c q) d -> q c d", q=128), writes=[r])
        wsb[nm] = (t, r)
    xin = TPool(p, [128, 515], F32, 4, "xin")
    gin = TPool(p, [128, 512], F32, 3, "gin")
    xch = [p.chan() for _ in range(4)]
    gch = [p.chan() for _ in range(3)]
    och = [p.chan() for _ in range(3)]
    xcp = TPool(p, [128, 512], F32, 4, "xc")
    xcb = TPool(p, [128, 512], BF16, 4, "xcb")
    tmp = TPool(p, [128, 512], F32, 8, "tmp")
    hp = TPool(p, [128, 512], F32, 8, "h")
    op_ = TPool(p, [128, 512], F32, 3, "o")
    psp = TPool(p, [128, 512], F32, 8, "ps", psum=True)
    hprev = {}
    xsrc = xbT.rearrange("(c q) n -> q c n", q=128)
    gsrc = gateT.rearrange("(c q) n -> q c n", q=128)
    ydst = yT.rearrange("(c q) n -> q c n", q=128)
    for tb in range(ntb):
        t0 = tb * 512
        for n in range(2):
            xcs = []
            for cc in range(2):
                c = 2 * n + cc
                i = xin.i
                xt, xr = xin.get()
                if tb == 0:
                    p.op("pool", lambda e, xt=xt: e.memset(xt[:, 0:3], 0.0), writes=[xr])
                    p.dma("sp", xch[i], xt[:, 3:515], xsrc[:, c, 0:512], writes=[xr])
                else:
                    p.dma("sp", xch[i], xt[:, 0:515], xsrc[:, c, t0 - 3:t0 + 512], writes=[xr])
                xc, xcr = xcp.get()
                o_ts(p, "dve", xc[:], xt[:, 3:515], vec[:, c, 3:4], vec[:, c, 4:5], ALU.mult, ALU.add, [xr, vec_r], [xcr])
                for j in range(3):
                    o_stt(p, xc[:], xt[:, j:j + 512], vec[:, c, j:j + 1], xc[:], ALU.mult, ALU.add, [xr, vec_r, xcr], [xcr])
                xb_, xbr = xcb.get()
                p.op("act", lambda e, xb_=xb_, xc=xc: e.copy(out=xb_[:], in_=xc[:]), reads=[xcr], writes=[xbr])
                xcs.append((xc, xcr, xb_, xbr))
            for dc in range(2):
                c = 2 * n + dc
                xc, xcr = xcs[dc][0], xcs[dc][1]
                pr, prr = psp.get()
                pi, pir = psp.get()
                for (pt, ptr, nm) in ((pr, prr, "r"), (pi, pir, "i")):
                    wt, wtr = wsb[nm]
                    for cc in range(2):
                        p.op("pe", lambda e, pt=pt, wt=wt, cc=cc, dc=dc, n=n, xb_=xcs[cc][2]: e.matmul(
                            pt[:], wt[:, n, cc, dc * 128:(dc + 1) * 128], xb_[:], start=(cc == 0), stop=(cc == 1)),
                            reads=[wtr, xcs[cc][3]], writes=[ptr])
                sr, srr = tmp.get()
                si, sir = tmp.get()
                o_act(p, sr[:], pr[:], AF.Sigmoid, [prr, vec_r], [srr], bias=vec[:, c, 5:6])
                o_act(p, si[:], pi[:], AF.Sigmoid, [pir, vec_r], [sir], bias=vec[:, c, 6:7])
                a_, ar = tmp.get()
                o_act(p, a_[:], sr[:], AF.Exp, [srr, c8_r], [ar], scale=c8[:, c:c + 1])
                m_, mr = tmp.get()
                o_tt(p, "pool", m_[:], a_[:], a_[:], ALU.mult, [ar], [mr])
                o_ts(p, "pool", m_[:], m_[:], -1.0, 1.0, ALU.mult, ALU.add, [mr], [mr])
                o_act(p, m_[:], m_[:], AF.Sqrt, [mr], [mr])
                o_tt(p, "pool", m_[:], m_[:], si[:], ALU.mult, [mr, sir], [mr])
                o_tt(p, "dve", m_[:], m_[:], xc[:], ALU.mult, [mr, xcr], [mr])
                h_, hr = hp.get()
                if c in hprev:
                    ph, phr = hprev[c]
                    p.op("dve", lambda e, h_=h_, a_=a_, m_=m_, ph=ph: e.tensor_tensor_scan(
                        out=h_[:], data0=a_[:], data1=m_[:], initial=ph[:, 511:512], op0=ALU.mult, op1=ALU.add),
                        reads=[ar, mr, phr], writes=[hr])
                else:
                    p.op("dve", lambda e, h_=h_, a_=a_, m_=m_: e.tensor_tensor_scan(
                        out=h_[:], data0=a_[:], data1=m_[:], initial=0.0, op0=ALU.mult, op1=ALU.add),
                        reads=[ar, mr], writes=[hr])
                hprev[c] = (h_, hr)
                gi_ = gin.i
                gt, gr_ = gin.get()
                p.dma("sp", gch[gi_], gt[:], gsrc[:, c, t0:t0 + 512], writes=[gr_])
                oi = op_.i
                ot, otr = op_.get()
                gelu_tanh(p, tmp, ot[:], gt[:], gr_, otr, eng="pool")
                o_tt(p, "dve", ot[:], ot[:], h_[:], ALU.mult, [otr, hr], [otr])
                p.dma("sp", och[oi], ydst[:, c, t0:t0 + 512], ot[:], reads=[otr], is_output=True)
    p.finish()
    return nc


TWO_PI = 6.283185307179586


def frac_wrap(p, x, ti, tf, reads, r):
    p.op("dve", lambda e: e.tensor_copy(out=ti, in_=x), reads=reads, writes=[r])
    p.op("dve", lambda e: e.tensor_copy(out=tf, in_=ti), reads=reads, writes=[r])
    o_tt(p, "dve", x, x, tf, ALU.subtract, reads, [r])
    o_stt(p, tf, x, 0.5, x, ALU.is_gt, ALU.subtract, reads, [r])
    o_stt(p, x, tf, 0.5, tf, ALU.is_gt, ALU.subtract, reads, [r])


def sincos_turns(p, S, C, ph, tabs, halfpi_ap, reads, rS, rC, rt):
    o_act(p, S, ph, AF.Sin, reads, [rS], scale=TWO_PI)
    o_act(p, tabs, ph, AF.Abs, reads, [rt])
    o_act(p, C, tabs, AF.Sin, [rt], [rC], scale=-TWO_PI, bias=halfpi_ap)


def s5_pre(p, lr, li, ldt, shape, pi_ap, tagr):
    r = Res()
    mk = lambda nm: p.sbuf(shape, F32, "s5" + nm)
    dt, mag, f0, f0c, sn, cs = mk("dt"), mk("mag"), mk("f0"), mk("f0c"), mk("sn"), mk("cs")
    are, aim, den, f1 = mk("are"), mk("aim"), mk("den"), mk("f1")
    t1, t2, gre, gim = f0c, sn, cs, dt
    R = [tagr, r]
    o_act(p, dt[:], ldt, AF.Exp, R, [r])
    o_tt(p, "dve", mag[:], lr, dt[:], ALU.mult, R, [r])
    o_act(p, mag[:], mag[:], AF.Exp, R, [r])
    o_tt(p, "dve", f0[:], li, dt[:], ALU.mult, R, [r])
    o_ts(p, "dve", f0[:], f0[:], 1.0 / TWO_PI, None, ALU.mult, None, R, [r])
    ti = p.sbuf(shape, mybir.dt.int32, "s5ti")
    frac_wrap(p, f0[:], ti[:], f0c[:], R, r)
    sincos_turns(p, sn[:], cs[:], f0[:], f0c[:], pi_ap, R, r, r, r)
    o_tt(p, "dve", are[:], mag[:], cs[:], ALU.mult, R, [r])
    o_tt(p, "dve", aim[:], mag[:], sn[:], ALU.mult, R, [r])
    o_tt(p, "dve", den[:], lr, lr, ALU.mult, R, [r])
    o_tt(p, "dve", t1[:], li, li, ALU.mult, R, [r])
    o_tt(p, "dve", den[:], den[:], t1[:], ALU.add, R, [r])
    p.op("dve", lambda e: e.reciprocal(out=den[:], in_=den[:]), reads=R, writes=[r])
    o_ts(p, "dve", t1[:], are[:], -1.0, None, ALU.add, None, R, [r])
    o_tt(p, "dve", gre[:], t1[:], lr, ALU.mult, R, [r])
    o_tt(p, "dve", t2[:], aim[:], li, ALU.mult, R, [r])
    o_tt(p, "dve", gre[:], gre[:], t2[:], ALU.add, R, [r])
    o_tt(p, "dve", gre[:], gre[:], den[:], ALU.mult, R, [r])
    o_tt(p, "dve", gim[:], aim[:], lr, ALU.mult, R, [r])
    o_tt(p, "dve", t2[:], t1[:], li, ALU.mult, R, [r])
    o_tt(p, "dve", gim[:], gim[:], t2[:], ALU.subtract, R, [r])
    o_tt(p, "dve", gim[:], gim[:], den[:], ALU.mult, R, [r])
    o_ts(p, "dve", f1[:], f0[:], 64.0, None, ALU.mult, None, R, [r])
    frac_wrap(p, f1[:], ti[:], den[:], R, r)
    return dict(mag=mag, f0=f0, f1=f1, gre=gre, gim=gim), r


def emit_s5(p, uT, lamP, lamR, bmat, cmat, dsk, yT, tlen):
    nc = p.nc
    ntb = tlen // 512
    pi_t = p.sbuf([128, 1], F32, "pi")
    pi_r = Res()
    p.op("pool", lambda e: e.memset(pi_t[:], 1.5707963267948966), writes=[pi_r])
    ch0 = p.chan()
    lp = p.sbuf([128, 3, 8], F32, "lamP")
    lp_r = Res()
    p.dma("sp", p.chan(), lp[:], lamP, writes=[lp_r])
    lrw = p.sbuf([32, 3, 1024], F32, "lamR")
    lrw_r = Res()
    p.dma("sp", p.chan(), lrw[:], lamR, writes=[lrw_r])
    p.op("dve", lambda e: e.tensor_copy(out=lp[:, 0, 0:1], in_=lp[:, 0, 0:1]), reads=[pi_r, lp_r], writes=[lp_r])
    P, Pr = s5_pre(p, lp[:, 0, :], lp[:, 1, :], lp[:, 2, :], [128, 8], pi_t[:, 0:1], lp_r)
    p.op("dve", lambda e: e.tensor_copy(out=lrw[:, 0, 0:1], in_=lrw[:, 0, 0:1]), reads=[pi_r, lrw_r], writes=[lrw_r])
    Rw, Rr = s5_pre(p, lrw[:, 0, :], lrw[:, 1, :], lrw[:, 2, :], [32, 1024], pi_t[0:32, 0:1], lrw_r)
    bsb = p.sbuf([32, 2, 1024], F32, "bsb")
    b_r = Res()
    p.dma("sp", p.chan(), bsb[:], bmat, writes=[b_r])
    BT = p.sbuf([32, 2, 1024], F32, "BT")
    BT_r = Res()
    tb1 = p.sbuf([32, 1024], F32, "tb1")
    tb1_r = Res()
    o_tt(p, "dve", tb1[:], bsb[:, 1, :], Rw["gim"][:], ALU.mult, [b_r, Rr], [tb1_r])
    o_tt(p, "dve", BT[:, 0, :], bsb[:, 0, :], Rw["gre"][:], ALU.mult, [b_r, Rr], [BT_r])
    o_tt(p, "dve", BT[:, 0, :], BT[:, 0, :], tb1[:], ALU.subtract, [BT_r, tb1_r], [BT_r])
    o_tt(p, "dve", tb1[:], bsb[:, 1, :], Rw["gre"][:], ALU.mult, [b_r, Rr, BT_r], [tb1_r])
    o_tt(p, "dve", BT[:, 1, :], bsb[:, 0, :], Rw["gim"][:], ALU.mult, [b_r, Rr], [BT_r])
    o_tt(p, "dve", BT[:, 1, :], BT[:, 1, :], tb1[:], ALU.add, [BT_r, tb1_r], [BT_r])
    csb = p.sbuf([128, 2, 8, 32], F32, "csb")
    c_r = Res()
    p.dma("sp", p.chan(), csb[:], cmat, writes=[c_r])
    o_ts(p, "dve", csb[:, 1, :, :], csb[:, 1, :, :], -1.0, None, ALU.mult, None, [c_r], [c_r])
    dsb = p.sbuf([32, 8], F32, "dsb")
    d_r = Res()
    p.dma("sp", p.chan(), dsb[:], dsk, writes=[d_r])
    ia = p.sbuf([128, 512], mybir.dt.int32, "ia")
    ib = p.sbuf([128, 512], mybir.dt.int32, "ib")
    A0 = p.sbuf([128, 512], F32, "A0")
    B0 = p.sbuf([128, 512], F32, "B0")
    io_r = Res()
    p.op("pool", lambda e: e.iota(ia[:], [[1, 8], [0, 64]], base=0, channel_multiplier=0), writes=[io_r])
    p.op("pool", lambda e: e.iota(ib[:], [[0, 8], [1, 64]], base=0, channel_multiplier=0), writes=[io_r])
    p.op("dve", lambda e: e.tensor_copy(out=A0[:], in_=ia[:]), reads=[io_r], writes=[io_r])
    p.op("dve", lambda e: e.tensor_copy(out=B0[:], in_=ib[:]), reads=[io_r], writes=[io_r])
    ones = p.sbuf([128, 512], F32, "ones5")
    p.op("pool", lambda e: e.memset(ones[:], 1.0), writes=[io_r])

    usrc = uT.rearrange("(m q) n -> q m n", q=32)
    ydst = yT.rearrange("(m q) n -> q m n", q=32)
    upool = TPool(p, [32, tlen], F32, 2, "u")
    uch = [p.chan(), p.chan()]
    rho_p = TPool(p, [128, 512], F32, 2, "rho")
    psb = TPool(p, [128, 512], F32, 4, "psbu", psum=True)
    psy = TPool(p, [32, 512], F32, 2, "psy", psum=True)
    tp = TPool(p, [128, 512], F32, 8 * S5_G + 2, "s5t")
    tip = TPool(p, [128, 512], mybir.dt.int32, S5_G + 1, "s5ti")
    zp = TPool(p, [128, 512], F32, 4 * S5_G + 2, "s5z")
    t32 = TPool(p, [32, 512], F32, 3 * S5_G + 2, "s5o")
    och = [p.chan() for _ in range(6)]
    def mloop(m):
        ui = upool.i
        u, u_r = upool.get()
        p.dma("sp", uch[ui], u[:], usrc[:, m, :], writes=[u_r])
        rho, rho_r = rho_p.get()
        o_ts(p, "dve", rho[:], ones[:], P["mag"][:, m:m + 1], None, ALU.mult, None, [io_r, Pr], [rho_r])
        zprev = None
        for tb in range(ntb):
            sl = slice(tb * 512, (tb + 1) * 512)
            pre, prer = psb.get()
            pim, pimr = psb.get()
            p.op("pe", lambda e: e.matmul(pre[:], BT[:, 0, m * 128:(m + 1) * 128], u[:, sl], start=True, stop=True),
                 reads=[BT_r, u_r], writes=[prer])
            p.op("pe", lambda e: e.matmul(pim[:], BT[:, 1, m * 128:(m + 1) * 128], u[:, sl], start=True, stop=True),
                 reads=[BT_r, u_r], writes=[pimr])
            bre, brer = tp.get()
            bim, bimr = tp.get()
            p.op("act", lambda e: e.copy(out=bre[:], in_=pre[:]), reads=[prer], writes=[brer])
            p.op("act", lambda e: e.copy(out=bim[:], in_=pim[:]), reads=[pimr], writes=[bimr])
            ph, phr = tp.get()
            phc, phcr = tp.get()
            o_ts(p, "dve", ph[:], A0[:], float(8 * tb), P["f1"][:, m:m + 1], ALU.add, ALU.mult, [io_r, Pr], [phr])
            o_stt(p, ph[:], B0[:], P["f0"][:, m:m + 1], ph[:], ALU.mult, ALU.add, [io_r, Pr, phr], [phr])
            ti, tir = tip.get()
            frac_wrap(p, ph[:], ti[:], phc[:], [phr, phcr, tir], phr)
            S, Sr = tp.get()
            C, Cr = tp.get()
            sincos_turns(p, S[:], C[:], ph[:], phc[:], pi_t[:, 0:1], [phr, pi_r], Sr, Cr, phcr)
            yield
            t1, t1r = tp.get()
            t2, t2r = tp.get()
            o_tt(p, "pool", t1[:], S[:], bim[:], ALU.mult, [Sr, bimr], [t1r])
            o_tt(p, "dve", t2[:], S[:], bre[:], ALU.mult, [Sr, brer], [t2r])
            o_tt(p, "pool", bre[:], C[:], bre[:], ALU.mult, [Cr, brer, t2r], [brer])
            o_tt(p, "dve", bim[:], C[:], bim[:], ALU.mult, [Cr, bimr, t1r], [bimr])
            o_tt(p, "pool", bre[:], bre[:], t1[:], ALU.add, [brer, t1r], [brer])
            o_tt(p, "dve", bim[:], bim[:], t2[:], ALU.subtract, [bimr, t2r], [bimr])
            yield
            zre, zrer = zp.get()
            zim, zimr = zp.get()
            for (z, zr, w_, wr_, k) in ((zre, zrer, bre, brer, 0), (zim, zimr, bim, bimr, 1)):
                if zprev is None:
                    p.op("dve", lambda e, z=z, w_=w_: e.tensor_tensor_scan(
                        out=z[:], data0=rho[:], data1=w_[:], initial=0.0, op0=ALU.mult, op1=ALU.add),
                        reads=[rho_r, wr_], writes=[zr])
                else:
                    pz, pzr = zprev[k]
                    p.op("dve", lambda e, z=z, w_=w_, pz=pz: e.tensor_tensor_scan(
                        out=z[:], data0=rho[:], data1=w_[:], initial=pz[:, 511:512], op0=ALU.mult, op1=ALU.add),
                        reads=[rho_r, wr_, pzr], writes=[zr])
            zprev = ((zre, zrer), (zim, zimr))
            yield
            o_tt(p, "pool", t1[:], S[:], zim[:], ALU.mult, [Sr, zimr, t1r], [t1r])
            o_tt(p, "dve", t2[:], S[:], zre[:], ALU.mult, [Sr, zrer, t2r], [t2r])
            o_tt(p, "pool", bre[:], C[:], zre[:], ALU.mult, [Cr, zrer, brer], [brer])
            o_tt(p, "dve", bim[:], C[:], zim[:], ALU.mult, [Cr, zimr, bimr], [bimr])
            o_tt(p, "pool", bre[:], bre[:], t1[:], ALU.subtract, [brer, t1r], [brer])
            o_tt(p, "dve", bim[:], bim[:], t2[:], ALU.add, [bimr, t2r], [bimr])
            py, pyr = psy.get()
            p.op("pe", lambda e: e.matmul(py[:], csb[:, 0, m, :], bre[:], start=True, stop=False),
                 reads=[c_r, brer], writes=[pyr])
            p.op("pe", lambda e: e.matmul(py[:], csb[:, 1, m, :], bim[:], start=False, stop=True),
                 reads=[c_r, bimr], writes=[pyr])
            y2, y2r = t32.get()
            o_stt(p, y2[:], u[:, sl], dsb[:, m:m + 1], py[:], ALU.mult, ALU.add, [u_r, d_r, pyr], [y2r])
            yo, yor = t32.get()
            gelu_tanh(p, t32, yo[:], y2[:], y2r, yor, eng="pool")
            oi = ocnt[0] % len(och)
            ocnt[0] += 1
            p.dma("sp", och[oi], ydst[:, m, sl], yo[:], reads=[yor], is_output=True)
            yield

    ocnt = [0]
    for m0 in range(0, 8, S5_G):
        gens = [mloop(m) for m in range(m0, min(8, m0 + S5_G))]
        while gens:
            for g in list(gens):
                try:
                    next(g)
                except StopIteration:
                    gens.remove(g)


def build_s5(tlen=T):
    nc = bass.Bass("TRN2", target_bir_lowering=False)
    uT = nc.dram_tensor("uT", [256, tlen], F32, kind="ExternalInput").ap()
    lamP = nc.dram_tensor("lamP", [128, 3, 8], F32, kind="ExternalInput").ap()
    lamR = nc.dram_tensor("lamR", [32, 3, 1024], F32, kind="ExternalInput").ap()
    bmat = nc.dram_tensor("bmat", [32, 2, 1024], F32, kind="ExternalInput").ap()
    cmat = nc.dram_tensor("cmat", [128, 2, 8, 32], F32, kind="ExternalInput").ap()
    dsk = nc.dram_tensor("dsk", [32, 8], F32, kind="ExternalInput").ap()
    yT = nc.dram_tensor("yT", [256, tlen], F32, kind="ExternalOutput").ap()
    p = Prog(nc)
    emit_s5(p, uT, lamP, lamR, bmat, cmat, dsk, yT, tlen)
    p.finish()
    return nc


def s5_host_layout(lam_re, lam_im, log_dt, b_re, b_im, c_re, c_im, d_skip, q):
    g0 = 16 * q
    lr = lam_re[g0:g0 + 16]
    li = lam_im[g0:g0 + 16]
    ld = np.broadcast_to(log_dt[g0:g0 + 16, None], (16, 64))
    st = np.stack([lr, li, ld], 0).reshape(3, 8, 2, 64)
    lamP = np.ascontiguousarray(st.transpose(2, 3, 0, 1).reshape(128, 3, 8))
    lamR = np.ascontiguousarray(np.broadcast_to(st.reshape(1, 3, 1024), (32, 3, 1024)))
    bmat = np.zeros((32, 2, 8, 2, 64), np.float32)
    cmat = np.zeros((2, 64, 2, 8, 2, 16), np.float32)
    for k, (bb, cc) in enumerate(((b_re, c_re), (b_im, c_im))):
        bq = bb[g0:g0 + 16].reshape(8, 2, 64, 16)
        cq = cc[g0:g0 + 16].reshape(8, 2, 16, 64)
        for gl in range(2):
            bmat[gl * 16:(gl + 1) * 16, k, :, gl, :] = bq[:, gl].transpose(2, 0, 1)
            cmat[gl, :, k, :, gl, :] = cq[:, gl].transpose(2, 0, 1)
    bmat = bmat.reshape(32, 2, 1024)
    cmat = cmat.reshape(128, 2, 8, 32)
    dsk = np.ascontiguousarray(d_skip[256 * q:256 * (q + 1)].reshape(8, 32).T)
    return dict(lamP=lamP, lamR=lamR, bmat=bmat, cmat=cmat, dsk=dsk)


RTB = 128
S5_G = 2
RW1_G = 2
RW2_G = 3
_DBG = [99]


def build_rw1(tlen=T):
    nc = bass.Bass("TRN2", target_bir_lowering=False)
    di = lambda n, sh: nc.dram_tensor(n, sh, F32, kind="ExternalInput").ap()
    do = lambda n, sh: nc.dram_tensor(n, sh, F32, kind="ExternalOutput").ap()
    zrkv = di("zrkv", [64, 3, 4, tlen + 1])
    zw = di("zw", [64, tlen + 1])
    za = di("za", [64, tlen + 1])
    zg0 = di("zg0", [128, tlen + 1])
    zg1 = di("zg1", [128, tlen + 1])
    mu_rkv = di("mu_rkv", [64, 3, 4, RTB])
    mu_l = di("mu_l", [128, 4])
    w2 = di("w2", [64, 256])
    a2 = di("a2", [64, 256])
    g2a = di("g2a", [128, 256])
    g2b = di("g2b", [128, 256])
    cvec = di("cvec", [64, 5, 4, RTB])
    outs = {n: do(n, [64, 4, tlen]) for n in ("at", "rt", "bt", "kt", "xv", "gg", "bs", "ec")}
    p = Prog(nc)
    ch0 = p.chan()
    N = 4 * RTB

    def const(ap, shape, nm):
        t = p.sbuf(shape, F32, nm)
        r = Res()
        p.dma("sp", p.chan(), t[:], ap, writes=[r])
        return t, r
    mu_t, mu_r = const(mu_rkv, [64, 3, 4, RTB], "mu")
    mul_t, mul_r = const(mu_l, [128, 4], "mul")
    w2_t, w2_r = const(w2, [64, 256], "w2")
    a2_t, a2_r = const(a2, [64, 256], "a2")
    g2a_t, g2a_r = const(g2a, [128, 256], "g2a")
    g2b_t, g2b_r = const(g2b, [128, 256], "g2b")
    cv_t, cv_r = const(cvec, [64, 5, 4, RTB], "cv")
    cst = p.sbuf([128, 4], F32, "cst")
    cst_r = Res()
    p.op("pool", lambda e: e.memset(cst[:, 0:1], 1.0), writes=[cst_r])
    p.op("pool", lambda e: e.memset(cst[:, 1:2], -0.5), writes=[cst_r])
    ones_in = di("ones_in", [64, 256])
    ones_f32, ones_r = const(ones_in, [64, 256], "ones_f32")
    ones = ones_f32[:, 0:64]
    cmask = p.sbuf([64, RTB], F32, "cmask")
    p.op("pool", lambda e: e.memset(cmask[:], 1.0), writes=[cst_r])
    for c in range(RTB // 64):
        p.op("pool", lambda e, c=c: e.memset(cmask[:, c * 64:c * 64 + 1], 0.0), writes=[cst_r])

    zin = TPool(p, [64, 3, 4, RTB + 1], F32, 2 * RW1_G, "zin")
    zch = [p.chan() for _ in range(2 * RW1_G)]
    lin = TPool(p, [128, 4, RTB + 1], F32, 2 * RW1_G, "lin")
    lch = [p.chan() for _ in range(2 * RW1_G)]
    tp = TPool(p, [64, 4, RTB], F32, 21 * RW1_G + 2, "r1t")
    tl = TPool(p, [128, RTB], F32, 4 * RW1_G + 4, "r1l")
    ps = TPool(p, [64, 4, RTB], F32, 6, "r1ps", psum=True)
    psg = TPool(p, [64, 4, RTB], F32, 2, "r1pg", psum=True)
    och = [p.chan() for _ in range(16)]

    ocnt = [0]

    def emit_out(name, t, r, t0):
        i = ocnt[0] % len(och)
        ocnt[0] += 1
        p.dma("sp", och[i], outs[name][:, :, t0:t0 + RTB], t, reads=[r], is_output=True)

    def block(blk):
        t0 = blk * RTB
        zi = zin.i
        z, z_r = zin.get()
        p.dma("sp", zch[zi], z[:], zrkv[:, :, :, t0:t0 + RTB + 1], writes=[z_r])
        li_ = lin.i
        l, l_r = lin.get()
        p.dma("sp", lch[li_], l[0:64, 0, :], zw[:, t0:t0 + RTB + 1], writes=[l_r])
        p.dma("sp", lch[li_], l[0:64, 1, :], za[:, t0:t0 + RTB + 1], writes=[l_r])
        p.dma("sp", lch[li_], l[:, 2, :], zg0[:, t0:t0 + RTB + 1], writes=[l_r])
        p.dma("sp", lch[li_], l[:, 3, :], zg1[:, t0:t0 + RTB + 1], writes=[l_r])
        xs = []
        for j in range(3):
            x, x_r = tp.get()
            o_tt(p, "pool", x[:], z[:, j, :, 0:RTB], z[:, j, :, 1:RTB + 1], ALU.subtract, [z_r], [x_r])
            o_tt(p, "pool", x[:], x[:], mu_t[:, j, :, :], ALU.mult, [x_r, mu_r], [x_r])
            o_tt(p, "pool", x[:], x[:], z[:, j, :, 1:RTB + 1], ALU.add, [x_r, z_r], [x_r])
            xs.append((x, x_r))
        (xr, xr_r), (xk, xk_r), (xv, xv_r) = xs
        yield
        ls = []
        for j, np_ in ((0, 64), (1, 64), (2, 128), (3, 128)):
            x, x_r = tl.get()
            o_tt(p, "dve", x[0:np_, :], l[0:np_, j, 0:RTB], l[0:np_, j, 1:RTB + 1], ALU.subtract, [l_r], [x_r])
            o_stt(p, x[0:np_, :], x[0:np_, :], mul_t[0:np_, j:j + 1], l[0:np_, j, 1:RTB + 1], ALU.mult, ALU.add,
                  [x_r, l_r, mul_r], [x_r])
            ls.append((x, x_r, np_))
        xw, xw_r, _ = ls[0]
        xa, xa_r, _ = ls[1]
        o_act(p, xw[0:64, :], xw[0:64, :], AF.Tanh, [xw_r], [xw_r])
        for (x, x_r, np_) in ls[2:]:
            o_act(p, x[0:np_, :], x[0:np_, :], AF.Sigmoid, [x_r], [x_r])
        yield
        pw, pw_r = ps.get()
        pa, pa_r = ps.get()
        pg, pg_r = psg.get()
        for h in range(4):
            hs = slice(h * 64, (h + 1) * 64)
            p.op("pe", lambda e, h=h, hs=hs: e.matmul(pw[:, h, :], w2_t[:, hs], xw[0:64, :], start=True, stop=True),
                 reads=[w2_r, xw_r], writes=[pw_r], same_bank_cont=(h > 0))
            p.op("pe", lambda e, h=h, hs=hs: e.matmul(pa[:, h, :], a2_t[:, hs], xa[0:64, :], start=True, stop=True),
                 reads=[a2_r, xa_r], writes=[pa_r], same_bank_cont=(h > 0))
            p.op("pe", lambda e, h=h, hs=hs: e.matmul(pg[:, h, :], g2a_t[:, hs], ls[2][0][:, :], start=True, stop=False),
                 reads=[g2a_r, ls[2][1]], writes=[pg_r], same_bank_cont=(h > 0))
            p.op("pe", lambda e, h=h, hs=hs: e.matmul(pg[:, h, :], g2b_t[:, hs], ls[3][0][:, :], start=False, stop=True),
                 reads=[g2b_r, ls[3][1]], writes=[pg_r])
        yield
        e2, e2_r = tp.get()
        o_tt(p, "dve", e2[:], pw[:], cv_t[:, 0, :, :], ALU.add, [pw_r, cv_r], [e2_r])
        o_act(p, e2[:], e2[:], AF.Exp, [e2_r], [e2_r], scale=-1.0)
        o_act(p, e2[:], e2[:], AF.Ln, [e2_r, cst_r], [e2_r], bias=cst[0:64, 0:1])
        o_act(p, e2[:], e2[:], AF.Exp, [e2_r, cst_r], [e2_r], scale=-1.0, bias=cst[0:64, 1:2])
        a_, a_r = tp.get()
        o_tt(p, "dve", a_[:], pa[:], cv_t[:, 1, :, :], ALU.add, [pa_r, cv_r], [a_r])
        o_act(p, a_[:], a_[:], AF.Sigmoid, [a_r], [a_r])
        gg, gg_r = tp.get()
        p.op("act", lambda e, gg=gg, pg=pg: e.copy(out=gg[:], in_=pg[:]), reads=[pg_r], writes=[gg_r])
        yield
        kk, kk_r = tp.get()
        sq, sq_r = tp.get()
        o_tt(p, "dve", kk[:], xk[:], cv_t[:, 2, :, :], ALU.mult, [xk_r, cv_r], [kk_r])
        o_tt(p, "dve", sq[:], kk[:], kk[:], ALU.mult, [kk_r], [sq_r])
        pss, pss_r = ps.get()
        for h in range(4):
            p.op("pe", lambda e, pss=pss, sq=sq, h=h: e.matmul(pss[:, h, :], ones, sq[:, h, :], start=True, stop=True),
                 reads=[ones_r, sq_r], writes=[pss_r], same_bank_cont=(h > 0))
        o_ts(p, "dve", sq[:], pss[:], 1.0, 1e-24, ALU.mult, ALU.max, [pss_r, sq_r], [sq_r])
        o_act(p, sq[:], sq[:], AF.Sqrt, [sq_r], [sq_r])
        p.op("dve", lambda e, sq=sq: e.reciprocal(out=sq[:], in_=sq[:]), reads=[sq_r], writes=[sq_r])
        o_tt(p, "dve", kk[:], kk[:], sq[:], ALU.mult, [kk_r, sq_r], [kk_r])
        yield
        km, km_r = tp.get()
        o_ts(p, "pool", km[:], a_[:], -1.0, None, ALU.add, None, [a_r], [km_r])
        o_tt(p, "pool", km[:], km[:], cv_t[:, 3, :, :], ALU.mult, [km_r, cv_r], [km_r])
        o_stt(p, km[:], km[:], 1.0, xk[:], ALU.add, ALU.mult, [km_r, xk_r], [km_r])
        vb, vb_r = tp.get()
        o_tt(p, "pool", vb[:], kk[:], a_[:], ALU.mult, [kk_r, a_r], [vb_r])
        rk, rk_r = tp.get()
        o_tt(p, "pool", rk[:], xr[:], km[:], ALU.mult, [xr_r, km_r], [rk_r])
        o_tt(p, "pool", rk[:], rk[:], cv_t[:, 4, :, :], ALU.mult, [rk_r, cv_r], [rk_r])
        pb, pb_r = ps.get()
        for h in range(4):
            p.op("pe", lambda e, pb=pb, rk=rk, h=h: e.matmul(pb[:, h, :], ones, rk[:, h, :], start=True, stop=True),
                 reads=[ones_r, rk_r], writes=[pb_r], same_bank_cont=(h > 0))
        bsv, bsv_r = tp.get()
        p.op("act", lambda e, bsv=bsv, pb=pb: e.copy(out=bsv[:], in_=pb[:]), reads=[pb_r], writes=[bsv_r])
        yield
        cu, cu_r = tp.get()
        for h in range(4):
            p.op("dve", lambda e, h=h, cu=cu, e2=e2: e.tensor_tensor_scan(
                out=cu[:, h, :], data0=cmask[:], data1=e2[:, h, :], initial=0.0, op0=ALU.mult, op1=ALU.add),
                reads=[cst_r, e2_r], writes=[cu_r])
        cex, cex_r = tp.get()
        o_tt(p, "pool", cex[:], cu[:], e2[:], ALU.subtract, [cu_r, e2_r], [cex_r])
        ec, ec_r = tp.get()
        en, en_r = tp.get()
        o_act(p, ec[:], cu[:], AF.Exp, [cu_r], [ec_r], scale=-1.0)
        o_act(p, en[:], cu[:], AF.Exp, [cu_r], [en_r])
        o_act(p, cex[:], cex[:], AF.Exp, [cex_r], [cex_r], scale=-1.0)
        yield
        at, at_r = tp.get()
        o_stt(p, at[:], kk[:], -1.0, cex[:], ALU.mult, ALU.mult, [kk_r, cex_r], [at_r])
        rt, rt_r = tp.get()
        o_tt(p, "dve", rt[:], xr[:], ec[:], ALU.mult, [xr_r, ec_r], [rt_r])
        bt, bt_r = tp.get()
        o_tt(p, "pool", bt[:], vb[:], en[:], ALU.mult, [vb_r, en_r], [bt_r])
        kt, kt_r = tp.get()
        o_tt(p, "dve", kt[:], km[:], en[:], ALU.mult, [km_r, en_r], [kt_r])
        for nm, (t_, r_) in (("at", (at, at_r)), ("rt", (rt, rt_r)), ("bt", (bt, bt_r)), ("kt", (kt, kt_r)),
                             ("xv", (xv, xv_r)), ("gg", (gg, gg_r)), ("bs", (bsv, bsv_r)), ("ec", (ec, ec_r))):
            emit_out(nm, t_[:], r_, t0)
    nblk = tlen // RTB
    for b0 in range(0, nblk, RW1_G):
        gens = [block(b_) for b_ in range(b0, min(nblk, b0 + RW1_G))]
        while gens:
            for g in list(gens):
                try:
                    next(g)
                except StopIteration:
                    gens.remove(g)
    p.finish()
    return nc


def build_rw2(tlen=T):
    nc = bass.Bass("TRN2", target_bir_lowering=False)
    nch = tlen // 64
    di = lambda n, sh: nc.dram_tensor(n, sh, F32, kind="ExternalInput").ap()
    AR = di("AR", [64, nch, 4, 2, 64])
    BK = di("BK", [64, nch, 4, 2, 64])
    TM = di("TM", [64, nch, 4, 5, 64])
    BS = di("BS", [64, nch, 4])
    GL = di("GL", [64, nch, 4])
    LN = di("LN", [64, 2, 4, 64])
    yo = nc.dram_tensor("yo", [64, nch, 4, 64], F32, kind="ExternalOutput").ap()
    p = Prog(nc)
    ch0 = p.chan()
    ln_t = p.sbuf([64, 2, 4, 64], F32, "ln")
    ln_r = Res()
    p.dma("sp", p.chan(), ln_t[:], LN, writes=[ln_r])
    gl_t = p.sbuf([64, nch, 4], F32, "gl")
    gl_r = Res()
    p.dma("sp", p.chan(), gl_t[:], GL, writes=[gl_r])
    bs_t = p.sbuf([64, nch, 4], F32, "bs")
    bs_r = Res()
    p.dma("sp", p.chan(), bs_t[:], BS, writes=[bs_r])
    ii = p.sbuf([64, 64], mybir.dt.int32, "ii")
    dif = p.sbuf([64, 64], F32, "dif")
    mk_r = Res()
    p.op("pool", lambda e: e.iota(ii[:], [[1, 64]], base=0, channel_multiplier=-1), writes=[mk_r])
    p.op("dve", lambda e: e.tensor_copy(out=dif[:], in_=ii[:]), reads=[mk_r], writes=[mk_r])
    msk = p.sbuf([64, 4, 4, 64], F32, "msk")
    mlow = p.sbuf([64, 4, 64], F32, "mlow")
    ident = p.sbuf([64, 4, 64], F32, "ident")
    for h in range(4):
        for q in range(4):
            op = ALU.is_gt if q % 2 == 0 else ALU.is_ge
            o_ts(p, "dve", msk[:, h, q, :], dif[:], 0.0, None, op, None, [mk_r], [mk_r])
        o_ts(p, "dve", mlow[:, h, :], dif[:], 0.0, None, ALU.is_lt, None, [mk_r], [mk_r])
        o_ts(p, "dve", ident[:, h, :], dif[:], 0.0, None, ALU.is_equal, None, [mk_r], [mk_r])

    G = RW2_G
    arp = TPool(p, [64, 4, 2, 64], F32, 2 * G, "ar")
    bkp = TPool(p, [64, 4, 2, 64], F32, 2 * G, "bk")
    tmp_ = TPool(p, [64, 4, 5, 64], F32, 2 * G, "tm")
    inch = [p.chan() for _ in range(2 * G)]
    inch2 = [p.chan() for _ in range(2 * G)]
    inch3 = [p.chan() for _ in range(2 * G)]
    psA = TPool(p, [64, 4, 4, 64], F32, 1, "psA", psum=True)
    ps4 = TPool(p, [64, 4, 64], F32, 5, "ps4", psum=True)
    psX = TPool(p, [64, 4, 128], F32, 1, "psX", psum=True)
    sb4 = TPool(p, [64, 4, 64], F32, 12 * G + 8, "sb4")
    keep = TPool(p, [64, 4, 64], F32, 3 * 2 * G, "keep")
    AT_p = TPool(p, [64, 4, 4, 64], F32, 2 * G, "AT")
    X_p = TPool(p, [64, 4, 128], F32, G + 1, "X")
    WU_p = TPool(p, [64, 4, 128], F32, 2 * G, "WU")
    S_p = TPool(p, [64, 4, 64], F32, 3, "S")
    st4 = TPool(p, [64, 4], F32, 8, "st4")
    outp = TPool(p, [64, 4, 64], F32, 3, "yo")
    och = [p.chan() for _ in range(3)]
    S0, S0_r = S_p.get()
    p.op("pool", lambda e: e.memset(S0[:], 0.0), writes=[S0_r])
    state = [S0, S0_r]

    def mm4(ps, ps_r, lhs_fn, rhs_fn, reads):
        mm4g(ps, ps_r, [(lhs_fn, rhs_fn, reads)])

    def mm4g(ps, ps_r, terms):
        for h in range(4):
            for ti, (lhs_fn, rhs_fn, reads) in enumerate(terms):
                p.op("pe", lambda e, h=h, lhs_fn=lhs_fn, rhs_fn=rhs_fn, ti=ti: e.matmul(
                    ps[:, h, :], lhs_fn(h), rhs_fn(h), start=(ti == 0), stop=(ti == len(terms) - 1)),
                    reads=reads, writes=[ps_r], same_bank_cont=(h > 0))

    def pre_phase(n, ctx):
        i_in = arp.i
        ar, ar_r = arp.get()
        bk, bk_r = bkp.get()
        tm, tm_r = tmp_.get()
        p.dma("sp", inch[i_in], ar[:], AR[:, n], writes=[ar_r])
        p.dma("sp", inch2[i_in], bk[:], BK[:, n], writes=[bk_r])
        p.dma("sp", inch3[i_in], tm[:], TM[:, n], writes=[tm_r])
        yield
        pA, pA_r = psA.get()
        for h in range(4):
            p.op("pe", lambda e, h=h: e.matmul(pA[:, h, 0:2, :], bk[:, h, 0, :], ar[:, h, :, :], start=True, stop=True),
                 reads=[bk_r, ar_r], writes=[pA_r], same_bank_cont=(h % 2 == 1))
            p.op("pe", lambda e, h=h: e.matmul(pA[:, h, 2:4, :], bk[:, h, 1, :], ar[:, h, :, :], start=True, stop=True),
                 reads=[bk_r, ar_r], writes=[pA_r], same_bank_cont=True)
        AT, AT_r = AT_p.get()
        o_tt(p, "dve", AT[:], pA[:], msk[:], ALU.mult, [pA_r, mk_r], [AT_r])
        pN, pN_r = ps4.get()
        mm4(pN, pN_r, lambda h: ar[:, h, 0, :], lambda h: bk[:, h, 0, :], [ar_r, bk_r])
        PT, PT_r = sb4.get()
        o_tt(p, "dve", PT[:], pN[:], mlow[:], ALU.mult, [pN_r, mk_r], [PT_r])
        yield
        P_, P_r = sb4.get()
        p.op("act", lambda e: e.copy(out=P_[:], in_=AT[:, :, 0, :]), reads=[AT_r], writes=[P_r])
        M, M_r = sb4.get()
        o_tt(p, "pool", M[:], AT[:, :, 0, :], ident[:], ALU.add, [AT_r, mk_r], [M_r])
        pv, pv_r = ps4.get()
        mm4(pv, pv_r, lambda h: AT[:, h, 2, :], lambda h: tm[:, h, 3, :], [AT_r, tm_r])
        X, X_r = X_p.get()
        p.op("act", lambda e: e.copy(out=X[:, :, 64:128], in_=pv[:]), reads=[pv_r], writes=[X_r])
        p.op("pool", lambda e: e.tensor_copy(out=X[:, :, 0:64], in_=tm[:, :, 0, :]), reads=[tm_r], writes=[X_r])
        yield
        for lev in range(1, 6):
            pq, pq_r = ps4.get()
            mm4(pq, pq_r, lambda h, P_=P_: P_[:, h, :], lambda h, PT=PT: PT[:, h, :], [P_r, PT_r])
            PT2, PT2_r = sb4.get()
            p.op("act", lambda e, PT2=PT2, pq=pq: e.copy(out=PT2[:], in_=pq[:]), reads=[pq_r], writes=[PT2_r])
            if lev < 5:
                pq2, pq2_r = ps4.get()
                mm4(pq2, pq2_r, lambda h, PT=PT: PT[:, h, :], lambda h, P_=P_: P_[:, h, :], [P_r, PT_r])
                P2, P2_r = sb4.get()
                p.op("dve", lambda e, P2=P2, pq2=pq2: e.tensor_copy(out=P2[:], in_=pq2[:]), reads=[pq2_r], writes=[P2_r])
            yield
            pm, pm_r = ps4.get()
            mm4(pm, pm_r, lambda h, PT2=PT2: PT2[:, h, :], lambda h, M=M: M[:, h, :], [PT2_r, M_r])
            M2, M2_r = sb4.get()
            o_tt(p, "dve", M2[:], pm[:], M[:], ALU.add, [pm_r, M_r], [M2_r])
            M, M_r = M2, M2_r
            PT, PT_r = PT2, PT2_r
            if lev < 5:
                P_, P_r = P2, P2_r
            yield
        pX, pX_r = psX.get()
        mm4(pX, pX_r, lambda h: M[:, h, :], lambda h: X[:, h, :], [M_r, X_r])
        WU, WU_r = WU_p.get()
        p.op("dve", lambda e: e.tensor_copy(out=WU[:], in_=pX[:]), reads=[pX_r], writes=[WU_r])
        yield
        pG, pG_r = ps4.get()
        mm4(pG, pG_r, lambda h: WU[:, h, 0:64], lambda h: tm[:, h, 1, :], [WU_r, tm_r])
        GT, GT_r = keep.get()
        o_tt(p, "dve", GT[:], pG[:], ident[:], ALU.add, [pG_r, mk_r], [GT_r])
        pH, pH_r = ps4.get()
        mm4g(pH, pH_r, [(lambda h: tm[:, h, 1, :], lambda h: WU[:, h, 64:128], [WU_r, tm_r]),
                        (lambda h: tm[:, h, 2, :], lambda h: tm[:, h, 3, :], [tm_r])])
        HG, HG_r = keep.get()
        o_tt(p, "dve", HG[:], pH[:], gl_t[:, n, :].unsqueeze(2).to_broadcast([64, 4, 64]), ALU.mult, [pH_r, gl_r], [HG_r])
        pQ, pQ_r = ps4.get()
        mm4(pQ, pQ_r, lambda h: WU[:, h, 0:64], lambda h: AT[:, h, 1, :], [WU_r, AT_r])
        QT, QT_r = keep.get()
        o_tt(p, "dve", QT[:], pQ[:], ar[:, :, 1, :], ALU.add, [pQ_r, ar_r], [QT_r])
        ctx.update(AT=AT, AT_r=AT_r, WU=WU, WU_r=WU_r, tm=tm, tm_r=tm_r, GT=GT, GT_r=GT_r, HG=HG, HG_r=HG_r,
                   QT=QT, QT_r=QT_r)

    def chain_phase(n, c):
        S, S_r = state
        AT, AT_r, WU, WU_r, tm, tm_r = c["AT"], c["AT_r"], c["WU"], c["WU_r"], c["tm"], c["tm_r"]
        GT, GT_r, HG, HG_r, QT, QT_r = c["GT"], c["GT_r"], c["HG"], c["HG_r"], c["QT"], c["QT_r"]
        pY, pY_r = ps4.get()
        mm4g(pY, pY_r, [(lambda h: AT[:, h, 1, :], lambda h: WU[:, h, 64:128], [AT_r, WU_r]),
                        (lambda h: AT[:, h, 3, :], lambda h: tm[:, h, 3, :], [AT_r, tm_r]),
                        (lambda h: QT[:, h, :], lambda h: S[:, h, :], [QT_r, S_r])])
        pS, pS_r = ps4.get()
        mm4(pS, pS_r, lambda h: GT[:, h, :], lambda h: S[:, h, :], [GT_r, S_r])
        S2, S2_r = S_p.get()
        sg, sg_r = sb4.get()
        o_tt(p, "dve", sg[:], pS[:], gl_t[:, n, :].unsqueeze(2).to_broadcast([64, 4, 64]), ALU.mult, [pS_r, gl_r], [sg_r])
        o_tt(p, "dve", S2[:], sg[:], HG[:], ALU.add, [sg_r, HG_r], [S2_r])
        state[0], state[1] = S2, S2_r
        y, y_r = sb4.get()
        p.op("act", lambda e: e.copy(out=y[:], in_=pY[:]), reads=[pY_r], writes=[y_r])
        s1, s1_r = st4.get()
        s2, s2_r = st4.get()
        ysq, ysq_r = sb4.get()
        p.op("dve", lambda e: e.reduce_sum(out=s1[:], in_=y[:], axis=AX.X), reads=[y_r], writes=[s1_r])
        o_tt(p, "pool", ysq[:], y[:], y[:], ALU.mult, [y_r], [ysq_r])
        p.op("dve", lambda e: e.reduce_sum(out=s2[:], in_=ysq[:], axis=AX.X), reads=[ysq_r], writes=[s2_r])
        o_ts(p, "dve", s1[:], s1[:], 1.0 / 64, None, ALU.mult, None, [s1_r], [s1_r])
        m2, m2_r = st4.get()
        o_tt(p, "dve", m2[:], s1[:], s1[:], ALU.mult, [s1_r], [m2_r])
        o_stt(p, s2[:], s2[:], 1.0 / 64, m2[:], ALU.mult, ALU.subtract, [s2_r, m2_r], [s2_r])
        o_ts(p, "dve", s2[:], s2[:], 64e-5, None, ALU.add, None, [s2_r], [s2_r])
        o_act(p, s2[:], s2[:], AF.Sqrt, [s2_r], [s2_r])
        p.op("dve", lambda e: e.reciprocal(out=s2[:], in_=s2[:]), reads=[s2_r], writes=[s2_r])
        yn, yn_r = sb4.get()
        o_tt(p, "dve", yn[:], y[:], s1[:].unsqueeze(2).to_broadcast([64, 4, 64]), ALU.subtract, [y_r, s1_r], [yn_r])
        o_tt(p, "dve", yn[:], yn[:], s2[:].unsqueeze(2).to_broadcast([64, 4, 64]), ALU.mult, [yn_r, s2_r], [yn_r])
        o_tt(p, "pool", yn[:], yn[:], ln_t[:, 0, :, :], ALU.mult, [yn_r, ln_r], [yn_r])
        o_tt(p, "pool", yn[:], yn[:], ln_t[:, 1, :, :], ALU.add, [yn_r, ln_r], [yn_r])
        bv, bv_r = sb4.get()
        o_tt(p, "dve", bv[:], tm[:, :, 3, :], bs_t[:, n, :].unsqueeze(2).to_broadcast([64, 4, 64]), ALU.mult, [tm_r, bs_r], [bv_r])
        o_tt(p, "pool", yn[:], yn[:], bv[:], ALU.add, [yn_r, bv_r], [yn_r])
        oi = outp.i
        o, o_r = outp.get()
        o_tt(p, "pool", o[:], yn[:], tm[:, :, 4, :], ALU.mult, [yn_r, tm_r], [o_r])
        p.dma("sp", och[oi], yo[:, n], o[:], reads=[o_r], is_output=True)

    for n0 in range(0, nch, G):
        ns = list(range(n0, min(nch, n0 + G)))
        ctxs = [dict() for _ in ns]
        gens = [pre_phase(n, c) for n, c in zip(ns, ctxs)]
        while gens:
            for g in list(gens):
                try:
                    next(g)
                except StopIteration:
                    gens.remove(g)
        for n, c in zip(ns, ctxs):
            chain_phase(n, c)
    p.finish()
    return nc


def rw1_host_layout(z, mu, w2, a2, g2, w0, a0, k_k, k_a, r_k, q):
    tl = z.shape[0]
    cs = slice(256 * q, 256 * (q + 1))
    zrkv = np.zeros((64, 3, 4, tl + 1), np.float32)
    mu_rkv = np.zeros((64, 3, 4, RTB), np.float32)
    for j in range(3):
        blk = z[:, j * 1024:(j + 1) * 1024][:, cs]
        zrkv[:, j, :, 1:] = blk.reshape(tl, 4, 64).transpose(2, 1, 0)
        mu_rkv[:, j, :, :] = mu[j * 1024:(j + 1) * 1024][cs].reshape(4, 64).T[:, :, None]

    def rows(c0, c1):
        o = np.zeros((c1 - c0, tl + 1), np.float32)
        o[:, 1:] = z[:, c0:c1].T
        return o
    mu_l = np.zeros((128, 4), np.float32)
    mu_l[0:64, 0] = mu[3072:3136]
    mu_l[0:64, 1] = mu[3136:3200]
    mu_l[0:128, 2] = mu[3200:3328]
    mu_l[0:32, 3] = mu[3328:3360]
    cvec = np.zeros((64, 5, 4, RTB), np.float32)
    for i, v in enumerate((w0, a0, k_k, k_a)):
        cvec[:, i, :, :] = v[cs].reshape(4, 64).T[:, :, None]
    cvec[:, 4, :, :] = r_k[4 * q:4 * q + 4].T[:, :, None]
    return dict(zrkv=zrkv, zw=rows(3072, 3136), za=rows(3136, 3200), zg0=rows(3200, 3328), zg1=np.concatenate([rows(3328, 3360), np.zeros((96, tl + 1), np.float32)], 0),
                mu_rkv=mu_rkv, mu_l=mu_l, w2=np.ascontiguousarray(w2[:, cs]), a2=np.ascontiguousarray(a2[:, cs]),
                g2a=np.ascontiguousarray(g2[0:128, cs]), g2b=np.concatenate([g2[128:160, cs], np.zeros((96, 256), np.float32)], 0), cvec=cvec,
                ones_in=np.ones((64, 256), np.float32))


def rw2_host_layout(o, lnx_w, lnx_b, q):
    tl = o["at"].shape[2]
    nch = tl // 64
    c5 = lambda a: np.asarray(a).reshape(64, 4, nch, 64)
    at, rt, bt, kt, xv, gg = (c5(o[k]) for k in ("at", "rt", "bt", "kt", "xv", "gg"))
    AR = np.stack([at, rt], 3).transpose(0, 2, 1, 3, 4)
    BK = np.stack([bt, kt], 3).transpose(0, 2, 1, 3, 4)
    TM = np.stack([at, bt, kt, xv, gg], 0).transpose(4, 3, 2, 0, 1)
    BS = c5(o["bs"])[0].transpose(2, 1, 0)
    GL = c5(o["ec"])[:, :, :, 63].transpose(0, 2, 1)
    cs = slice(256 * q, 256 * (q + 1))
    LN = np.broadcast_to(np.stack([lnx_w[cs].reshape(4, 64), lnx_b[cs].reshape(4, 64)], 0)[None], (64, 2, 4, 64))
    f = lambda a: np.ascontiguousarray(a, dtype=np.float32)
    return dict(AR=f(AR), BK=f(BK), TM=f(TM), BS=f(BS), GL=f(GL), LN=f(LN))


_PROGS = {}


def _prog(key, fn):
    if key not in _PROGS:
        _PROGS[key] = fn()
    return _PROGS[key]


def _run(nc, maps):
    return run_bass_kernel_spmd(nc, maps, core_ids=list(range(NCORES))).results


def _tok_shards(a):
    return [np.ascontiguousarray(a[c // 4, (c % 4) * NT:(c % 4 + 1) * NT, :].T) for c in range(NCORES)]


def _from_tok_shards(res, key, ncols):
    out = np.empty((B, T, ncols), np.float32)
    for c in range(NCORES):
        out[c // 4, (c % 4) * NT:(c % 4 + 1) * NT, :] = np.asarray(res[c][key]).T
    return out


def kernel(**inputs):
    I = {k: np.asarray(v) for k, v in inputs.items()}
    f32 = lambda a: np.ascontiguousarray(a, dtype=np.float32)
    xs = _tok_shards(f32(I["x"]))
    for layer in range(4):
        i = layer // 2
        if layer % 2 == 0:
            nc = _prog("k1e", lambda: build_k1(EVEN_IN))
            r = _run(nc, [{"xT": xs[c], "g": f32(I["norm_mix_pre"][layer]), "w": f32(I["ev_w_in"][i])} for c in range(NCORES)])
            pfull = _from_tok_shards(r, "oT", EVEN_IN)
            maps = []
            for c in range(NCORES):
                b, q = c // 4, c % 4
                d = s5_host_layout(f32(I["s5_lam_re"][i]), f32(I["s5_lam_im"][i]), f32(I["s5_log_dt"][i]),
                                   f32(I["s5_b_re"][i]), f32(I["s5_b_im"][i]), f32(I["s5_c_re"][i]),
                                   f32(I["s5_c_im"][i]), f32(I["s5_d"][i]), q)
                d["uT"] = np.ascontiguousarray(pfull[b, :, 256 * q:256 * (q + 1)].T)
                maps.append(d)
            rs = _run(_prog("s5", build_s5), maps)
            ycat = np.empty((B, T, D), np.float32)
            for c in range(NCORES):
                b, q = c // 4, c % 4
                ycat[b, :, 256 * q:256 * (q + 1)] = np.asarray(rs[c]["yT"]).T
            maps = [rw1_host_layout(pfull[c // 4, :, 1024:], f32(I["ev_shift_mu"][i]), f32(I["rw_w2"][i]),
                                    f32(I["rw_a2"][i]), f32(I["rw_g2"][i]), f32(I["rw_w0"][i]), f32(I["rw_a0"][i]),
                                    f32(I["rw_k_k"][i]), f32(I["rw_k_a"][i]), f32(I["rw_r_k"][i]), c % 4)
                    for c in range(NCORES)]
            r1 = _run(_prog("rw1", build_rw1), maps)
            maps = [rw2_host_layout(r1[c], f32(I["rw_lnx_w"][i]), f32(I["rw_lnx_b"][i]), c % 4) for c in range(NCORES)]
            r2 = _run(_prog("rw2", build_rw2), maps)
            for c in range(NCORES):
                b, q = c // 4, c % 4
                yo = np.asarray(r2[c]["yo"])
                ycat[b, :, 1024 + 256 * q:1024 + 256 * (q + 1)] = yo.transpose(1, 0, 2, 3).reshape(T, 256)
            ys = _tok_shards(ycat)
            nc = _prog("k2g", lambda: build_k2(D, glu=True))
            r = _run(nc, [{"inT": ys[c], "xT": xs[c], "g": f32(I["norm_mix_post"][layer]), "w": f32(I["ev_w_out"][i]),
                           "wglu": f32(I["s5_w_glu"][i])} for c in range(NCORES)])
        else:
            nc = _prog("k1o", lambda: build_k1(2 * D))
            r = _run(nc, [{"xT": xs[c], "g": f32(I["norm_mix_pre"][layer]), "w": f32(I["od_w_in"][i])} for c in range(NCORES)])
            pfull = _from_tok_shards(r, "oT", 2 * D)
            maps = []
            for c in range(NCORES):
                b, q = c // 4, c % 4
                cs = slice(512 * q, 512 * (q + 1))
                v = np.stack([I["od_conv_w"][i][0][cs], I["od_conv_w"][i][1][cs], I["od_conv_w"][i][2][cs],
                              I["od_conv_w"][i][3][cs], I["od_conv_b"][i][cs], I["lru_b_r"][i][cs],
                              I["lru_b_i"][i][cs], I["lru_lam"][i][cs]], -1)
                maps.append({"gateT": np.ascontiguousarray(pfull[b, :, cs].T),
                             "xbT": np.ascontiguousarray(pfull[b, :, D + 512 * q:D + 512 * (q + 1)].T),
                             "vecs": f32(v.reshape(4, 128, 8).transpose(1, 0, 2)),
                             "wr": f32(I["lru_w_r"][i][2 * q:2 * q + 2]), "wi": f32(I["lru_w_i"][i][2 * q:2 * q + 2])})
            rl = _run(_prog("lru", build_lru), maps)
            ycat = np.empty((B, T, D), np.float32)
            for c in range(NCORES):
                b, q = c // 4, c % 4
                ycat[b, :, 512 * q:512 * (q + 1)] = np.asarray(rl[c]["yT"]).T
            ys = _tok_shards(ycat)
            nc = _prog("k2p", lambda: build_k2(D))
            r = _run(nc, [{"inT": ys[c], "xT": xs[c], "g": f32(I["norm_mix_post"][layer]), "w": f32(I["od_w_out"][i])}
                          for c in range(NCORES)])
        xs = [f32(r[c]["oT"]) for c in range(NCORES)]
        nc = _prog("k1f", lambda: build_k1(DFF, swiglu=True))
        r = _run(nc, [{"xT": xs[c], "g": f32(I["norm_ffn_pre"][layer]), "w": f32(I["ffn_w_gate"][layer]),
                       "w2": f32(I["ffn_w_up"][layer])} for c in range(NCORES)])
        aT = [np.asarray(r[c]["oT"]) for c in range(NCORES)]
        nc = _prog("k2f", lambda: build_k2(DFF, in_bf16=True))
        r = _run(nc, [{"inT": aT[c], "xT": xs[c], "g": f32(I["norm_ffn_post"][layer]), "w": f32(I["ffn_w_down"][layer])}
                      for c in range(NCORES)])
        xs = [f32(r[c]["oT"]) for c in range(NCORES)]
    out = np.empty((B, T, D), np.float32)
    for c in range(NCORES):
        out[c // 4, (c % 4) * NT:(c % 4 + 1) * NT, :] = xs[c].T
    return out
```

```python
import contextlib
import numpy as np
import concourse.bass as bass
import concourse.mybir as mybir
from concourse.bass_utils import run_bass_kernel_spmd

F32 = mybir.dt.float32
BF16 = mybir.dt.bfloat16
AF = mybir.ActivationFunctionType
ALU = mybir.AluOpType
AX = mybir.AxisListType

NCORES = 8
D = 2048
B = 2
T = 4096
NT = 1024
DFF = 5632
EVEN_IN = 4384
NORM_EPS = 1e-6

ENGS = ("pe", "act", "dve", "pool", "sp")


class Res:
    __slots__ = ("lw", "rd")

    def __init__(self):
        self.lw = None
        self.rd = []


class Chan:
    __slots__ = ("sem", "cnt")

    def __init__(self, sem):
        self.sem = sem
        self.cnt = 0


class _Rec:
    def __init__(self):
        self.call = None

    def __getattr__(self, name):
        def f(*a, **k):
            assert self.call is None, "op closure must emit exactly one instruction"
            self.call = (name, a, k)
            return self
        return f


class Prog:
    def __init__(self, nc):
        self.nc = nc
        self.stack = contextlib.ExitStack()
        self.q = {e: [] for e in ENGS}
        self.esem = {e: self.stack.enter_context(nc.semaphore("es_" + e)) for e in ENGS}
        self.ecnt = {e: 0 for e in ENGS}
        self.seen = {e: {} for e in ENGS}
        self.out_events = []
        self._n = 0

    def name(self, base):
        self._n += 1
        return "%s_%d" % (base, self._n)

    def sbuf(self, shape, dt, name="sb"):
        return self.stack.enter_context(self.nc.sbuf_tensor(self.name(name), list(shape), dt))

    def psum(self, shape, dt=F32, name="ps"):
        return self.stack.enter_context(self.nc.psum_tensor(self.name(name), list(shape), dt))

    def chan(self, name="ch"):
        return Chan(self.stack.enter_context(self.nc.semaphore(self.name(name))))

    def _deps(self, eng, reads, writes, pe_group_start=True):
        evs = []
        for r in reads:
            if r.lw is not None and not (eng == "pe" and r.lw[2] == "pe"):
                evs.append(r.lw)
        for w in writes:
            if w.lw is not None and not (eng == "pe" and w.lw[2] == "pe" and not pe_group_start):
                evs.append(w.lw)
            for ev in w.rd:
                if ev[2] != eng:
                    evs.append(ev)
        waits = {}
        seen = self.seen[eng]
        for (sem, val, _e) in evs:
            k = id(sem)
            if seen.get(k, 0) >= val:
                continue
            if k not in waits or waits[k][1] < val:
                waits[k] = (sem, val)
        for k, (sem, val) in waits.items():
            seen[k] = val
        return list(waits.values())

    def _commit(self, ev, reads, writes):
        for w in writes:
            w.lw = ev
            w.rd = []
        for r in reads:
            r.rd.append(ev)

    def op(self, eng, fn, reads=(), writes=(), same_bank_cont=False):
        rec = _Rec()
        fn(rec)
        gs = True
        if eng == "pe" and rec.call[0] == "matmul":
            gs = bool(rec.call[2].get("start", True)) and not same_bank_cont
        waits = self._deps(eng, reads, writes, pe_group_start=gs)
        self.ecnt[eng] += 1
        ev = (self.esem[eng], self.ecnt[eng], eng)
        self.q[eng].append((waits, rec.call, (self.esem[eng], 1)))
        self._commit(ev, reads, writes)
        return ev

    def dma(self, queue, chan, out, in_, reads=(), writes=(), is_output=False, **kw):
        waits = self._deps(queue, reads, writes)
        chan.cnt += 1
        ev = (chan.sem, 16 * chan.cnt, "dma")
        self.q[queue].append((waits, ("dma_start", (), dict(out=out, in_=in_, **kw)), (chan.sem, 16)))
        self._commit(ev, reads, writes)
        if is_output:
            self.out_events.append(ev)
        return ev

    def finish(self):
        fin = {}
        for (sem, val, _e) in self.out_events:
            k = id(sem)
            if k not in fin or fin[k][1] < val:
                fin[k] = (sem, val)
        nc = self.nc
        q = self.q
        fin_waits = list(fin.values())

        def replay(engobj, items, extra_waits=()):
            for (waits, fn, inc) in items:
                for (sem, val) in waits:
                    engobj.wait_ge(sem, val)
                name, a, k = fn
                ins = getattr(engobj, name)(*a, **k)
                if inc is not None:
                    ins.then_inc(inc[0], inc[1])
            for (sem, val) in extra_waits:
                engobj.wait_ge(sem, val)

        with nc.Block() as block:
            @block.sync
            def _(e):
                replay(e, q["sp"], fin_waits)

            @block.tensor
            def _(e):
                replay(e, q["pe"])

            @block.scalar
            def _(e):
                replay(e, q["act"])

            @block.vector
            def _(e):
                replay(e, q["dve"])

            @block.gpsimd
            def _(e):
                replay(e, q["pool"])
        self.stack.close()


class Dense:
    def __init__(self, p, nt=NT):
        self.p = p
        nc = p.nc
        self.nt = nt
        self.ntb = nt // 512
        self.ones = p.sbuf([128, 128], BF16, "ones")
        self.ones_r = Res()
        p.op("pool", lambda e: e.memset(self.ones[:], 1.0), writes=[self.ones_r])
        self.banks = [p.psum([128, 512], F32, "bank") for _ in range(8)]
        self.bank_r = [Res() for _ in range(8)]
        self._bk = 0
        self.wslots = [p.sbuf([128, 16, 256], BF16, "wslot") for _ in range(3)]
        self.wslot_r = [Res() for _ in range(3)]
        self.wchan = [p.chan("wch") for _ in range(3)]
        self._ws = 0
        self.ostg = [p.sbuf([128, 512], F32, "ostg") for _ in range(4)]
        self.ostg_r = [Res() for _ in range(4)]
        self.ochan = [p.chan("och") for _ in range(4)]
        self._os = 0

    def bank(self):
        i = self._bk
        self._bk = (self._bk + 1) % 8
        return self.banks[i], self.bank_r[i]

    def wslot(self):
        i = self._ws
        self._ws = (self._ws + 1) % 3
        return self.wslots[i], self.wslot_r[i], self.wchan[i]

    def ostage(self):
        i = self._os
        self._os = (self._os + 1) % 4
        return self.ostg[i], self.ostg_r[i], self.ochan[i]


def load_fm(p, chan, dram_ap, nchunks, nt, dt=F32, name="act", queue="sp"):
    t = p.sbuf([128, nchunks, nt], dt, name)
    rs = [Res() for _ in range(nchunks)]
    src = dram_ap.rearrange("(c q) n -> q c n", q=128)
    step = max(1, 4)
    for c0 in range(0, nchunks, step):
        c1 = min(nchunks, c0 + step)
        p.dma(queue, chan if c0 == 0 else p.chan("chld"), t[:, c0:c1, :], src[:, c0:c1, :], writes=rs[c0:c1])
    return t, rs


def load_vec(p, chan, dram_ap, nchunks, name="vec"):
    t = p.sbuf([128, nchunks], F32, name)
    r = Res()
    src = dram_ap.rearrange("(c q) -> q c", q=128)
    p.dma("sp", chan, t[:], src, writes=[r], allow_slow_non_contiguous=True)
    return t, r


def rmsnorm_fm(p, dn, x_sb, x_r, nchunks, g_sb, g_r, out_dt=BF16, name="h", inplace=False):
    nt = dn.nt
    dmodel = nchunks * 128
    if inplace:
        h, h_r = x_sb, x_r
    else:
        h = p.sbuf([128, nchunks, nt], out_dt, name)
        h_r = [Res() for _ in range(nchunks)]
    sq = [p.sbuf([128, nt], BF16, "sq") for _ in range(2)]
    sq_r = [Res(), Res()]
    rstd = p.sbuf([128, nt], F32, "rstd")
    rstd_r = [Res() for _ in range(dn.ntb)]
    banks = [dn.bank() for _ in range(dn.ntb)]
    for c in range(nchunks):
        s, sr = sq[c % 2], sq_r[c % 2]
        p.op("act", lambda e, s=s, c=c: e.activation(out=s[:], in_=x_sb[:, c, :], func=AF.Square),
             reads=[x_r[c]], writes=[sr])
        for tb in range(dn.ntb):
            bk, bkr = banks[tb]
            p.op("pe", lambda e, bk=bk, s=s, tb=tb, c=c: e.matmul(
                bk[:], dn.ones[:], s[:, tb * 512:(tb + 1) * 512],
                start=(c == 0), stop=(c == nchunks - 1)),
                reads=[sr, dn.ones_r], writes=[bkr])
    for tb in range(dn.ntb):
        bk, bkr = banks[tb]
        sl = slice(tb * 512, (tb + 1) * 512)
        p.op("dve", lambda e, bk=bk, sl=sl: e.tensor_scalar(
            rstd[:, sl], bk[:], 1.0 / dmodel, NORM_EPS, ALU.mult, ALU.add),
            reads=[bkr], writes=[rstd_r[tb]])
        p.op("act", lambda e, sl=sl: e.activation(out=rstd[:, sl], in_=rstd[:, sl], func=AF.Sqrt),
             reads=[rstd_r[tb]], writes=[rstd_r[tb]])
        p.op("dve", lambda e, sl=sl: e.reciprocal(out=rstd[:, sl], in_=rstd[:, sl]),
             reads=[rstd_r[tb]], writes=[rstd_r[tb]])
    for c in range(nchunks):
        for tb in range(dn.ntb):
            sl = slice(tb * 512, (tb + 1) * 512)
            p.op("dve", lambda e, c=c, sl=sl: e.scalar_tensor_tensor(
                out=h[:, c, sl], in0=x_sb[:, c, sl], scalar=g_sb[:, c:c + 1], in1=rstd[:, sl],
                op0=ALU.mult, op1=ALU.mult),
                reads=[x_r[c], g_r, rstd_r[tb]], writes=[h_r[c]])
    return h, h_r


def linear_fm(p, dn, h, h_r, kchunks, w_dram, fdim, epilogue, w2_dram=None):
    nt = dn.nt
    FB = 256
    kstep = 16
    for f0 in range(0, fdim, FB):
        fb = min(FB, fdim - f0)
        slots = []
        for wd in ([w_dram] if w2_dram is None else [w_dram, w2_dram]):
            parts = []
            for k0 in range(0, kchunks, kstep):
                k1 = min(kchunks, k0 + kstep)
                ws, wr, wc = dn.wslot()
                src = wd[k0 * 128:k1 * 128, f0:f0 + fb].rearrange("(c q) f -> q c f", q=128)
                p.dma("pool", wc, ws[:, 0:k1 - k0, 0:fb], src, writes=[wr])
                parts.append((ws, wr, k0, k1))
            slots.append(parts)
        for fs in range(0, fb, 128):
            fsz = min(128, fb - fs)
            for tb in range(dn.ntb):
                outs = []
                for parts in slots:
                    bk, bkr = dn.bank()
                    nk = kchunks
                    for (ws, wr, k0, k1) in parts:
                        for k in range(k0, k1):
                            p.op("pe", lambda e, bk=bk, ws=ws, k=k, k0=k0, fs=fs, fsz=fsz, tb=tb: e.matmul(
                                bk[0:fsz, :], ws[:, k - k0, fs:fs + fsz], h[:, k, tb * 512:(tb + 1) * 512],
                                start=(k == 0), stop=(k == nk - 1)),
                                reads=[wr, h_r[k]], writes=[bkr])
                    outs += [bk, bkr]
                epilogue(f0 + fs, fsz, tb, *outs)


def build_k1(fdim, swiglu=False, nt=NT):
    nc = bass.Bass("TRN2", target_bir_lowering=False)
    xT = nc.dram_tensor("xT", [D, nt], F32, kind="ExternalInput").ap()
    g = nc.dram_tensor("g", [D], F32, kind="ExternalInput").ap()
    w = nc.dram_tensor("w", [D, fdim], F32, kind="ExternalInput").ap()
    w2 = nc.dram_tensor("w2", [D, fdim], F32, kind="ExternalInput").ap() if swiglu else None
    odt = BF16 if swiglu else F32
    oT = nc.dram_tensor("oT", [fdim, nt], odt, kind="ExternalOutput").ap()
    p = Prog(nc)
    dn = Dense(p, nt)
    ch_in = p.chan("chin")
    x_sb, x_r = load_fm(p, ch_in, xT, D // 128, nt, name="x")
    g_sb, g_r = load_vec(p, p.chan("chg"), g, D // 128)
    h, h_r = rmsnorm_fm(p, dn, x_sb, x_r, D // 128, g_sb, g_r)
    ostg_bf = [p.sbuf([128, 512], BF16, "ostgb") for _ in range(4)]
    sil = [p.sbuf([128, 512], F32, "sil") for _ in range(2)]
    sil_r = [Res(), Res()]
    cnt = [0]

    def epi_plain(f0, fsz, tb, bk, bkr):
        st, sr, sc = dn.ostage()
        eng = "act" if cnt[0] % 2 == 0 else "dve"
        cnt[0] += 1
        if eng == "act":
            p.op("act", lambda e: e.copy(out=st[0:fsz, :], in_=bk[0:fsz, :]), reads=[bkr], writes=[sr])
        else:
            p.op("dve", lambda e: e.tensor_copy(out=st[0:fsz, :], in_=bk[0:fsz, :]), reads=[bkr], writes=[sr])
        p.dma("sp", sc, oT[f0:f0 + fsz, tb * 512:(tb + 1) * 512], st[0:fsz, :], reads=[sr], is_output=True)

    def epi_swiglu(f0, fsz, tb, bg, bgr, bu, bur):
        i = dn._os
        st, sr, sc = dn.ostage()
        stb = ostg_bf[i]
        s, s_r = sil[cnt[0] % 2], sil_r[cnt[0] % 2]
        cnt[0] += 1
        p.op("act", lambda e: e.activation(out=s[0:fsz, :], in_=bg[0:fsz, :], func=AF.Silu),
             reads=[bgr], writes=[s_r])
        p.op("dve", lambda e: e.tensor_tensor(out=stb[0:fsz, :], in0=s[0:fsz, :], in1=bu[0:fsz, :], op=ALU.mult),
             reads=[s_r, bur], writes=[sr])
        p.dma("sp", sc, oT[f0:f0 + fsz, tb * 512:(tb + 1) * 512], stb[0:fsz, :], reads=[sr], is_output=True)

    linear_fm(p, dn, h, h_r, D // 128, w, fdim, epi_swiglu if swiglu else epi_plain, w2_dram=w2)
    p.finish()
    return nc


def build_k2(kdim, in_bf16=False, glu=False, nt=NT):
    nc = bass.Bass("TRN2", target_bir_lowering=False)
    in_dt = BF16 if in_bf16 else F32
    inT = nc.dram_tensor("inT", [kdim, nt], in_dt, kind="ExternalInput").ap()
    xT = nc.dram_tensor("xT", [D, nt], F32, kind="ExternalInput").ap()
    g = nc.dram_tensor("g", [D], F32, kind="ExternalInput").ap()
    w = nc.dram_tensor("w", [kdim, D], F32, kind="ExternalInput").ap()
    wg = nc.dram_tensor("wglu", [1024, 1024], F32, kind="ExternalInput").ap() if glu else None
    oT = nc.dram_tensor("oT", [D, nt], F32, kind="ExternalOutput").ap()
    p = Prog(nc)
    dn = Dense(p, nt)
    kch = kdim // 128
    h = p.sbuf([128, kch, nt], BF16, "hin")
    h_r = [Res() for _ in range(kch)]
    ch_in = p.chan("chin")
    src = inT.rearrange("(c q) n -> q c n", q=128)
    g_sb, g_r = load_vec(p, p.chan("chg"), g, D // 128)
    if glu:
        ys = p.sbuf([128, 8, nt], BF16, "ys5")
        ys_r = [Res() for _ in range(8)]
        for c0 in range(0, 8, 4):
            p.dma("pool", p.chan("chin"), ys[:, c0:c0 + 4, :], src[:, c0:c0 + 4, :], writes=ys_r[c0:c0 + 4])
        for c0 in range(8, 16, 4):
            p.dma("pool", p.chan("chin"), h[:, c0:c0 + 4, :], src[:, c0:c0 + 4, :], writes=h_r[c0:c0 + 4])
        sg = [p.sbuf([128, 512], F32, "sg") for _ in range(2)]
        sg_r = [Res(), Res()]
        cg = [0]

        def epi_glu(f0, fsz, tb, bk, bkr):
            s_, sr_ = sg[cg[0] % 2], sg_r[cg[0] % 2]
            cg[0] += 1
            c = f0 // 128
            sl = slice(tb * 512, (tb + 1) * 512)
            p.op("act", lambda e: e.activation(out=s_[:], in_=bk[:], func=AF.Sigmoid), reads=[bkr], writes=[sr_])
            p.op("dve", lambda e: e.tensor_tensor(out=h[:, c, sl], in0=s_[:], in1=ys[:, c, sl], op=ALU.mult),
                 reads=[sr_, ys_r[c]], writes=[h_r[c]])
        linear_fm(p, dn, ys, ys_r, 8, wg, 1024, epi_glu)
    else:
        q = "sp" if in_bf16 else "pool"
        for c0 in range(0, kch, 4):
            c1 = min(kch, c0 + 4)
            p.dma(q, p.chan("chin"), h[:, c0:c1, :], src[:, c0:c1, :], writes=h_r[c0:c1])
    o_sb = p.sbuf([128, D // 128, nt], F32, "osb")
    o_r = [Res() for _ in range(D // 128)]
    ce = [0]

    def epi_o(f0, fsz, tb, bk, bkr):
        c = f0 // 128
        sl = slice(tb * 512, (tb + 1) * 512)
        ce[0] += 1
        if ce[0] % 2 == 0:
            p.op("act", lambda e: e.copy(out=o_sb[:, c, sl], in_=bk[:]), reads=[bkr], writes=[o_r[c]])
        else:
            p.op("dve", lambda e: e.tensor_copy(out=o_sb[:, c, sl], in_=bk[:]), reads=[bkr], writes=[o_r[c]])
    linear_fm(p, dn, h, h_r, kch, w, D, epi_o)
    hn, hn_r = rmsnorm_fm(p, dn, o_sb, o_r, D // 128, g_sb, g_r, out_dt=F32, inplace=True)
    xsrc = xT.rearrange("(c q) n -> q c n", q=128)
    odst = oT.rearrange("(c q) n -> q c n", q=128)
    xst = [p.sbuf([128, nt], F32, "xst") for _ in range(3)]
    xst_r = [Res() for _ in range(3)]
    xch = [p.chan("xch") for _ in range(3)]
    for c in range(D // 128):
        i = c % 3
        p.dma("sp", xch[i], xst[i][:], xsrc[:, c, :], writes=[xst_r[i]])
        p.op("dve" if c % 2 == 0 else "pool", lambda e, i=i, c=c: e.tensor_tensor(
            out=xst[i][:], in0=xst[i][:], in1=hn[:, c, :], op=ALU.add),
            reads=[hn_r[c], xst_r[i]], writes=[xst_r[i]])
        p.dma("sp", xch[i], odst[:, c, :], xst[i][:], reads=[xst_r[i]], is_output=True)
    p.finish()
    return nc


class TPool:
    def __init__(self, p, shape, dt, n, name="tp", psum=False):
        mk = p.psum if psum else p.sbuf
        self.t = [mk(shape, dt, name) for _ in range(n)]
        self.r = [Res() for _ in range(n)]
        self.i = 0

    def get(self):
        i = self.i
        self.i = (i + 1) % len(self.t)
        return self.t[i], self.r[i]


def o_tt(p, eng, out, in0, in1, op, reads, writes):
    return p.op(eng, lambda e: e.tensor_tensor(out=out, in0=in0, in1=in1, op=op), reads=reads, writes=writes)


def o_ts(p, eng, out, in0, s1, s2, op0, op1, reads, writes):
    if s2 is None:
        return p.op(eng, lambda e: e.tensor_scalar(out, in0, s1, None, op0), reads=reads, writes=writes)
    return p.op(eng, lambda e: e.tensor_scalar(out, in0, s1, s2, op0, op1), reads=reads, writes=writes)


def o_stt(p, out, in0, scalar, in1, op0, op1, reads, writes):
    return p.op("dve", lambda e: e.scalar_tensor_tensor(out=out, in0=in0, scalar=scalar, in1=in1, op0=op0, op1=op1),
                reads=reads, writes=writes)


def o_act(p, out, in_, func, reads, writes, scale=1.0, bias=None):
    if bias is None:
        return p.op("act", lambda e: e.activation(out=out, in_=in_, func=func, scale=scale), reads=reads, writes=writes)
    return p.op("act", lambda e: e.activation(out=out, in_=in_, func=func, scale=scale, bias=bias),
                reads=reads, writes=writes)


def gelu_tanh(p, tp, out, x, xr, outr, eng="pool"):
    t1, r1 = tp.get()
    o_act(p, t1[:], x, AF.Square, [xr], [r1], scale=0.21145921592425385)
    o_stt(p, t1[:], t1[:], 1.0, x, ALU.add, ALU.mult, [r1, xr], [r1])
    o_act(p, t1[:], t1[:], AF.Sigmoid, [r1], [r1], scale=1.5957691216057308)
    o_tt(p, eng, out, t1[:], x, ALU.mult, [r1, xr], [outr])


def build_lru(tlen=T):
    nc = bass.Bass("TRN2", target_bir_lowering=False)
    CH = 512
    gateT = nc.dram_tensor("gateT", [CH, tlen], F32, kind="ExternalInput").ap()
    xbT = nc.dram_tensor("xbT", [CH, tlen], F32, kind="ExternalInput").ap()
    vecs = nc.dram_tensor("vecs", [128, 4, 8], F32, kind="ExternalInput").ap()
    wr = nc.dram_tensor("wr", [2, 256, 256], F32, kind="ExternalInput").ap()
    wi = nc.dram_tensor("wi", [2, 256, 256], F32, kind="ExternalInput").ap()
    yT = nc.dram_tensor("yT", [CH, tlen], F32, kind="ExternalOutput").ap()
    p = Prog(nc)
    ntb = tlen // 512
    vec = p.sbuf([128, 4, 8], F32, "vec")
    vec_r = Res()
    p.dma("sp", p.chan(), vec[:], vecs, writes=[vec_r])
    c8 = p.sbuf([128, 4], F32, "c8")
    c8_r = Res()
    o_act(p, c8[:], vec[:, :, 7], AF.Exp, [vec_r], [c8_r], scale=-1.0)
    o_ts(p, "dve", c8[:], c8[:], 1.0, None, ALU.add, None, [c8_r], [c8_r])
    o_act(p, c8[:], c8[:], AF.Ln, [c8_r], [c8_r])
    o_ts(p, "dve", c8[:], c8[:], -8.0, None, ALU.mult, None, [c8_r], [c8_r])
    wsb = {}
    wch = p.chan()
    for nm, wd in (("r", wr), ("i", wi)):
        t = p.sbuf([128, 2, 2, 256], BF16, "w" + nm)
        r = Res()
        wch = p.chan()
        for n in range(2):
            p.dma("pool", wch, t[:, n, :, :], wd[n].rearrange("(c q) d -> q c d", q=128), writes=[r])
        wsb[nm] = (t, r)
    xin = TPool(p, [128, 515], F32, 4, "xin")
    gin = TPool(p, [128, 512], F32, 3, "gin")
    xch = [p.chan() for _ in range(4)]
    gch = [p.chan() for _ in range(3)]
    och = [p.chan() for _ in range(3)]
    xcp = TPool(p, [128, 512], F32, 4, "xc")
    xcb = TPool(p, [128, 512], BF16, 4, "xcb")
    tmp = TPool(p, [128, 512], F32, 8, "tmp")
    hp = TPool(p, [128, 512], F32, 8, "h")
    op_ = TPool(p, [128, 512], F32, 3, "o")
    psp = TPool(p, [128, 512], F32, 8, "ps", psum=True)
    hprev = {}
    xsrc = xbT.rearrange("(c q) n -> q c n", q=128)
    gsrc = gateT.rearrange("(c q) n -> q c n", q=128)
    ydst = yT.rearrange("(c q) n -> q c n", q=128)
    for tb in range(ntb):
        t0 = tb * 512
        for n in range(2):
            xcs = []
            for cc in range(2):
                c = 2 * n + cc
                i = xin.i
                xt, xr = xin.get()
                if tb == 0:
                    p.op("pool", lambda e, xt=xt: e.memset(xt[:, 0:3], 0.0), writes=[xr])
                    p.dma("sp", xch[i], xt[:, 3:515], xsrc[:, c, 0:512], writes=[xr])
                else:
                    p.dma("sp", xch[i], xt[:, 0:515], xsrc[:, c, t0 - 3:t0 + 512], writes=[xr])
                xc, xcr = xcp.get()
                o_ts(p, "dve", xc[:], xt[:, 3:515], vec[:, c, 3:4], vec[:, c, 4:5], ALU.mult, ALU.add, [xr, vec_r], [xcr])
                for j in range(3):
                    o_stt(p, xc[:], xt[:, j:j + 512], vec[:, c, j:j + 1], xc[:], ALU.mult, ALU.add, [xr, vec_r, xcr], [xcr])
                xb_, xbr = xcb.get()
                p.op("act", lambda e, xb_=xb_, xc=xc: e.copy(out=xb_[:], in_=xc[:]), reads=[xcr], writes=[xbr])
                xcs.append((xc, xcr, xb_, xbr))
            for dc in range(2):
                c = 2 * n + dc
                xc, xcr = xcs[dc][0], xcs[dc][1]
                pr, prr = psp.get()
                pi, pir = psp.get()
                for (pt, ptr, nm) in ((pr, prr, "r"), (pi, pir, "i")):
                    wt, wtr = wsb[nm]
                    for cc in range(2):
                        p.op("pe", lambda e, pt=pt, wt=wt, cc=cc, dc=dc, n=n, xb_=xcs[cc][2]: e.matmul(
                            pt[:], wt[:, n, cc, dc * 128:(dc + 1) * 128], xb_[:], start=(cc == 0), stop=(cc == 1)),
                            reads=[wtr, xcs[cc][3]], writes=[ptr])
                sr, srr = tmp.get()
                si, sir = tmp.get()
                o_act(p, sr[:], pr[:], AF.Sigmoid, [prr, vec_r], [srr], bias=vec[:, c, 5:6])
                o_act(p, si[:], pi[:], AF.Sigmoid, [pir, vec_r], [sir], bias=vec[:, c, 6:7])
                a_, ar = tmp.get()
                o_act(p, a_[:], sr[:], AF.Exp, [srr, c8_r], [ar], scale=c8[:, c:c + 1])
                m_, mr = tmp.get()
                o_tt(p, "pool", m_[:], a_[:], a_[:], ALU.mult, [ar], [mr])
                o_ts(p, "pool", m_[:], m_[:], -1.0, 1.0, ALU.mult, ALU.add, [mr], [mr])
                o_act(p, m_[:], m_[:], AF.Sqrt, [mr], [mr])
                o_tt(p, "pool", m_[:], m_[:], si[:], ALU.mult, [mr, sir], [mr])
                o_tt(p, "dve", m_[:], m_[:], xc[:], ALU.mult, [mr, xcr], [mr])
                h_, hr = hp.get()
                if c in hprev:
                    ph, phr = hprev[c]
                    p.op("dve", lambda e, h_=h_, a_=a_, m_=m_, ph=ph: e.tensor_tensor_scan(
                        out=h_[:], data0=a_[:], data1=m_[:], initial=ph[:, 511:512], op0=ALU.mult, op1=ALU.add),
                        reads=[ar, mr, phr], writes=[hr])
                else:
                    p.op("dve", lambda e, h_=h_, a_=a_, m_=m_: e.tensor_tensor_scan(
                        out=h_[:], data0=a_[:], data1=m_[:], initial=0.0, op0=ALU.mult, op1=ALU.add),
                        reads=[ar, mr], writes=[hr])
                hprev[c] = (h_, hr)
                gi_ = gin.i
                gt, gr_ = gin.get()
                p.dma("sp", gch[gi_], gt[:], gsrc[:, c, t0:t0 + 512], writes=[gr_])
                oi = op_.i
                ot, otr = op_.get()
                gelu_tanh(p, tmp, ot[:], gt[:], gr_, otr, eng="pool")
                o_tt(p, "dve", ot[:], ot[:], h_[:], ALU.mult, [otr, hr], [otr])
                p.dma("sp", och[oi], ydst[:, c, t0:t0 + 512], ot[:], reads=[otr], is_output=True)
    p.finish()
    return nc


TWO_PI = 6.283185307179586


def frac_wrap(p, x, ti, tf, reads, r):
    p.op("dve", lambda e: e.tensor_copy(out=ti, in_=x), reads=reads, writes=[r])
    p.op("dve", lambda e: e.tensor_copy(out=tf, in_=ti), reads=reads, writes=[r])
    o_tt(p, "dve", x, x, tf, ALU.subtract, reads, [r])
    o_stt(p, tf, x, 0.5, x, ALU.is_gt, ALU.subtract, reads, [r])
    o_stt(p, x, tf, 0.5, tf, ALU.is_gt, ALU.subtract, reads, [r])


def sincos_turns(p, S, C, ph, tabs, halfpi_ap, reads, rS, rC, rt):
    o_act(p, S, ph, AF.Sin, reads, [rS], scale=TWO_PI)
    o_act(p, tabs, ph, AF.Abs, reads, [rt])
    o_act(p, C, tabs, AF.Sin, [rt], [rC], scale=-TWO_PI, bias=halfpi_ap)


def s5_pre(p, lr, li, ldt, shape, pi_ap, tagr, blockrot=False):
    r = Res()
    mk = lambda nm: p.sbuf(shape, F32, "s5" + nm)
    dt, mag, f0, f0c, sn, cs = mk("dt"), mk("mag"), mk("f0"), mk("f0c"), mk("sn"), mk("cs")
    are, aim, den, f1 = mk("are"), mk("aim"), mk("den"), mk("f1")
    t1, t2, gre, gim = f0c, sn, cs, dt
    R = [tagr, r]
    o_act(p, dt[:], ldt, AF.Exp, R, [r])
    o_tt(p, "dve", mag[:], lr, dt[:], ALU.mult, R, [r])
    o_act(p, mag[:], mag[:], AF.Exp, R, [r])
    o_tt(p, "dve", f0[:], li, dt[:], ALU.mult, R, [r])
    o_ts(p, "dve", f0[:], f0[:], 1.0 / TWO_PI, None, ALU.mult, None, R, [r])
    ti = p.sbuf(shape, mybir.dt.int32, "s5ti")
    frac_wrap(p, f0[:], ti[:], f0c[:], R, r)
    sincos_turns(p, sn[:], cs[:], f0[:], f0c[:], pi_ap, R, r, r, r)
    o_tt(p, "dve", are[:], mag[:], cs[:], ALU.mult, R, [r])
    o_tt(p, "dve", aim[:], mag[:], sn[:], ALU.mult, R, [r])
    o_tt(p, "dve", den[:], lr, lr, ALU.mult, R, [r])
    o_tt(p, "dve", t1[:], li, li, ALU.mult, R, [r])
    o_tt(p, "dve", den[:], den[:], t1[:], ALU.add, R, [r])
    p.op("dve", lambda e: e.reciprocal(out=den[:], in_=den[:]), reads=R, writes=[r])
    o_ts(p, "dve", t1[:], are[:], -1.0, None, ALU.add, None, R, [r])
    o_tt(p, "dve", gre[:], t1[:], lr, ALU.mult, R, [r])
    o_tt(p, "dve", t2[:], aim[:], li, ALU.mult, R, [r])
    o_tt(p, "dve", gre[:], gre[:], t2[:], ALU.add, R, [r])
    o_tt(p, "dve", gre[:], gre[:], den[:], ALU.mult, R, [r])
    o_tt(p, "dve", gim[:], aim[:], lr, ALU.mult, R, [r])
    o_tt(p, "dve", t2[:], t1[:], li, ALU.mult, R, [r])
    o_tt(p, "dve", gim[:], gim[:], t2[:], ALU.subtract, R, [r])
    o_tt(p, "dve", gim[:], gim[:], den[:], ALU.mult, R, [r])
    o_ts(p, "dve", f1[:], f0[:], 64.0, None, ALU.mult, None, R, [r])
    frac_wrap(p, f1[:], ti[:], den[:], R, r)
    if not blockrot:
        return dict(mag=mag, f0=f0, f1=f1, gre=gre, gim=gim), r
    f512, c5r, c5i, tab5 = mk("f512"), mk("c5r"), mk("c5i"), mk("tab5")
    o_ts(p, "dve", f512[:], f1[:], 8.0, None, ALU.mult, None, R, [r])
    frac_wrap(p, f512[:], ti[:], den[:], R, r)
    sincos_turns(p, c5i[:], c5r[:], f512[:], tab5[:], pi_ap, R, r, r, r)
    return dict(mag=mag, f0=f0, f1=f1, gre=gre, gim=gim, c5r=c5r, c5i=c5i), r


def emit_s5(p, uT, lamP, lamR, bmat, cmat, dsk, yT, tlen):
    nc = p.nc
    ntb = tlen // 512
    pi_t = p.sbuf([128, 1], F32, "pi")
    pi_r = Res()
    p.op("pool", lambda e: e.memset(pi_t[:], 1.5707963267948966), writes=[pi_r])
    ch0 = p.chan()
    lp = p.sbuf([128, 3, 8], F32, "lamP")
    lp_r = Res()
    p.dma("sp", p.chan(), lp[:], lamP, writes=[lp_r])
    lrw = p.sbuf([32, 3, 1024], F32, "lamR")
    lrw_r = Res()
    p.dma("sp", p.chan(), lrw[:], lamR, writes=[lrw_r])
    p.op("dve", lambda e: e.tensor_copy(out=lp[:, 0, 0:1], in_=lp[:, 0, 0:1]), reads=[pi_r, lp_r], writes=[lp_r])
    P, Pr = s5_pre(p, lp[:, 0, :], lp[:, 1, :], lp[:, 2, :], [128, 8], pi_t[:, 0:1], lp_r, blockrot=True)
    p.op("dve", lambda e: e.tensor_copy(out=lrw[:, 0, 0:1], in_=lrw[:, 0, 0:1]), reads=[pi_r, lrw_r], writes=[lrw_r])
    Rw, Rr = s5_pre(p, lrw[:, 0, :], lrw[:, 1, :], lrw[:, 2, :], [32, 1024], pi_t[0:32, 0:1], lrw_r)
    bsb = p.sbuf([32, 2, 1024], F32, "bsb")
    b_r = Res()
    p.dma("sp", p.chan(), bsb[:], bmat, writes=[b_r])
    BT = p.sbuf([32, 2, 1024], F32, "BT")
    BT_r = Res()
    tb1 = p.sbuf([32, 1024], F32, "tb1")
    tb1_r = Res()
    o_tt(p, "dve", tb1[:], bsb[:, 1, :], Rw["gim"][:], ALU.mult, [b_r, Rr], [tb1_r])
    o_tt(p, "dve", BT[:, 0, :], bsb[:, 0, :], Rw["gre"][:], ALU.mult, [b_r, Rr], [BT_r])
    o_tt(p, "dve", BT[:, 0, :], BT[:, 0, :], tb1[:], ALU.subtract, [BT_r, tb1_r], [BT_r])
    o_tt(p, "dve", tb1[:], bsb[:, 1, :], Rw["gre"][:], ALU.mult, [b_r, Rr, BT_r], [tb1_r])
    o_tt(p, "dve", BT[:, 1, :], bsb[:, 0, :], Rw["gim"][:], ALU.mult, [b_r, Rr], [BT_r])
    o_tt(p, "dve", BT[:, 1, :], BT[:, 1, :], tb1[:], ALU.add, [BT_r, tb1_r], [BT_r])
    csb = p.sbuf([128, 2, 8, 32], F32, "csb")
    c_r = Res()
    p.dma("sp", p.chan(), csb[:], cmat, writes=[c_r])
    o_ts(p, "dve", csb[:, 1, :, :], csb[:, 1, :, :], -1.0, None, ALU.mult, None, [c_r], [c_r])
    dsb = p.sbuf([32, 8], F32, "dsb")
    d_r = Res()
    p.dma("sp", p.chan(), dsb[:], dsk, writes=[d_r])
    ia = p.sbuf([128, 512], mybir.dt.int32, "ia")
    ib = p.sbuf([128, 512], mybir.dt.int32, "ib")
    A0 = p.sbuf([128, 512], F32, "A0")
    B0 = p.sbuf([128, 512], F32, "B0")
    io_r = Res()
    p.op("pool", lambda e: e.iota(ia[:], [[1, 8], [0, 64]], base=0, channel_multiplier=0), writes=[io_r])
    p.op("pool", lambda e: e.iota(ib[:], [[0, 8], [1, 64]], base=0, channel_multiplier=0), writes=[io_r])
    p.op("dve", lambda e: e.tensor_copy(out=A0[:], in_=ia[:]), reads=[io_r], writes=[io_r])
    p.op("dve", lambda e: e.tensor_copy(out=B0[:], in_=ib[:]), reads=[io_r], writes=[io_r])
    ones = p.sbuf([128, 512], F32, "ones5")
    p.op("pool", lambda e: e.memset(ones[:], 1.0), writes=[io_r])

    usrc = uT.rearrange("(m q) n -> q m n", q=32)
    ydst = yT.rearrange("(m q) n -> q m n", q=32)
    upool = TPool(p, [32, tlen], F32, 2, "u")
    uch = [p.chan(), p.chan()]
    rho_p = TPool(p, [128, 512], F32, 2, "rho")
    psb = TPool(p, [128, 512], F32, 4, "psbu", psum=True)
    psy = TPool(p, [32, 512], F32, 2, "psy", psum=True)
    tp = TPool(p, [128, 512], F32, 4 * S5_G + 4, "s5t")
    tabp = TPool(p, [128, 512], F32, 2 * S5_G + 2, "s5tab")
    zip_ = TPool(p, [128, 4], F32, 4 * S5_G, "s5zi")
    tip = TPool(p, [128, 512], mybir.dt.int32, S5_G + 1, "s5ti")
    zp = TPool(p, [128, 512], F32, 4 * S5_G + 2, "s5z")
    t32 = TPool(p, [32, 512], F32, 3 * S5_G + 2, "s5o")
    och = [p.chan() for _ in range(6)]
    def mloop(m):
        ui = upool.i
        u, u_r = upool.get()
        p.dma("sp", uch[ui], u[:], usrc[:, m, :], writes=[u_r])
        rho, rho_r = rho_p.get()
        o_ts(p, "dve", rho[:], ones[:], P["mag"][:, m:m + 1], None, ALU.mult, None, [io_r, Pr], [rho_r])
        ph, phr = tp.get()
        phc, phcr = tp.get()
        o_ts(p, "dve", ph[:], A0[:], P["f1"][:, m:m + 1], None, ALU.mult, None, [io_r, Pr], [phr])
        o_stt(p, ph[:], B0[:], P["f0"][:, m:m + 1], ph[:], ALU.mult, ALU.add, [io_r, Pr, phr], [phr])
        ti, tir = tip.get()
        frac_wrap(p, ph[:], ti[:], phc[:], [phr, phcr, tir], phr)
        S, Sr = tabp.get()
        C, Cr = tabp.get()
        sincos_turns(p, S[:], C[:], ph[:], phc[:], pi_t[:, 0:1], [phr, pi_r], Sr, Cr, phcr)
        zprev = None
        for tb in range(ntb):
            sl = slice(tb * 512, (tb + 1) * 512)
            pre, prer = psb.get()
            pim, pimr = psb.get()
            p.op("pe", lambda e: e.matmul(pre[:], BT[:, 0, m * 128:(m + 1) * 128], u[:, sl], start=True, stop=True),
                 reads=[BT_r, u_r], writes=[prer])
            p.op("pe", lambda e: e.matmul(pim[:], BT[:, 1, m * 128:(m + 1) * 128], u[:, sl], start=True, stop=True),
                 reads=[BT_r, u_r], writes=[pimr])
            bre, brer = tp.get()
            bim, bimr = tp.get()
            p.op("act", lambda e: e.copy(out=bre[:], in_=pre[:]), reads=[prer], writes=[brer])
            p.op("act", lambda e: e.copy(out=bim[:], in_=pim[:]), reads=[pimr], writes=[bimr])
            zi = None
            if zprev is not None:
                (pzre, pzrer), (pzim, pzimr) = zprev
                zi, zir = zip_.get()
                cr_, ci_ = P["c5r"][:, m:m + 1], P["c5i"][:, m:m + 1]
                o_ts(p, "dve", zi[:, 0:1], pzim[:, 511:512], ci_, None, ALU.mult, None, [pzimr, Pr], [zir])
                o_stt(p, zi[:, 1:2], pzre[:, 511:512], cr_, zi[:, 0:1], ALU.mult, ALU.subtract, [pzrer, Pr, zir], [zir])
                o_ts(p, "dve", zi[:, 2:3], pzim[:, 511:512], cr_, None, ALU.mult, None, [pzimr, Pr, zir], [zir])
                o_stt(p, zi[:, 3:4], pzre[:, 511:512], ci_, zi[:, 2:3], ALU.mult, ALU.add, [pzrer, Pr, zir], [zir])
            yield
            t1, t1r = tp.get()
            t2, t2r = tp.get()
            o_tt(p, "pool", t1[:], S[:], bim[:], ALU.mult, [Sr, bimr], [t1r])
            o_tt(p, "dve", t2[:], S[:], bre[:], ALU.mult, [Sr, brer], [t2r])
            o_tt(p, "pool", bre[:], C[:], bre[:], ALU.mult, [Cr, brer, t2r], [brer])
            o_tt(p, "dve", bim[:], C[:], bim[:], ALU.mult, [Cr, bimr, t1r], [bimr])
            o_tt(p, "pool", bre[:], bre[:], t1[:], ALU.add, [brer, t1r], [brer])
            o_tt(p, "dve", bim[:], bim[:], t2[:], ALU.subtract, [bimr, t2r], [bimr])
            yield
            zre, zrer = zp.get()
            zim, zimr = zp.get()
            for (z, zr, w_, wr_, k) in ((zre, zrer, bre, brer, 0), (zim, zimr, bim, bimr, 1)):
                if zi is None:
                    p.op("dve", lambda e, z=z, w_=w_: e.tensor_tensor_scan(
                        out=z[:], data0=rho[:], data1=w_[:], initial=0.0, op0=ALU.mult, op1=ALU.add),
                        reads=[rho_r, wr_], writes=[zr])
                else:
                    p.op("dve", lambda e, z=z, w_=w_, k=k: e.tensor_tensor_scan(
                        out=z[:], data0=rho[:], data1=w_[:], initial=zi[:, 2 * k + 1:2 * k + 2], op0=ALU.mult, op1=ALU.add),
                        reads=[rho_r, wr_, zir], writes=[zr])
            zprev = ((zre, zrer), (zim, zimr))
            yield
            o_tt(p, "pool", t1[:], S[:], zim[:], ALU.mult, [Sr, zimr, t1r], [t1r])
            o_tt(p, "dve", t2[:], S[:], zre[:], ALU.mult, [Sr, zrer, t2r], [t2r])
            o_tt(p, "pool", bre[:], C[:], zre[:], ALU.mult, [Cr, zrer, brer], [brer])
            o_tt(p, "dve", bim[:], C[:], zim[:], ALU.mult, [Cr, zimr, bimr], [bimr])
            o_tt(p, "pool", bre[:], bre[:], t1[:], ALU.subtract, [brer, t1r], [brer])
            o_tt(p, "dve", bim[:], bim[:], t2[:], ALU.add, [bimr, t2r], [bimr])
            py, pyr = psy.get()
            p.op("pe", lambda e: e.matmul(py[:], csb[:, 0, m, :], bre[:], start=True, stop=False),
                 reads=[c_r, brer], writes=[pyr])
            p.op("pe", lambda e: e.matmul(py[:], csb[:, 1, m, :], bim[:], start=False, stop=True),
                 reads=[c_r, bimr], writes=[pyr])
            y2, y2r = t32.get()
            o_stt(p, y2[:], u[:, sl], dsb[:, m:m + 1], py[:], ALU.mult, ALU.add, [u_r, d_r, pyr], [y2r])
            yo, yor = t32.get()
            gelu_tanh(p, t32, yo[:], y2[:], y2r, yor, eng="pool")
            oi = ocnt[0] % len(och)
            ocnt[0] += 1
            p.dma("sp", och[oi], ydst[:, m, sl], yo[:], reads=[yor], is_output=True)
            yield

    ocnt = [0]
    for m0 in range(0, 8, S5_G):
        gens = [mloop(m) for m in range(m0, min(8, m0 + S5_G))]
        while gens:
            for g in list(gens):
                try:
                    next(g)
                except StopIteration:
                    gens.remove(g)


def build_s5(tlen=T):
    nc = bass.Bass("TRN2", target_bir_lowering=False)
    uT = nc.dram_tensor("uT", [256, tlen], F32, kind="ExternalInput").ap()
    lamP = nc.dram_tensor("lamP", [128, 3, 8], F32, kind="ExternalInput").ap()
    lamR = nc.dram_tensor("lamR", [32, 3, 1024], F32, kind="ExternalInput").ap()
    bmat = nc.dram_tensor("bmat", [32, 2, 1024], F32, kind="ExternalInput").ap()
    cmat = nc.dram_tensor("cmat", [128, 2, 8, 32], F32, kind="ExternalInput").ap()
    dsk = nc.dram_tensor("dsk", [32, 8], F32, kind="ExternalInput").ap()
    yT = nc.dram_tensor("yT", [256, tlen], F32, kind="ExternalOutput").ap()
    p = Prog(nc)
    emit_s5(p, uT, lamP, lamR, bmat, cmat, dsk, yT, tlen)
    p.finish()
    return nc


def s5_host_layout(lam_re, lam_im, log_dt, b_re, b_im, c_re, c_im, d_skip, q):
    g0 = 16 * q
    lr = lam_re[g0:g0 + 16]
    li = lam_im[g0:g0 + 16]
    ld = np.broadcast_to(log_dt[g0:g0 + 16, None], (16, 64))
    st = np.stack([lr, li, ld], 0).reshape(3, 8, 2, 64)
    lamP = np.ascontiguousarray(st.transpose(2, 3, 0, 1).reshape(128, 3, 8))
    lamR = np.ascontiguousarray(np.broadcast_to(st.reshape(1, 3, 1024), (32, 3, 1024)))
    bmat = np.zeros((32, 2, 8, 2, 64), np.float32)
    cmat = np.zeros((2, 64, 2, 8, 2, 16), np.float32)
    for k, (bb, cc) in enumerate(((b_re, c_re), (b_im, c_im))):
        bq = bb[g0:g0 + 16].reshape(8, 2, 64, 16)
        cq = cc[g0:g0 + 16].reshape(8, 2, 16, 64)
        for gl in range(2):
            bmat[gl * 16:(gl + 1) * 16, k, :, gl, :] = bq[:, gl].transpose(2, 0, 1)
            cmat[gl, :, k, :, gl, :] = cq[:, gl].transpose(2, 0, 1)
    bmat = bmat.reshape(32, 2, 1024)
    cmat = cmat.reshape(128, 2, 8, 32)
    dsk = np.ascontiguousarray(d_skip[256 * q:256 * (q + 1)].reshape(8, 32).T)
    return dict(lamP=lamP, lamR=lamR, bmat=bmat, cmat=cmat, dsk=dsk)


RTB = 128
S5_G = 2
RW1_G = 2
RW2_F32R = False
RW2_G = 3
_DBG = [99]


def build_rw1(tlen=T):
    nc = bass.Bass("TRN2", target_bir_lowering=False)
    di = lambda n, sh: nc.dram_tensor(n, sh, F32, kind="ExternalInput").ap()
    do = lambda n, sh: nc.dram_tensor(n, sh, F32, kind="ExternalOutput").ap()
    zrkv = di("zrkv", [64, 3, 4, tlen + 1])
    zw = di("zw", [64, tlen + 1])
    za = di("za", [64, tlen + 1])
    zg0 = di("zg0", [128, tlen + 1])
    zg1 = di("zg1", [128, tlen + 1])
    mu_rkv = di("mu_rkv", [64, 3, 4, RTB])
    mu_l = di("mu_l", [128, 4])
    w2 = di("w2", [64, 256])
    a2 = di("a2", [64, 256])
    g2a = di("g2a", [128, 256])
    g2b = di("g2b", [128, 256])
    cvec = di("cvec", [64, 5, 4, RTB])
    outs = {n: do(n, [64, 4, tlen]) for n in ("at", "rt", "bt", "kt", "xv", "gg", "bs", "ec")}
    p = Prog(nc)
    ch0 = p.chan()
    N = 4 * RTB

    def const(ap, shape, nm):
        t = p.sbuf(shape, F32, nm)
        r = Res()
        p.dma("sp", p.chan(), t[:], ap, writes=[r])
        return t, r
    mu_t, mu_r = const(mu_rkv, [64, 3, 4, RTB], "mu")
    mul_t, mul_r = const(mu_l, [128, 4], "mul")
    w2_t, w2_r = const(w2, [64, 256], "w2")
    a2_t, a2_r = const(a2, [64, 256], "a2")
    g2a_t, g2a_r = const(g2a, [128, 256], "g2a")
    g2b_t, g2b_r = const(g2b, [128, 256], "g2b")
    cv_t, cv_r = const(cvec, [64, 5, 4, RTB], "cv")
    cst = p.sbuf([128, 4], F32, "cst")
    cst_r = Res()
    p.op("pool", lambda e: e.memset(cst[:, 0:1], 1.0), writes=[cst_r])
    p.op("pool", lambda e: e.memset(cst[:, 1:2], -0.5), writes=[cst_r])
    ones_in = di("ones_in", [64, 256])
    ones_f32, ones_r = const(ones_in, [64, 256], "ones_f32")
    ones = ones_f32[:, 0:64]
    cmask = p.sbuf([64, RTB], F32, "cmask")
    p.op("pool", lambda e: e.memset(cmask[:], 1.0), writes=[cst_r])
    for c in range(RTB // 64):
        p.op("pool", lambda e, c=c: e.memset(cmask[:, c * 64:c * 64 + 1], 0.0), writes=[cst_r])

    zin = TPool(p, [64, 3, 4, RTB + 1], F32, 2 * RW1_G, "zin")
    zch = [p.chan() for _ in range(2 * RW1_G)]
    lin = TPool(p, [128, 4, RTB + 1], F32, 2 * RW1_G, "lin")
    lch = [p.chan() for _ in range(2 * RW1_G)]
    tp = TPool(p, [64, 4, RTB], F32, 21 * RW1_G + 2, "r1t")
    tl = TPool(p, [128, RTB], F32, 4 * RW1_G + 4, "r1l")
    ps = TPool(p, [64, 4, RTB], F32, 6, "r1ps", psum=True)
    psg = TPool(p, [64, 4, RTB], F32, 2, "r1pg", psum=True)
    och = [p.chan() for _ in range(16)]

    ocnt = [0]

    def emit_out(name, t, r, t0):
        i = ocnt[0] % len(och)
        ocnt[0] += 1
        p.dma("sp", och[i], outs[name][:, :, t0:t0 + RTB], t, reads=[r], is_output=True)

    def block(blk):
        t0 = blk * RTB
        zi = zin.i
        z, z_r = zin.get()
        p.dma("sp", zch[zi], z[:], zrkv[:, :, :, t0:t0 + RTB + 1], writes=[z_r])
        li_ = lin.i
        l, l_r = lin.get()
        p.dma("sp", lch[li_], l[0:64, 0, :], zw[:, t0:t0 + RTB + 1], writes=[l_r])
        p.dma("sp", lch[li_], l[0:64, 1, :], za[:, t0:t0 + RTB + 1], writes=[l_r])
        p.dma("sp", lch[li_], l[:, 2, :], zg0[:, t0:t0 + RTB + 1], writes=[l_r])
        p.dma("sp", lch[li_], l[:, 3, :], zg1[:, t0:t0 + RTB + 1], writes=[l_r])
        xs = []
        for j in range(3):
            x, x_r = tp.get()
            o_tt(p, "pool", x[:], z[:, j, :, 0:RTB], z[:, j, :, 1:RTB + 1], ALU.subtract, [z_r], [x_r])
            o_tt(p, "pool", x[:], x[:], mu_t[:, j, :, :], ALU.mult, [x_r, mu_r], [x_r])
            o_tt(p, "pool", x[:], x[:], z[:, j, :, 1:RTB + 1], ALU.add, [x_r, z_r], [x_r])
            xs.append((x, x_r))
        (xr, xr_r), (xk, xk_r), (xv, xv_r) = xs
        yield
        ls = []
        for j, np_ in ((0, 64), (1, 64), (2, 128), (3, 128)):
            x, x_r = tl.get()
            o_tt(p, "dve", x[0:np_, :], l[0:np_, j, 0:RTB], l[0:np_, j, 1:RTB + 1], ALU.subtract, [l_r], [x_r])
            o_stt(p, x[0:np_, :], x[0:np_, :], mul_t[0:np_, j:j + 1], l[0:np_, j, 1:RTB + 1], ALU.mult, ALU.add,
                  [x_r, l_r, mul_r], [x_r])
            ls.append((x, x_r, np_))
        xw, xw_r, _ = ls[0]
        xa, xa_r, _ = ls[1]
        o_act(p, xw[0:64, :], xw[0:64, :], AF.Tanh, [xw_r], [xw_r])
        for (x, x_r, np_) in ls[2:]:
            o_act(p, x[0:np_, :], x[0:np_, :], AF.Sigmoid, [x_r], [x_r])
        yield
        pw, pw_r = ps.get()
        pa, pa_r = ps.get()
        pg, pg_r = psg.get()
        for h in range(4):
            hs = slice(h * 64, (h + 1) * 64)
            p.op("pe", lambda e, h=h, hs=hs: e.matmul(pw[:, h, :], w2_t[:, hs], xw[0:64, :], start=True, stop=True),
                 reads=[w2_r, xw_r], writes=[pw_r], same_bank_cont=(h > 0))
            p.op("pe", lambda e, h=h, hs=hs: e.matmul(pa[:, h, :], a2_t[:, hs], xa[0:64, :], start=True, stop=True),
                 reads=[a2_r, xa_r], writes=[pa_r], same_bank_cont=(h > 0))
            p.op("pe", lambda e, h=h, hs=hs: e.matmul(pg[:, h, :], g2a_t[:, hs], ls[2][0][:, :], start=True, stop=False),
                 reads=[g2a_r, ls[2][1]], writes=[pg_r], same_bank_cont=(h > 0))
            p.op("pe", lambda e, h=h, hs=hs: e.matmul(pg[:, h, :], g2b_t[:, hs], ls[3][0][:, :], start=False, stop=True),
                 reads=[g2b_r, ls[3][1]], writes=[pg_r])
        yield
        e2, e2_r = tp.get()
        o_tt(p, "dve", e2[:], pw[:], cv_t[:, 0, :, :], ALU.add, [pw_r, cv_r], [e2_r])
        o_act(p, e2[:], e2[:], AF.Exp, [e2_r], [e2_r], scale=-1.0)
        o_act(p, e2[:], e2[:], AF.Ln, [e2_r, cst_r], [e2_r], bias=cst[0:64, 0:1])
        o_act(p, e2[:], e2[:], AF.Exp, [e2_r, cst_r], [e2_r], scale=-1.0, bias=cst[0:64, 1:2])
        a_, a_r = tp.get()
        o_tt(p, "dve", a_[:], pa[:], cv_t[:, 1, :, :], ALU.add, [pa_r, cv_r], [a_r])
        o_act(p, a_[:], a_[:], AF.Sigmoid, [a_r], [a_r])
        gg, gg_r = tp.get()
        p.op("act", lambda e, gg=gg, pg=pg: e.copy(out=gg[:], in_=pg[:]), reads=[pg_r], writes=[gg_r])
        yield
        kk, kk_r = tp.get()
        sq, sq_r = tp.get()
        o_tt(p, "dve", kk[:], xk[:], cv_t[:, 2, :, :], ALU.mult, [xk_r, cv_r], [kk_r])
        o_tt(p, "dve", sq[:], kk[:], kk[:], ALU.mult, [kk_r], [sq_r])
        pss, pss_r = ps.get()
        for h in range(4):
            p.op("pe", lambda e, pss=pss, sq=sq, h=h: e.matmul(pss[:, h, :], ones, sq[:, h, :], start=True, stop=True),
                 reads=[ones_r, sq_r], writes=[pss_r], same_bank_cont=(h > 0))
        o_ts(p, "dve", sq[:], pss[:], 1.0, 1e-24, ALU.mult, ALU.max, [pss_r, sq_r], [sq_r])
        o_act(p, sq[:], sq[:], AF.Sqrt, [sq_r], [sq_r])
        p.op("dve", lambda e, sq=sq: e.reciprocal(out=sq[:], in_=sq[:]), reads=[sq_r], writes=[sq_r])
        o_tt(p, "dve", kk[:], kk[:], sq[:], ALU.mult, [kk_r, sq_r], [kk_r])
        yield
        km, km_r = tp.get()
        o_ts(p, "pool", km[:], a_[:], -1.0, None, ALU.add, None, [a_r], [km_r])
        o_tt(p, "pool", km[:], km[:], cv_t[:, 3, :, :], ALU.mult, [km_r, cv_r], [km_r])
        o_stt(p, km[:], km[:], 1.0, xk[:], ALU.add, ALU.mult, [km_r, xk_r], [km_r])
        vb, vb_r = tp.get()
        o_tt(p, "pool", vb[:], kk[:], a_[:], ALU.mult, [kk_r, a_r], [vb_r])
        rk, rk_r = tp.get()
        o_tt(p, "pool", rk[:], xr[:], km[:], ALU.mult, [xr_r, km_r], [rk_r])
        o_tt(p, "pool", rk[:], rk[:], cv_t[:, 4, :, :], ALU.mult, [rk_r, cv_r], [rk_r])
        pb, pb_r = ps.get()
        for h in range(4):
            p.op("pe", lambda e, pb=pb, rk=rk, h=h: e.matmul(pb[:, h, :], ones, rk[:, h, :], start=True, stop=True),
                 reads=[ones_r, rk_r], writes=[pb_r], same_bank_cont=(h > 0))
        bsv, bsv_r = tp.get()
        p.op("act", lambda e, bsv=bsv, pb=pb: e.copy(out=bsv[:], in_=pb[:]), reads=[pb_r], writes=[bsv_r])
        yield
        cu, cu_r = tp.get()
        for h in range(4):
            p.op("dve", lambda e, h=h, cu=cu, e2=e2: e.tensor_tensor_scan(
                out=cu[:, h, :], data0=cmask[:], data1=e2[:, h, :], initial=0.0, op0=ALU.mult, op1=ALU.add),
                reads=[cst_r, e2_r], writes=[cu_r])
        cex, cex_r = tp.get()
        o_tt(p, "pool", cex[:], cu[:], e2[:], ALU.subtract, [cu_r, e2_r], [cex_r])
        ec, ec_r = tp.get()
        en, en_r = tp.get()
        o_act(p, ec[:], cu[:], AF.Exp, [cu_r], [ec_r], scale=-1.0)
        o_act(p, en[:], cu[:], AF.Exp, [cu_r], [en_r])
        o_act(p, cex[:], cex[:], AF.Exp, [cex_r], [cex_r], scale=-1.0)
        yield
        at, at_r = tp.get()
        o_stt(p, at[:], kk[:], -1.0, cex[:], ALU.mult, ALU.mult, [kk_r, cex_r], [at_r])
        rt, rt_r = tp.get()
        o_tt(p, "dve", rt[:], xr[:], ec[:], ALU.mult, [xr_r, ec_r], [rt_r])
        bt, bt_r = tp.get()
        o_tt(p, "pool", bt[:], vb[:], en[:], ALU.mult, [vb_r, en_r], [bt_r])
        kt, kt_r = tp.get()
        o_tt(p, "dve", kt[:], km[:], en[:], ALU.mult, [km_r, en_r], [kt_r])
        for nm, (t_, r_) in (("at", (at, at_r)), ("rt", (rt, rt_r)), ("bt", (bt, bt_r)), ("kt", (kt, kt_r)),
                             ("xv", (xv, xv_r)), ("gg", (gg, gg_r)), ("bs", (bsv, bsv_r)), ("ec", (ec, ec_r))):
            emit_out(nm, t_[:], r_, t0)
    nblk = tlen // RTB
    for b0 in range(0, nblk, RW1_G):
        gens = [block(b_) for b_ in range(b0, min(nblk, b0 + RW1_G))]
        while gens:
            for g in list(gens):
                try:
                    next(g)
                except StopIteration:
                    gens.remove(g)
    p.finish()
    return nc


def build_rw2(tlen=T):
    nc = bass.Bass("TRN2", target_bir_lowering=False)
    nch = tlen // 64
    di = lambda n, sh: nc.dram_tensor(n, sh, F32, kind="ExternalInput").ap()
    AR = di("AR", [64, nch, 4, 2, 64])
    BK = di("BK", [64, nch, 4, 2, 64])
    TM = di("TM", [64, nch, 4, 5, 64])
    BS = di("BS", [64, nch, 4])
    GL = di("GL", [64, nch, 4])
    LN = di("LN", [64, 2, 4, 64])
    yo = nc.dram_tensor("yo", [64, nch, 4, 64], F32, kind="ExternalOutput").ap()
    p = Prog(nc)
    ch0 = p.chan()
    ln_t = p.sbuf([64, 2, 4, 64], F32, "ln")
    ln_r = Res()
    p.dma("sp", p.chan(), ln_t[:], LN, writes=[ln_r])
    gl_t = p.sbuf([64, nch, 4], F32, "gl")
    gl_r = Res()
    p.dma("sp", p.chan(), gl_t[:], GL, writes=[gl_r])
    bs_t = p.sbuf([64, nch, 4], F32, "bs")
    bs_r = Res()
    p.dma("sp", p.chan(), bs_t[:], BS, writes=[bs_r])
    ii = p.sbuf([64, 64], mybir.dt.int32, "ii")
    dif = p.sbuf([64, 64], F32, "dif")
    mk_r = Res()
    p.op("pool", lambda e: e.iota(ii[:], [[1, 64]], base=0, channel_multiplier=-1), writes=[mk_r])
    p.op("dve", lambda e: e.tensor_copy(out=dif[:], in_=ii[:]), reads=[mk_r], writes=[mk_r])
    msk = p.sbuf([64, 4, 4, 64], F32, "msk")
    mlow = p.sbuf([64, 4, 64], F32, "mlow")
    ident = p.sbuf([64, 4, 64], F32, "ident")
    for h in range(4):
        for q in range(4):
            op = ALU.is_gt if q % 2 == 0 else ALU.is_ge
            o_ts(p, "dve", msk[:, h, q, :], dif[:], 0.0, None, op, None, [mk_r], [mk_r])
        o_ts(p, "dve", mlow[:, h, :], dif[:], 0.0, None, ALU.is_lt, None, [mk_r], [mk_r])
        o_ts(p, "dve", ident[:, h, :], dif[:], 0.0, None, ALU.is_equal, None, [mk_r], [mk_r])

    G = RW2_G
    arp = TPool(p, [64, 4, 2, 64], F32, 2 * G, "ar")
    bkp = TPool(p, [64, 4, 2, 64], F32, 2 * G, "bk")
    tmp_ = TPool(p, [64, 4, 5, 64], F32, 2 * G, "tm")
    inch = [p.chan() for _ in range(2 * G)]
    inch2 = [p.chan() for _ in range(2 * G)]
    inch3 = [p.chan() for _ in range(2 * G)]
    psA = TPool(p, [64, 4, 4, 64], F32, 1, "psA", psum=True)
    ps4 = TPool(p, [64, 4, 64], F32, 5, "ps4", psum=True)
    psX = TPool(p, [64, 4, 128], F32, 1, "psX", psum=True)
    sb4 = TPool(p, [64, 4, 64], F32, 12 * G + 8, "sb4")
    keep = TPool(p, [64, 4, 64], F32, 3 * 2 * G, "keep")
    AT_p = TPool(p, [64, 4, 4, 64], F32, 2 * G, "AT")
    X_p = TPool(p, [64, 4, 128], F32, G + 1, "X")
    WU_p = TPool(p, [64, 4, 128], F32, 2 * G, "WU")
    S_p = TPool(p, [64, 4, 64], F32, 3, "S")
    st4 = TPool(p, [64, 4], F32, 8, "st4")
    outp = TPool(p, [64, 4, 64], F32, 3, "yo")
    och = [p.chan() for _ in range(3)]
    S0, S0_r = S_p.get()
    p.op("pool", lambda e: e.memset(S0[:], 0.0), writes=[S0_r])
    state = [S0, S0_r]

    def r32(ap):
        return ap.bitcast(mybir.dt.float32r) if RW2_F32R else ap

    def mm4(ps, ps_r, lhs_fn, rhs_fn, reads):
        mm4g(ps, ps_r, [(lhs_fn, rhs_fn, reads)])

    def mm4g(ps, ps_r, terms):
        for h in range(4):
            for ti, (lhs_fn, rhs_fn, reads) in enumerate(terms):
                p.op("pe", lambda e, h=h, lhs_fn=lhs_fn, rhs_fn=rhs_fn, ti=ti: e.matmul(
                    ps[:, h, :], r32(lhs_fn(h)), r32(rhs_fn(h)), start=(ti == 0), stop=(ti == len(terms) - 1)),
                    reads=reads, writes=[ps_r], same_bank_cont=(h > 0))

    def pre_phase(n, ctx):
        i_in = arp.i
        ar, ar_r = arp.get()
        bk, bk_r = bkp.get()
        tm, tm_r = tmp_.get()
        p.dma("sp", inch[i_in], ar[:], AR[:, n], writes=[ar_r])
        p.dma("sp", inch2[i_in], bk[:], BK[:, n], writes=[bk_r])
        p.dma("sp", inch3[i_in], tm[:], TM[:, n], writes=[tm_r])
        yield
        pA, pA_r = psA.get()
        for h in range(4):
            p.op("pe", lambda e, h=h: e.matmul(pA[:, h, 0:2, :], r32(bk[:, h, 0, :]), r32(ar[:, h, :, :]), start=True, stop=True),
                 reads=[bk_r, ar_r], writes=[pA_r], same_bank_cont=(h % 2 == 1))
            p.op("pe", lambda e, h=h: e.matmul(pA[:, h, 2:4, :], r32(bk[:, h, 1, :]), r32(ar[:, h, :, :]), start=True, stop=True),
                 reads=[bk_r, ar_r], writes=[pA_r], same_bank_cont=True)
        AT, AT_r = AT_p.get()
        o_tt(p, "dve", AT[:], pA[:], msk[:], ALU.mult, [pA_r, mk_r], [AT_r])
        pN, pN_r = ps4.get()
        mm4(pN, pN_r, lambda h: ar[:, h, 0, :], lambda h: bk[:, h, 0, :], [ar_r, bk_r])
        PT, PT_r = sb4.get()
        o_tt(p, "dve", PT[:], pN[:], mlow[:], ALU.mult, [pN_r, mk_r], [PT_r])
        yield
        P_, P_r = sb4.get()
        p.op("act", lambda e: e.copy(out=P_[:], in_=AT[:, :, 0, :]), reads=[AT_r], writes=[P_r])
        M, M_r = sb4.get()
        o_tt(p, "pool", M[:], AT[:, :, 0, :], ident[:], ALU.add, [AT_r, mk_r], [M_r])
        pv, pv_r = ps4.get()
        mm4(pv, pv_r, lambda h: AT[:, h, 2, :], lambda h: tm[:, h, 3, :], [AT_r, tm_r])
        X, X_r = X_p.get()
        p.op("act", lambda e: e.copy(out=X[:, :, 64:128], in_=pv[:]), reads=[pv_r], writes=[X_r])
        p.op("pool", lambda e: e.tensor_copy(out=X[:, :, 0:64], in_=tm[:, :, 0, :]), reads=[tm_r], writes=[X_r])
        yield
        for lev in range(1, 6):
            pq, pq_r = ps4.get()
            mm4(pq, pq_r, lambda h, P_=P_: P_[:, h, :], lambda h, PT=PT: PT[:, h, :], [P_r, PT_r])
            PT2, PT2_r = sb4.get()
            p.op("act", lambda e, PT2=PT2, pq=pq: e.copy(out=PT2[:], in_=pq[:]), reads=[pq_r], writes=[PT2_r])
            if lev < 5:
                pq2, pq2_r = ps4.get()
                mm4(pq2, pq2_r, lambda h, PT=PT: PT[:, h, :], lambda h, P_=P_: P_[:, h, :], [P_r, PT_r])
                P2, P2_r = sb4.get()
                p.op("dve", lambda e, P2=P2, pq2=pq2: e.tensor_copy(out=P2[:], in_=pq2[:]), reads=[pq2_r], writes=[P2_r])
            yield
            pm, pm_r = ps4.get()
            mm4(pm, pm_r, lambda h, PT2=PT2: PT2[:, h, :], lambda h, M=M: M[:, h, :], [PT2_r, M_r])
            M2, M2_r = sb4.get()
            o_tt(p, "dve", M2[:], pm[:], M[:], ALU.add, [pm_r, M_r], [M2_r])
            M, M_r = M2, M2_r
            PT, PT_r = PT2, PT2_r
            if lev < 5:
                P_, P_r = P2, P2_r
            yield
        pX, pX_r = psX.get()
        mm4(pX, pX_r, lambda h: M[:, h, :], lambda h: X[:, h, :], [M_r, X_r])
        WU, WU_r = WU_p.get()
        p.op("dve", lambda e: e.tensor_copy(out=WU[:], in_=pX[:]), reads=[pX_r], writes=[WU_r])
        yield
        pG, pG_r = ps4.get()
        mm4(pG, pG_r, lambda h: WU[:, h, 0:64], lambda h: tm[:, h, 1, :], [WU_r, tm_r])
        GT, GT_r = keep.get()
        o_tt(p, "dve", GT[:], pG[:], ident[:], ALU.add, [pG_r, mk_r], [GT_r])
        pH, pH_r = ps4.get()
        mm4g(pH, pH_r, [(lambda h: tm[:, h, 1, :], lambda h: WU[:, h, 64:128], [WU_r, tm_r]),
                        (lambda h: tm[:, h, 2, :], lambda h: tm[:, h, 3, :], [tm_r])])
        HG, HG_r = keep.get()
        o_tt(p, "dve", HG[:], pH[:], gl_t[:, n, :].unsqueeze(2).to_broadcast([64, 4, 64]), ALU.mult, [pH_r, gl_r], [HG_r])
        pQ, pQ_r = ps4.get()
        mm4(pQ, pQ_r, lambda h: WU[:, h, 0:64], lambda h: AT[:, h, 1, :], [WU_r, AT_r])
        QT, QT_r = keep.get()
        o_tt(p, "dve", QT[:], pQ[:], ar[:, :, 1, :], ALU.add, [pQ_r, ar_r], [QT_r])
        ctx.update(AT=AT, AT_r=AT_r, WU=WU, WU_r=WU_r, tm=tm, tm_r=tm_r, GT=GT, GT_r=GT_r, HG=HG, HG_r=HG_r,
                   QT=QT, QT_r=QT_r)

    def chain_phase(n, c):
        S, S_r = state
        AT, AT_r, WU, WU_r, tm, tm_r = c["AT"], c["AT_r"], c["WU"], c["WU_r"], c["tm"], c["tm_r"]
        GT, GT_r, HG, HG_r, QT, QT_r = c["GT"], c["GT_r"], c["HG"], c["HG_r"], c["QT"], c["QT_r"]
        pY, pY_r = ps4.get()
        mm4g(pY, pY_r, [(lambda h: AT[:, h, 1, :], lambda h: WU[:, h, 64:128], [AT_r, WU_r]),
                        (lambda h: AT[:, h, 3, :], lambda h: tm[:, h, 3, :], [AT_r, tm_r]),
                        (lambda h: QT[:, h, :], lambda h: S[:, h, :], [QT_r, S_r])])
        pS, pS_r = ps4.get()
        mm4(pS, pS_r, lambda h: GT[:, h, :], lambda h: S[:, h, :], [GT_r, S_r])
        S2, S2_r = S_p.get()
        sg, sg_r = sb4.get()
        o_tt(p, "dve", sg[:], pS[:], gl_t[:, n, :].unsqueeze(2).to_broadcast([64, 4, 64]), ALU.mult, [pS_r, gl_r], [sg_r])
        o_tt(p, "dve", S2[:], sg[:], HG[:], ALU.add, [sg_r, HG_r], [S2_r])
        state[0], state[1] = S2, S2_r
        y, y_r = sb4.get()
        p.op("act", lambda e: e.copy(out=y[:], in_=pY[:]), reads=[pY_r], writes=[y_r])
        s1, s1_r = st4.get()
        s2, s2_r = st4.get()
        ysq, ysq_r = sb4.get()
        p.op("dve", lambda e: e.reduce_sum(out=s1[:], in_=y[:], axis=AX.X), reads=[y_r], writes=[s1_r])
        o_tt(p, "pool", ysq[:], y[:], y[:], ALU.mult, [y_r], [ysq_r])
        p.op("dve", lambda e: e.reduce_sum(out=s2[:], in_=ysq[:], axis=AX.X), reads=[ysq_r], writes=[s2_r])
        o_ts(p, "dve", s1[:], s1[:], 1.0 / 64, None, ALU.mult, None, [s1_r], [s1_r])
        m2, m2_r = st4.get()
        o_tt(p, "dve", m2[:], s1[:], s1[:], ALU.mult, [s1_r], [m2_r])
        o_stt(p, s2[:], s2[:], 1.0 / 64, m2[:], ALU.mult, ALU.subtract, [s2_r, m2_r], [s2_r])
        o_ts(p, "dve", s2[:], s2[:], 64e-5, None, ALU.add, None, [s2_r], [s2_r])
        o_act(p, s2[:], s2[:], AF.Sqrt, [s2_r], [s2_r])
        p.op("dve", lambda e: e.reciprocal(out=s2[:], in_=s2[:]), reads=[s2_r], writes=[s2_r])
        yn, yn_r = sb4.get()
        o_tt(p, "dve", yn[:], y[:], s1[:].unsqueeze(2).to_broadcast([64, 4, 64]), ALU.subtract, [y_r, s1_r], [yn_r])
        o_tt(p, "dve", yn[:], yn[:], s2[:].unsqueeze(2).to_broadcast([64, 4, 64]), ALU.mult, [yn_r, s2_r], [yn_r])
        o_tt(p, "pool", yn[:], yn[:], ln_t[:, 0, :, :], ALU.mult, [yn_r, ln_r], [yn_r])
        o_tt(p, "pool", yn[:], yn[:], ln_t[:, 1, :, :], ALU.add, [yn_r, ln_r], [yn_r])
        bv, bv_r = sb4.get()
        o_tt(p, "dve", bv[:], tm[:, :, 3, :], bs_t[:, n, :].unsqueeze(2).to_broadcast([64, 4, 64]), ALU.mult, [tm_r, bs_r], [bv_r])
        o_tt(p, "pool", yn[:], yn[:], bv[:], ALU.add, [yn_r, bv_r], [yn_r])
        oi = outp.i
        o, o_r = outp.get()
        o_tt(p, "pool", o[:], yn[:], tm[:, :, 4, :], ALU.mult, [yn_r, tm_r], [o_r])
        p.dma("sp", och[oi], yo[:, n], o[:], reads=[o_r], is_output=True)

    for n0 in range(0, nch, G):
        ns = list(range(n0, min(nch, n0 + G)))
        ctxs = [dict() for _ in ns]
        gens = [pre_phase(n, c) for n, c in zip(ns, ctxs)]
        while gens:
            for g in list(gens):
                try:
                    next(g)
                except StopIteration:
                    gens.remove(g)
        for n, c in zip(ns, ctxs):
            chain_phase(n, c)
    p.finish()
    return nc


def rw1_host_layout(z, mu, w2, a2, g2, w0, a0, k_k, k_a, r_k, q):
    tl = z.shape[0]
    cs = slice(256 * q, 256 * (q + 1))
    zrkv = np.zeros((64, 3, 4, tl + 1), np.float32)
    mu_rkv = np.zeros((64, 3, 4, RTB), np.float32)
    for j in range(3):
        blk = z[:, j * 1024:(j + 1) * 1024][:, cs]
        zrkv[:, j, :, 1:] = blk.reshape(tl, 4, 64).transpose(2, 1, 0)
        mu_rkv[:, j, :, :] = mu[j * 1024:(j + 1) * 1024][cs].reshape(4, 64).T[:, :, None]

    def rows(c0, c1):
        o = np.zeros((c1 - c0, tl + 1), np.float32)
        o[:, 1:] = z[:, c0:c1].T
        return o
    mu_l = np.zeros((128, 4), np.float32)
    mu_l[0:64, 0] = mu[3072:3136]
    mu_l[0:64, 1] = mu[3136:3200]
    mu_l[0:128, 2] = mu[3200:3328]
    mu_l[0:32, 3] = mu[3328:3360]
    cvec = np.zeros((64, 5, 4, RTB), np.float32)
    for i, v in enumerate((w0, a0, k_k, k_a)):
        cvec[:, i, :, :] = v[cs].reshape(4, 64).T[:, :, None]
    cvec[:, 4, :, :] = r_k[4 * q:4 * q + 4].T[:, :, None]
    return dict(zrkv=zrkv, zw=rows(3072, 3136), za=rows(3136, 3200), zg0=rows(3200, 3328), zg1=np.concatenate([rows(3328, 3360), np.zeros((96, tl + 1), np.float32)], 0),
                mu_rkv=mu_rkv, mu_l=mu_l, w2=np.ascontiguousarray(w2[:, cs]), a2=np.ascontiguousarray(a2[:, cs]),
                g2a=np.ascontiguousarray(g2[0:128, cs]), g2b=np.concatenate([g2[128:160, cs], np.zeros((96, 256), np.float32)], 0), cvec=cvec,
                ones_in=np.ones((64, 256), np.float32))


def rw2_host_layout(o, lnx_w, lnx_b, q):
    tl = o["at"].shape[2]
    nch = tl // 64
    c5 = lambda a: np.asarray(a).reshape(64, 4, nch, 64)
    at, rt, bt, kt, xv, gg = (c5(o[k]) for k in ("at", "rt", "bt", "kt", "xv", "gg"))
    AR = np.stack([at, rt], 3).transpose(0, 2, 1, 3, 4)
    BK = np.stack([bt, kt], 3).transpose(0, 2, 1, 3, 4)
    TM = np.stack([at, bt, kt, xv, gg], 0).transpose(4, 3, 2, 0, 1)
    BS = c5(o["bs"])[0].transpose(2, 1, 0)
    GL = c5(o["ec"])[:, :, :, 63].transpose(0, 2, 1)
    cs = slice(256 * q, 256 * (q + 1))
    LN = np.broadcast_to(np.stack([lnx_w[cs].reshape(4, 64), lnx_b[cs].reshape(4, 64)], 0)[None], (64, 2, 4, 64))
    f = lambda a: np.ascontiguousarray(a, dtype=np.float32)
    return dict(AR=f(AR), BK=f(BK), TM=f(TM), BS=f(BS), GL=f(GL), LN=f(LN))


_PROGS = {}


def _prog(key, fn):
    if key not in _PROGS:
        _PROGS[key] = fn()
    return _PROGS[key]


def _run(nc, maps):
    return run_bass_kernel_spmd(nc, maps, core_ids=list(range(NCORES))).results


def _tok_shards(a):
    return [np.ascontiguousarray(a[c // 4, (c % 4) * NT:(c % 4 + 1) * NT, :].T) for c in range(NCORES)]


def _from_tok_shards(res, key, ncols):
    out = np.empty((B, T, ncols), np.float32)
    for c in range(NCORES):
        out[c // 4, (c % 4) * NT:(c % 4 + 1) * NT, :] = np.asarray(res[c][key]).T
    return out


def kernel(**inputs):
    I = {k: np.asarray(v) for k, v in inputs.items()}
    f32 = lambda a: np.ascontiguousarray(a, dtype=np.float32)
    xs = _tok_shards(f32(I["x"]))
    for layer in range(4):
        i = layer // 2
        if layer % 2 == 0:
            nc = _prog("k1e", lambda: build_k1(EVEN_IN))
            r = _run(nc, [{"xT": xs[c], "g": f32(I["norm_mix_pre"][layer]), "w": f32(I["ev_w_in"][i])} for c in range(NCORES)])
            pfull = _from_tok_shards(r, "oT", EVEN_IN)
            maps = []
            for c in range(NCORES):
                b, q = c // 4, c % 4
                d = s5_host_layout(f32(I["s5_lam_re"][i]), f32(I["s5_lam_im"][i]), f32(I["s5_log_dt"][i]),
                                   f32(I["s5_b_re"][i]), f32(I["s5_b_im"][i]), f32(I["s5_c_re"][i]),
                                   f32(I["s5_c_im"][i]), f32(I["s5_d"][i]), q)
                d["uT"] = np.ascontiguousarray(pfull[b, :, 256 * q:256 * (q + 1)].T)
                maps.append(d)
            rs = _run(_prog("s5", build_s5), maps)
            ycat = np.empty((B, T, D), np.float32)
            for c in range(NCORES):
                b, q = c // 4, c % 4
                ycat[b, :, 256 * q:256 * (q + 1)] = np.asarray(rs[c]["yT"]).T
            maps = [rw1_host_layout(pfull[c // 4, :, 1024:], f32(I["ev_shift_mu"][i]), f32(I["rw_w2"][i]),
                                    f32(I["rw_a2"][i]), f32(I["rw_g2"][i]), f32(I["rw_w0"][i]), f32(I["rw_a0"][i]),
                                    f32(I["rw_k_k"][i]), f32(I["rw_k_a"][i]), f32(I["rw_r_k"][i]), c % 4)
                    for c in range(NCORES)]
            r1 = _run(_prog("rw1", build_rw1), maps)
            maps = [rw2_host_layout(r1[c], f32(I["rw_lnx_w"][i]), f32(I["rw_lnx_b"][i]), c % 4) for c in range(NCORES)]
            r2 = _run(_prog("rw2", build_rw2), maps)
            for c in range(NCORES):
                b, q = c // 4, c % 4
                yo = np.asarray(r2[c]["yo"])
                ycat[b, :, 1024 + 256 * q:1024 + 256 * (q + 1)] = yo.transpose(1, 0, 2, 3).reshape(T, 256)
            ys = _tok_shards(ycat)
            nc = _prog("k2g", lambda: build_k2(D, glu=True))
            r = _run(nc, [{"inT": ys[c], "xT": xs[c], "g": f32(I["norm_mix_post"][layer]), "w": f32(I["ev_w_out"][i]),
                           "wglu": f32(I["s5_w_glu"][i])} for c in range(NCORES)])
        else:
            nc = _prog("k1o", lambda: build_k1(2 * D))
            r = _run(nc, [{"xT": xs[c], "g": f32(I["norm_mix_pre"][layer]), "w": f32(I["od_w_in"][i])} for c in range(NCORES)])
            pfull = _from_tok_shards(r, "oT", 2 * D)
            maps = []
            for c in range(NCORES):
                b, q = c // 4, c % 4
                cs = slice(512 * q, 512 * (q + 1))
                v = np.stack([I["od_conv_w"][i][0][cs], I["od_conv_w"][i][1][cs], I["od_conv_w"][i][2][cs],
                              I["od_conv_w"][i][3][cs], I["od_conv_b"][i][cs], I["lru_b_r"][i][cs],
                              I["lru_b_i"][i][cs], I["lru_lam"][i][cs]], -1)
                maps.append({"gateT": np.ascontiguousarray(pfull[b, :, cs].T),
                             "xbT": np.ascontiguousarray(pfull[b, :, D + 512 * q:D + 512 * (q + 1)].T),
                             "vecs": f32(v.reshape(4, 128, 8).transpose(1, 0, 2)),
                             "wr": f32(I["lru_w_r"][i][2 * q:2 * q + 2]), "wi": f32(I["lru_w_i"][i][2 * q:2 * q + 2])})
            rl = _run(_prog("lru", build_lru), maps)
            ycat = np.empty((B, T, D), np.float32)
            for c in range(NCORES):
                b, q = c // 4, c % 4
                ycat[b, :, 512 * q:512 * (q + 1)] = np.asarray(rl[c]["yT"]).T
            ys = _tok_shards(ycat)
            nc = _prog("k2p", lambda: build_k2(D))
            r = _run(nc, [{"inT": ys[c], "xT": xs[c], "g": f32(I["norm_mix_post"][layer]), "w": f32(I["od_w_out"][i])}
                          for c in range(NCORES)])
        xs = [f32(r[c]["oT"]) for c in range(NCORES)]
        nc = _prog("k1f", lambda: build_k1(DFF, swiglu=True))
        r = _run(nc, [{"xT": xs[c], "g": f32(I["norm_ffn_pre"][layer]), "w": f32(I["ffn_w_gate"][layer]),
                       "w2": f32(I["ffn_w_up"][layer])} for c in range(NCORES)])
        aT = [np.asarray(r[c]["oT"]) for c in range(NCORES)]
        nc = _prog("k2f", lambda: build_k2(DFF, in_bf16=True))
        r = _run(nc, [{"inT": aT[c], "xT": xs[c], "g": f32(I["norm_ffn_post"][layer]), "w": f32(I["ffn_w_down"][layer])}
                      for c in range(NCORES)])
        xs = [f32(r[c]["oT"]) for c in range(NCORES)]
    out = np.empty((B, T, D), np.float32)
    for c in range(NCORES):
        out[c // 4, (c % 4) * NT:(c % 4 + 1) * NT, :] = xs[c].T
    return out
```
